# Optimizing a Trainium2 kernel written in Bass

```python
import jax, jax.numpy as jnp
from jax import lax
import numpy as np

D_MODEL = 1024
BATCH = 4
SEQ = 4096
DEPTH = 1
DEC_BATCH = 16
DEC_SEQ = 32
PAST_LEN = 4096

CHUNK = 64
GLA_HEADS = 4
GLA_DK = 128
GLA_DV = 256
GLA_KEY = GLA_HEADS * GLA_DK
GLA_VAL = GLA_HEADS * GLA_DV
GLA_RANK = 16
GLA_TAU = 16.0
CONV_DIM = D_MODEL
CONV_W = 3
D_FF = -(-8 * D_MODEL // (3 * 256)) * 256
N_MOD = 6
EPS = 1e-6
IN_SIZES = (GLA_KEY, GLA_KEY, GLA_VAL, GLA_VAL, GLA_RANK, CONV_DIM, CONV_DIM, CONV_DIM, D_MODEL, D_MODEL)
IN_DIM = sum(IN_SIZES)

kernel_name = 'streaming_gla_shortconv_hybrid'


def rmsnorm(x, g):
    xf = x.astype(jnp.float32)
    xf = xf * lax.rsqrt(jnp.mean(xf * xf, axis=-1, keepdims=True) + EPS)
    return (xf * g.astype(jnp.float32)).astype(x.dtype)


def gla_recurrence(q, k, v, la, S0):
    Bn, L = q.shape[0], q.shape[1]
    blk = min(CHUNK, L)
    n = L // blk

    def to_blocks(t):
        return jnp.moveaxis(t.reshape((Bn, n, blk) + t.shape[2:]), 1, 0)

    mask = jnp.tril(jnp.ones((blk, blk), bool))[None, :, :, None, None]

    def step(S, inp):
        qb, kb, vb, lb = inp
        bcum = jnp.cumsum(lb, axis=1)
        diff = bcum[:, :, None] - bcum[:, None]
        decay = jnp.exp(jnp.where(mask, diff, -jnp.inf))
        scores = jnp.einsum('bihd,bjhd,bijhd->bhij', qb, kb, decay)
        o = (jnp.einsum('bhij,bjhv->bihv', scores, vb)
             + jnp.einsum('bihd,bhdv->bihv', qb * jnp.exp(bcum), S))
        blast = bcum[:, -1]
        S = (jnp.exp(blast)[..., None] * S
             + jnp.einsum('bjhd,bjhv->bhdv', kb * jnp.exp(blast[:, None] - bcum), vb))
        return S, o

    S, o = lax.scan(step, S0, (to_blocks(q), to_blocks(k), to_blocks(v), to_blocks(la)))
    o = jnp.moveaxis(o, 0, 1).reshape(Bn, L, GLA_HEADS, GLA_DV)
    return S, o


def causal_conv(u, prev, conv_w, conv_b):
    L = u.shape[1]
    up = jnp.concatenate([prev.astype(u.dtype), u], axis=1)
    y = up[:, 0:L] * conv_w[0] + up[:, 1:L + 1] * conv_w[1] + up[:, 2:L + 2] * conv_w[2] + conv_b
    return y, up[:, L:]


def token_mixer(h, S0, conv_prev, w_in, w_alpha, b_alpha, gla_norm_g, w_gla_out, conv_w, conv_b, w_conv_out, w_o):
    Bn, L, _ = h.shape
    z = h @ w_in
    q, k, v, g, a, cb, cc, ch, ga, gb = jnp.split(z, list(np.cumsum(IN_SIZES)[:-1]), axis=-1)
    q = q.reshape(Bn, L, GLA_HEADS, GLA_DK).astype(jnp.float32) * (GLA_DK ** -0.5)
    k = k.reshape(Bn, L, GLA_HEADS, GLA_DK).astype(jnp.float32)
    v = v.reshape(Bn, L, GLA_HEADS, GLA_DV).astype(jnp.float32)
    la = jax.nn.log_sigmoid((a @ w_alpha + b_alpha).astype(jnp.float32)) / GLA_TAU
    la = la.reshape(Bn, L, GLA_HEADS, GLA_DK)
    S_new, o = gla_recurrence(q, k, v, la, S0.astype(jnp.float32))
    o = o * lax.rsqrt(jnp.mean(o * o, axis=-1, keepdims=True) + EPS) * gla_norm_g.astype(jnp.float32)
    o = o.reshape(Bn, L, GLA_VAL).astype(h.dtype) * jax.nn.silu(g)
    y_a = o @ w_gla_out
    conv, new_buf = causal_conv(cc * ch, conv_prev, conv_w, conv_b)
    y_b = (cb * conv) @ w_conv_out
    merged = jax.nn.sigmoid(ga) * y_a + jax.nn.sigmoid(gb) * y_b
    return merged @ w_o, S_new, new_buf


def layer(x, c, S0, conv_prev, p):
    (w_mod, b_mod, norm1_g, w_in, w_alpha, b_alpha, gla_norm_g, w_gla_out,
     conv_w, conv_b, w_conv_out, w_o, norm2_g, w_ffn_in, w_ffn_out) = p
    mod = (c @ w_mod + b_mod)[:, None, :]
    sh1, sc1, g1, sh2, sc2, g2 = jnp.split(mod, N_MOD, axis=-1)
    h = rmsnorm(x, norm1_g) * (1 + sc1) + sh1
    m, S_new, buf = token_mixer(h, S0, conv_prev, w_in, w_alpha, b_alpha, gla_norm_g, w_gla_out,
                                conv_w, conv_b, w_conv_out, w_o)
    x = x + g1 * m
    h = rmsnorm(x, norm2_g) * (1 + sc2) + sh2
    gt, upv = jnp.split(h @ w_ffn_in, 2, axis=-1)
    x = x + g2 * ((jax.nn.silu(gt) * upv) @ w_ffn_out)
    return x, S_new, buf


def setup_inputs(seed: int = 0) -> dict:
    key = jax.random.key(seed)
    ks = jax.random.split(key, 24)
    f32 = jnp.float32
    D = D_MODEL

    def nrm(k, shape, scale):
        return jax.random.normal(k, shape, f32) * scale

    return {
        'x_prompt': nrm(ks[0], (BATCH, SEQ, D), 1.0),
        'x_sample': nrm(ks[1], (DEC_BATCH, DEC_SEQ, D), 1.0),
        'c_prompt': nrm(ks[2], (BATCH, D), 1.0),
        'c_sample': nrm(ks[3], (DEC_BATCH, D), 1.0),
        'state_gla': nrm(ks[4], (DEPTH, DEC_BATCH, GLA_HEADS, GLA_DK, GLA_DV), 0.5),
        'cache_conv': nrm(ks[5], (DEPTH, DEC_BATCH, CONV_W - 1, CONV_DIM), 1.0),
        'w_mod': nrm(ks[6], (DEPTH, D, N_MOD * D), 0.5 * D ** -0.5),
        'b_mod': nrm(ks[7], (DEPTH, N_MOD * D), 0.02),
        'norm1_g': 1.0 + nrm(ks[8], (DEPTH, D), 0.02),
        'w_in': nrm(ks[9], (DEPTH, D, IN_DIM), D ** -0.5),
        'w_alpha': nrm(ks[10], (DEPTH, GLA_RANK, GLA_KEY), GLA_RANK ** -0.5),
        'b_alpha': nrm(ks[11], (DEPTH, GLA_KEY), 0.02),
        'gla_norm_g': 1.0 + nrm(ks[12], (DEPTH, GLA_DV), 0.02),
        'w_gla_out': nrm(ks[13], (DEPTH, GLA_VAL, D), GLA_VAL ** -0.5),
        'conv_w': nrm(ks[14], (DEPTH, CONV_W, CONV_DIM), CONV_W ** -0.5),
        'conv_b': nrm(ks[15], (DEPTH, CONV_DIM), 0.02),
        'w_conv_out': nrm(ks[16], (DEPTH, CONV_DIM, D), CONV_DIM ** -0.5),
        'w_o': nrm(ks[17], (DEPTH, D, D), D ** -0.5),
        'norm2_g': 1.0 + nrm(ks[18], (DEPTH, D), 0.02),
        'w_ffn_in': nrm(ks[19], (DEPTH, D, 2 * D_FF), D ** -0.5),
        'w_ffn_out': nrm(ks[20], (DEPTH, D_FF, D), D_FF ** -0.5),
        'norm_f_g': 1.0 + nrm(ks[21], (D,), 0.02),
    }


def reference(x_prompt, x_sample, c_prompt, c_sample, state_gla, cache_conv, w_mod, b_mod, norm1_g,
              w_in, w_alpha, b_alpha, gla_norm_g, w_gla_out, conv_w, conv_b, w_conv_out, w_o,
              norm2_g, w_ffn_in, w_ffn_out, norm_f_g):
    Bp = x_prompt.shape[0]
    yp, ys = x_prompt, x_sample
    sp_list, cp_list, ss_list, cs_list = [], [], [], []
    for l in range(DEPTH):
        p = (w_mod[l], b_mod[l], norm1_g[l], w_in[l], w_alpha[l], b_alpha[l], gla_norm_g[l],
             w_gla_out[l], conv_w[l], conv_b[l], w_conv_out[l], w_o[l], norm2_g[l],
             w_ffn_in[l], w_ffn_out[l])
        S0_p = jnp.zeros((Bp, GLA_HEADS, GLA_DK, GLA_DV), jnp.float32)
        buf0_p = jnp.zeros((Bp, CONV_W - 1, CONV_DIM), x_prompt.dtype)
        yp, sp, cp = layer(yp, c_prompt, S0_p, buf0_p, p)
        ys, ss, cs = layer(ys, c_sample, state_gla[l], cache_conv[l], p)
        sp_list.append(sp)
        cp_list.append(cp)
        ss_list.append(ss)
        cs_list.append(cs)
    y_prompt = rmsnorm(yp, norm_f_g)
    y_sample = rmsnorm(ys, norm_f_g)
    state_gla_p = jnp.stack(sp_list)
    cache_conv_p = jnp.stack(cp_list)
    state_gla_s = jnp.stack(ss_list)
    cache_conv_s = jnp.stack(cs_list)
    return (y_prompt, y_sample, state_gla_p, cache_conv_p, state_gla_s, cache_conv_s)
```

```python
import numpy as np
from contextlib import ExitStack
import concourse.bass as bass
import concourse.mybir as mybir
from concourse.bass_utils import run_bass_kernel_spmd

F32 = mybir.dt.float32
BF16 = mybir.dt.bfloat16
AF = mybir.ActivationFunctionType
ALU = mybir.AluOpType

D = 1024
KT = 8
IN_DIM = 8208
DFF = 2816
NH = 4
EPS = 1e-6
O_Q, O_K, O_V, O_G, O_A, O_CB, O_CC, O_CH, O_GA, O_GB = 0, 512, 1024, 2048, 3072, 3088, 4112, 5136, 6160, 7184

V_N1, V_N2, V_BMOD, V_BAL, V_CW, V_CBIAS = 0, 8, 16, 64, 68, 92
NV = 100


class Reg:
    __slots__ = ("name", "lw", "rd", "ov")

    def __init__(self, name):
        self.name = name
        self.lw = None
        self.rd = []
        self.ov = []


def overlap(a, b):
    a.ov.append(b)
    b.ov.append(a)


class Op:
    __slots__ = ("eng", "fn", "deps", "signal", "count", "dma", "dsem", "dval", "idx", "tag")

    def __init__(self, eng, fn, dma):
        self.eng = eng
        self.fn = fn
        self.deps = []
        self.signal = False
        self.count = 0
        self.dma = dma
        self.dsem = None
        self.dval = 0


class Prog:
    ENGS = ("pe", "act", "dve", "pool", "sp")
    NDMA = {"pe": 8, "act": 8, "dve": 8, "pool": 6, "sp": 8}

    def __init__(self):
        self.ops = {e: [] for e in self.ENGS}
        self.ndma = {e: 0 for e in self.ENGS}
        self.dma_ops = {e: [] for e in self.ENGS}
        self.tag = ""

    def op(self, eng, fn, reads=(), writes=(), dma=False, pe_acc=False):
        o = Op(eng, fn, dma)
        o.tag = self.tag
        deps = []
        for r in reads:
            if r.lw is not None:
                deps.append(r.lw)
            for b in r.ov:
                if b.lw is not None:
                    deps.append(b.lw)
        for w in writes:
            if w.lw is not None:
                deps.append(w.lw)
            deps.extend(w.rd)
            for b in w.ov:
                if b.lw is not None:
                    deps.append(b.lw)
                deps.extend(b.rd)
        seen = set()
        for d in deps:
            if id(d) in seen or d is o:
                continue
            seen.add(id(d))
            if eng == "pe" and d.eng == "pe" and not d.dma:
                continue
            d.signal = True
            o.deps.append(d)
        for r in reads:
            r.rd.append(o)
        for w in writes:
            w.lw = o
            w.rd = []
        if dma:
            i = self.ndma[eng]
            self.ndma[eng] += 1
            o.idx = i
            if i >= self.NDMA[eng]:
                prev = self.dma_ops[eng][i - self.NDMA[eng]]
                o.deps.append(prev)
            self.dma_ops[eng].append(o)
            o.signal = True
        self.ops[eng].append(o)
        return o

    def emit(self, nc, es, final_wait_eng="sp"):
        sems = {e: es.enter_context(nc.semaphore("s_" + e)) for e in self.ENGS}
        dsems = {e: [es.enter_context(nc.semaphore("d_%s%d" % (e, i))) for i in range(self.NDMA[e])]
                 for e in self.ENGS if self.ndma[e] > 0}
        for e in self.ENGS:
            c = 0
            for o in self.ops[e]:
                if o.dma:
                    o.dsem = dsems[e][o.idx % self.NDMA[e]]
                    o.dval = 16 * (o.idx // self.NDMA[e] + 1)
                elif o.signal:
                    c += 1
                    o.count = c
        block = es.enter_context(nc.Block())
        handles = {"pe": block.tensor, "act": block.scalar, "dve": block.vector, "pool": block.gpsimd, "sp": block.sync}
        prog = self

        def make(e):
            def body(eng):
                waited = {}
                for o in prog.ops[e]:
                    need = {}
                    for d in o.deps:
                        if d.dma:
                            s, v = d.dsem, d.dval
                        else:
                            s, v = sems[d.eng], d.count
                        k = id(s)
                        if waited.get(k, 0) >= v:
                            continue
                        if k not in need or need[k][1] < v:
                            need[k] = (s, v)
                    for k, (s, v) in need.items():
                        eng.wait_ge(s, v)
                        waited[k] = v
                    ins = o.fn(eng)
                    if o.dma:
                        ins.then_inc(o.dsem, 16)
                    elif o.signal:
                        ins.then_inc(sems[e], 1)
                if e == final_wait_eng:
                    for e2 in prog.ENGS:
                        n = prog.ndma[e2]
                        for i in range(min(n, prog.NDMA[e2])):
                            cnt = (n - 1 - i) // prog.NDMA[e2] + 1
                            eng.wait_ge(dsems[e2][i], 16 * cnt)
            return body

        for e in self.ENGS:
            handles[e](make(e))


class Cfg:
    def __init__(self, ntm=16, ntp=16, tpb=4):
        self.ntm = ntm
        self.ntp = ntp
        self.tpb = tpb
        assert ntm % tpb == 0 and ntp % tpb == 0


def build(cfg):
    nc = bass.Bass("TRN2", target_bir_lowering=False)
    P = Prog()
    TPB = cfg.tpb
    TP = TPB * 128
    TM = TP + 64
    T = TM + 2

    def din(name, shape):
        return nc.dram_tensor(name, list(shape), F32, kind="ExternalInput").ap()

    def dout(name, shape):
        return nc.dram_tensor(name, list(shape), F32, kind="ExternalOutput").ap()

    xm = din("xm", (cfg.ntm * 128, D))
    xp = din("xp", (cfg.ntp * 128, D))
    xs = din("xs", (64, D))
    xh = din("xh", (2, D))
    cT_d = din("cT", (128, KT * 3))
    flag_d = din("flag", (128, 1))
    s0_d = din("s0", (2, NH, 128, 256))
    cc0_d = din("cc0", (4, D))
    w_mod = din("w_mod", (D, 6 * D))
    w_in = din("w_in", (D, IN_DIM))
    w_alpha = din("w_alpha", (16, 512))
    w_gla_out = din("w_gla_out", (D, D))
    w_conv_out = din("w_conv_out", (D, D))
    w_o = din("w_o", (D, D))
    w_ffn_in = din("w_ffn_in", (D, 2 * DFF))
    w_ffn_out = din("w_ffn_out", (DFF, D))
    vecs_d = din("vecs", (128, NV))
    bcp_d = din("bcp", (128, 4 * D))
    ident_d = din("ident", (128, 128))
    maskT_d = din("maskT", (128, 128))
    rmask_d = din("rmask", (128, T))

    ym = dout("ym", (cfg.ntm * 128, D))
    ys = dout("ys", (64, D))
    sp_o = dout("sp", (NH, 128, 256))
    cp_o = dout("cp", (2, D))
    ss_o = dout("ss", (2, NH, 128, 256))
    cs_o = dout("cs", (4, D))

    es = ExitStack()
    with es:
        def sb(name, shape, dt=F32):
            return es.enter_context(nc.sbuf_tensor("sb_" + name, list(shape), dt))

        NW = 8
        wring = sb("wring", (128, NW // 2, KT, 512), BF16)
        wregs = [Reg("w%d" % i) for i in range(NW)]
        wa = sb("wa", (128, KT, 16), BF16); r_wa = Reg("wa")
        walpha = sb("walpha", (16, 512), BF16); r_walpha = Reg("walpha")
        vecs = sb("vecs", (128, NV)); r_vecs = Reg("vecs")
        gn4 = sb("gn4", (128, 512)); r_gn4 = Reg("gn4")
        nfg = sb("nfg", (128, D)); r_nfg = Reg("nfg")
        identf = sb("identf", (128, 128)); r_identf = Reg("identf")
        identb = sb("identb", (128, 128), BF16); r_identb = Reg("identb")
        maskT = sb("maskT", (128, 128)); r_maskT = Reg("maskT")
        rmask = sb("rmask", (128, T)); r_rmask = Reg("rmask")
        flag = sb("flag", (128, 1)); r_flag = Reg("flag")
        cTf = sb("cTf", (128, KT, 3)); r_cTf = Reg("cTf")
        cTb = sb("cTb", (128, KT, 3), BF16); r_cTb = Reg("cTb")
        r_cB = Reg("cB")
        gB = sb("gB", (128, 2, 2, D)); r_gB = [[Reg("gB%d%d" % (a, b)) for b in range(2)] for a in range(2)]
        modT = sb("modT", (128, 4, KT, 3)); r_modT = [Reg("modT%d" % i) for i in range(4)]
        consts = sb("consts", (128, 4)); r_consts = Reg("consts")
        nbal = sb("nbal", (128, 4)); r_nbal = Reg("nbal")
        hT = sb("hT", (128, KT, T), BF16); r_hT = [Reg("hT%d" % k) for k in range(KT)]
        NXT = TPB + 1
        xres = sb("xres", (128, NXT, D)); r_xres = [[Reg("xres%d_%d" % (i, hf)) for hf in range(2)] for i in range(NXT)]
        ssq = sb("ssq", (128, 16)); r_ssq = [Reg("ssq%d" % i) for i in range(4)]
        rstd = sb("rstd", (128, 16)); r_rstd = [Reg("rstd%d" % i) for i in range(4)]
        NXN = 2
        xnb = sb("xnb", (128, NXN, D), BF16); r_xnb = [Reg("xnb%d" % i) for i in range(NXN)]

        NT3 = 4
        tmpA = sb("tmpA", (128, NT3, D)); r_tmpA = [Reg("tmpA%d" % i) for i in range(NT3)]
        NT4 = 4
        tmpB = sb("tmpB", (128, NT4, T)); r_tmpB = [Reg("tmpB%d" % i) for i in range(NT4)]
        aT = sb("aT", (16, T), BF16); r_aT = Reg("aT")
        S32 = sb("S32", (128, NH, 256)); r_S32 = [Reg("S32_%d" % h) for h in range(NH)]
        Sbf = sb("Sbf", (128, NH, 256), BF16); r_Sbf = [Reg("Sbf_%d" % h) for h in range(NH)]
        Sbfs = sb("Sbfs", (128, 2, NH, 256), BF16); r_Sbfs = [Reg("Sbfs_%d" % s) for s in range(2)]
        eb = sb("eb", (128, NH, TPB + 2)); r_eb = [Reg("eb%d" % h) for h in range(NH)]
        eb2 = sb("eb2", (128, NH, TPB + 2)); r_eb2 = [Reg("eb2_%d" % h) for h in range(NH)]
        ucbF = sb("ucbT", (128, max(KT * TM, 2 * KT * 128)), BF16); r_ucbT = [Reg("ucbT%d" % k) for k in range(KT)]
        ucbT = ucbF[:, 0:KT * TM].rearrange("p (k t) -> p k t", k=KT)
        cB = ucbF[:, 0:2 * KT * 128].rearrange("p (a k r) -> p a k r", a=2, k=KT)
        for rj in r_ucbT:
            overlap(r_cB, rj)
        NSM = 2
        sTm = sb("sTm", (128, NSM, NH, 128), BF16); r_sTm = [Reg("sTm%d" % i) for i in range(NSM)]
        kdtm = sb("kdtm", (128, NSM, NH, 128), BF16); r_kdtm = [Reg("kdtm%d" % i) for i in range(NSM)]
        stmp = sb("stmp", (128, 2, 256)); r_stmp = [Reg("stmp%d" % i) for i in range(2)]
        ssqo = sb("ssqo", (128, 2, 4)); r_ssqo = [Reg("ssqo%d" % i) for i in range(2)]
        rstdo = sb("rstdo", (128, 2, 4)); r_rstdo = [Reg("rstdo%d" % i) for i in range(2)]
        ogb = sb("ogb", (128, 2, D), BF16); r_ogb = [Reg("ogb%d" % i) for i in range(2)]
        ogT = sb("ogT", (128, KT, TM), BF16); r_ogT = Reg("ogT")
        ue = sb("ue", (128, 2, TP + 2)); r_ue = [Reg("ue%d" % i) for i in range(2)]
        ues = sb("ues", (128, 2, 2, 34)); r_ues = [Reg("ues%d" % i) for i in range(2)]
        uprev = sb("uprev", (128, KT, 2)); r_uprev = Reg("uprev")
        prevS = sb("prevS", (128, KT, 4)); r_prevS = Reg("prevS")
        ulastS = sb("ulastS", (128, KT, 4)); r_ulastS = Reg("ulastS")
        cacc = sb("cacc", (128, 1, TM)); r_cacc = [Reg("cacc0")]
        lay = {}
        off = 0
        for nm, sz in (("bc", NH * TM * 2), ("qdT", NH * TM), ("kdT", NH * TM), ("vtm", (TPB + 2) * D), ("gsg", (TPB + 2) * D)):
            lay[nm] = (off, sz, "G1")
            off += sz
        NA = off
        lay["Pa"] = (0, KT * TM, "G2")
        lay["mT"] = (KT * TM, KT * TM, "G2")
        lay["actT"] = (NA - 22 * TM, 22 * TM, "G3")
        lay["kdP1"] = (lay["qdT"][0], NH * TP, "P1")
        o1 = lay["vtm"][0] + TPB * D
        lay["bcP1"] = (o1, NH * TP * 2, "P1")
        lay["vtP1"] = (o1 + NH * TP * 2, TPB * D, "P1")
        assert o1 + NH * TP * 2 + TPB * D <= NA and NH * TP <= lay["qdT"][1] and lay["actT"][0] >= lay["kdT"][0]
        arena = sb("arena", (128, NA), BF16)

        def aview(nm):
            lo, sz, _ = lay[nm]
            return arena[:, lo:lo + sz]

        bc = aview("bc").bitcast(F32).rearrange("p (h t) -> p h t", h=NH)
        qdT = aview("qdT").rearrange("p (h t) -> p h t", h=NH)
        kdT = aview("kdT").rearrange("p (h t) -> p h t", h=NH)
        vtm = aview("vtm").rearrange("p (n d) -> p n d", d=D)
        gsg = aview("gsg").rearrange("p (n d) -> p n d", d=D)
        Pa = aview("Pa").rearrange("p (k t) -> p k t", k=KT)
        mT = aview("mT").rearrange("p (k t) -> p k t", k=KT)
        actT = aview("actT").rearrange("p (k t) -> p k t", k=22)
        kdP1 = aview("kdP1").rearrange("p (h t) -> p h t", h=NH)
        bcP1 = aview("bcP1").bitcast(F32).rearrange("p (h t) -> p h t", h=NH)
        vtP1 = aview("vtP1").rearrange("p (n d) -> p n d", d=D)
        areg = []

        def mk(nm, nparts):
            lo, sz, grp_ = lay[nm]
            step = sz // nparts
            lst = []
            for i in range(nparts):
                r = Reg("%s_%d" % (nm, i))
                areg.append((r, lo + i * step, lo + (i + 1) * step, grp_))
                lst.append(r)
            return lst

        r_bc = mk("bc", NH)
        r_qdT = mk("qdT", NH)
        r_kdT = mk("kdT", NH)
        r_vtm = mk("vtm", TPB + 2)
        r_gsg = mk("gsg", TPB + 2)
        r_Pa = mk("Pa", KT)
        r_mT = mk("mT", KT)
        r_actT = mk("actT", 22)
        r_kdP1 = mk("kdP1", NH)
        r_bcP1 = mk("bcP1", NH)
        r_vtP1 = mk("vtP1", TPB)
        for ia in range(len(areg)):
            for ib in range(ia + 1, len(areg)):
                (ra, lo1, hi1, g1), (rb, lo2, hi2, g2) = areg[ia], areg[ib]
                if g1 != g2 and lo1 < hi2 and lo2 < hi1:
                    overlap(ra, rb)
        bcP, r_bcP = [bc, bcP1], [r_bc, r_bcP1]
        kdP, r_kdP = [kdT, kdP1], [r_kdT, r_kdP1]
        vtP, r_vtP = [vtm, vtP1], [r_vtm, r_vtP1]
        ebP, r_ebP = [eb, eb2], [r_eb, r_eb2]

        psum = es.enter_context(nc.psum_tensor("psum", [128, 8, 512], F32))
        r_ps = [Reg("ps%d" % i) for i in range(8)]
        ps_ctr = [0]

        reserved = set()

        def newbank():
            while True:
                i = ps_ctr[0] % 8
                ps_ctr[0] += 1
                if i not in reserved:
                    return i

        def psf(b):
            return psum[:, b, :]

        def psb(b):
            return psum[:, b, :].bitcast(BF16)

        obanks = [[0, 1], [2, 3]]
        ctr = {"ob": 0, "xnb": 0, "ssq": 0, "tmpA": 0, "tmpB": 0, "w": 0, "sm": 0, "st": 0, "so": 0, "og": 0, "ue": 0, "ca": 0}

        held = {"tmpA": set()}

        def rot(name, n):
            while True:
                i = ctr[name] % n
                ctr[name] += 1
                if i not in held.get(name, ()):
                    return i

        def dma(eng, out, in_, reads, writes):
            return P.op(eng, lambda e: e.dma_start(out=out, in_=in_), reads=reads, writes=writes, dma=True)

        def act(out, in_, func, reads, writes, bias=None, scale=None, accum_out=None):
            kw = {}
            if bias is not None:
                kw["bias"] = bias
            if scale is not None:
                kw["scale"] = scale
            if accum_out is not None:
                kw["accum_out"] = accum_out
            return P.op("act", lambda e: e.activation(out=out, in_=in_, func=func, **kw), reads=reads, writes=writes)

        def tt(out, in0, in1, op, reads, writes, eng="dve"):
            return P.op(eng, lambda e: e.tensor_tensor(out=out, in0=in0, in1=in1, op=op), reads=reads, writes=writes)

        def ts(out, in0, s1, s2, op0, op1, reads, writes, eng="dve"):
            if s2 is None:
                return P.op(eng, lambda e: e.tensor_scalar(out=out, in0=in0, scalar1=s1, scalar2=None, op0=op0), reads=reads, writes=writes)
            return P.op(eng, lambda e: e.tensor_scalar(out=out, in0=in0, scalar1=s1, scalar2=s2, op0=op0, op1=op1), reads=reads, writes=writes)

        def stt(out, in0, scalar, in1, op0, op1, reads, writes, eng="dve"):
            return P.op(eng, lambda e: e.scalar_tensor_tensor(out=out, in0=in0, scalar=scalar, in1=in1, op0=op0, op1=op1), reads=reads, writes=writes)

        def cp(out, in_, reads, writes, eng="dve"):
            return P.op(eng, lambda e: e.tensor_copy(out=out, in_=in_), reads=reads, writes=writes)

        def mm(out, lhsT, rhs, start, stop, reads, writes):
            return P.op("pe", lambda e: e.matmul(out, lhsT=lhsT, rhs=rhs, start=start, stop=stop), reads=reads, writes=writes)

        def tr(out, in_, ident, reads, writes):
            return P.op("pe", lambda e: e.transpose(out=out, in_=in_, identity=ident), reads=reads, writes=writes)

        class W:
            def __init__(self, slots):
                self.slots = slots
                self.regs = [wregs[i] for i in slots]

            def lhs(self, k, col, m):
                sl = self.slots[col // 256]
                c = (sl % 2) * 256 + col % 256
                return wring[:, sl // 2, k, c:c + m]

            def rhs(self, k):
                return wring[:, self.slots[0] // 2, k, :]

        wreserved = set()
        wcache = {}
        cur_blk = [0]
        CACHE_W = True

        def wfetch(src_ap, nk, dst, dregs, ncols):
            key = (src_ap.name, src_ap.offset, tuple(src_ap.shape))
            cacheable = CACHE_W and src_ap.name != "w_mod" and (cur_blk[0] >= 1 or not src_ap.name.startswith("w_ffn"))
            if cacheable and key in wcache:
                sc_ap, sreg = wcache[key]
                dma("pool", dst, sc_ap.rearrange("p (k c) -> p k c", k=nk), reads=[sreg], writes=dregs)
                wflush(2)
                return
            dma("pool", dst, src_ap.rearrange("(k p) c -> p k c", p=128), reads=[], writes=dregs)
            if cacheable and key not in wpend_keys:
                wpend.append((key, nk, ncols, dst, dregs, ctr["w"]))
                wpend_keys.add(key)
            wflush(2)

        wpend = []
        wpend_keys = set()
        wnum = [0]
        WB_Q = "pool"

        def wflush(keep):
            while wpend and (len(wpend) > keep or ctr["w"] - wpend[0][5] >= 3):
                key, nk, ncols, dst, dregs, _ = wpend.pop(0)
                sc_ap = nc.dram_tensor("wsc%d" % wnum[0], [128, nk * ncols], BF16, kind="Internal").ap()
                sreg = Reg("wsc%d" % wnum[0])
                wnum[0] += 1
                dma(WB_Q, sc_ap.rearrange("p (k c) -> p k c", k=nk), dst, reads=dregs, writes=[sreg])
                wcache[key] = (sc_ap, sreg)
                wpend_keys.discard(key)

        def wload(src_ap, nk=KT):
            while True:
                if ctr["w"] % 2:
                    ctr["w"] += 1
                s0 = ctr["w"] % NW
                ctr["w"] += 2
                if s0 not in wreserved and s0 + 1 not in wreserved:
                    break
            wfetch(src_ap, nk, wring[:, s0 // 2, 0:nk, :], [wregs[s0], wregs[s0 + 1]], 512)
            return W([s0, s0 + 1])

        def wload1(src_ap, nk=KT):
            while True:
                s0 = ctr["w"] % NW
                ctr["w"] += 1
                if s0 not in wreserved:
                    break
            wfetch(src_ap, nk, wring[:, s0 // 2, 0:nk, (s0 % 2) * 256:(s0 % 2 + 1) * 256], [wregs[s0]], 256)
            return W([s0])

        def groups_of(ncols):
            g = []
            c = 0
            while c < min(ncols, TP):
                n = min(512, TP - c)
                g.append((c, n))
                c += n
            if ncols > TP:
                g.append((TP, ncols - TP))
            return g

        P.tag = "setup"
        dma("sp", vecs[:], vecs_d[:, :], [], [r_vecs])
        dma("sp", identf[:], ident_d[:, :], [], [r_identf])
        dma("sp", maskT[:], maskT_d[:, :], [], [r_maskT])
        dma("sp", rmask[:], rmask_d[:, :], [], [r_rmask])
        dma("sp", flag[:], flag_d[:, :], [], [r_flag])
        dma("sp", cTf[:], cT_d.rearrange("p (k s) -> p k s", s=3), [], [r_cTf])
        dma("sp", gn4[:], bcp_d[:, 0:512], [], [r_gn4])
        dma("sp", nfg[:], bcp_d[:, D:2 * D], [], [r_nfg])
        dma("pool", wa[:], w_in[:, O_A:O_A + 16].rearrange("(k p) c -> p k c", p=128), [], [r_wa])
        dma("pool", walpha[:], w_alpha[:, :], [], [r_walpha])
        P.op("dve", lambda e: e.memset(consts[:, 0:1], EPS), writes=[r_consts])
        P.op("dve", lambda e: e.memset(consts[:, 1:2], 1.0), writes=[r_consts])
        P.op("dve", lambda e: e.memset(consts[:, 2:3], float(np.log(128.0 ** -0.5))), writes=[r_consts])
        P.op("dve", lambda e: e.memset(consts[:, 3:4], 0.0), writes=[r_consts])
        ts(nbal[:], vecs[:, V_BAL:V_BAL + 4], -1.0, None, ALU.mult, None, [r_vecs], [r_nbal])
        P.op("dve", lambda e: e.memset(ssq[:], 1.0), writes=r_ssq)
        P.op("dve", lambda e: e.memset(ssqo[:], 1.0), writes=r_ssqo)
        cp(identb[:], identf[:], [r_identf], [r_identb])
        cp(cTb[:], cTf[:], [r_cTf], [r_cTb])
        for k in range(KT):
            cp(cB[:, 0, k, :], cTf[:, k, 0:1].broadcast_to([128, 128]), [r_cTf], [r_cB])
            cp(cB[:, 1, k, 0:32], cTf[:, k, 1:2].broadcast_to([128, 32]), [r_cTf], [r_cB])
            cp(cB[:, 1, k, 32:64], cTf[:, k, 2:3].broadcast_to([128, 32]), [r_cTf], [r_cB])
        for h in range(NH):
            P.op("dve", lambda e, h=h: e.memset(S32[:, h, :], 0.0), writes=[r_S32[h]])
        P.op("dve", lambda e: e.memset(uprev[:], 0.0), writes=[r_uprev])
        b = newbank()
        tc0 = rot("tmpA", NT3)
        dma("sp", tmpA[0:4, tc0, :], cc0_d[:, :], [], [r_tmpA[tc0]])
        for c in range(KT):
            tr(psf(b)[:, c * 4:(c + 1) * 4], tmpA[0:4, tc0, c * 128:(c + 1) * 128], identf[0:4, 0:4], [r_tmpA[tc0], r_identf], [r_ps[b]])
        cp(prevS[:], psf(b)[:, 0:KT * 4].rearrange("p (c f) -> p c f", f=4), [r_ps[b]], [r_prevS])

        P.tag = "mod"
        def mod_fm_gen(parts):
            for (six, mi) in parts:
                for half in range(2):
                    c0 = six * D + half * 512
                    slot = wload(w_mod[:, c0:c0 + 512])
                    b = newbank()
                    for sub in range(4):
                        for k in range(KT):
                            mm(psf(b)[:, sub * 3:(sub + 1) * 3], slot.lhs(k, sub * 128, 128), cTb[:, k, :],
                               k == 0, k == KT - 1, slot.regs + [r_cTb], [r_ps[b]])
                    for sub in range(4):
                        kk = half * 4 + sub
                        col = V_BMOD + six * 8 + kk
                        ts(modT[:, mi, kk, :], psf(b)[:, sub * 3:(sub + 1) * 3], vecs[:, col:col + 1], None, ALU.add, None,
                           [r_ps[b], r_vecs], [r_modT[mi]])
                    yield
                if mi in (1, 3):
                    voff = V_N1 if mi == 1 else V_N2
                    for s_ in range(3):
                        stt(modT[:, mi, :, s_], modT[:, mi, :, s_], 1.0, vecs[:, voff:voff + 8], ALU.add, ALU.mult,
                            [r_modT[mi], r_vecs], [r_modT[mi]])

        def mod_tm_gen():
            for gi, six in enumerate((2, 5)):
                tb = rot("tmpA", NT3)
                dma("sp", tmpA[:, tb, :], bcp_d[:, (2 + gi) * D:(3 + gi) * D], [], [r_tmpA[tb]])
                for half in range(2):
                    c0 = six * D + half * 512
                    slot = wload(w_mod[:, c0:c0 + 512])
                    for ty in range(2):
                        b = newbank()
                        R = 128 if ty == 0 else 64
                        for k in range(KT):
                            mm(psf(b)[0:R, :], cB[:, ty, k, 0:R], slot.rhs(k), k == 0, k == KT - 1,
                               [r_cB] + slot.regs, [r_ps[b]])
                        tt(gB[0:R, gi, ty, half * 512:(half + 1) * 512], psf(b)[0:R, :], tmpA[0:R, tb, half * 512:(half + 1) * 512], ALU.add,
                           [r_ps[b], r_tmpA[tb]], [r_gB[gi][ty]])
                    yield

        def mod_rest_gen():
            P.tag = "mod"
            yield from mod_fm_gen([(3, 2), (4, 3)])
            yield from mod_tm_gen()

        PRE0 = {}
        for _ in mod_fm_gen([(0, 0), (1, 1)]):
            pass

        def run(g):
            for _ in g:
                pass

        def tag_gen(g, tag):
            while True:
                P.tag = tag
                try:
                    next(g)
                except StopIteration:
                    return
                yield

        def interleave(*gens):
            gens = [g for g in gens if g is not None]
            while gens:
                for g in list(gens):
                    try:
                        next(g)
                    except StopIteration:
                        gens.remove(g)

        def norm_stats(grp_):
            gi = rot("ssq", 4)
            info = []
            Rm = max(t["R"] for t in grp_)
            for j, t in enumerate(grp_):
                R = t["R"]
                if "dram" in t:
                    ta = rot("tmpA", NT3)
                    dma("sp", tmpA[0:R, ta, :], t["dram"], [], [r_tmpA[ta]])
                    src, regs = tmpA[0:R, ta, :], [r_tmpA[ta]]
                else:
                    src, regs = t["src"], t["regs"]
                xi = rot("xnb", NXN)
                sc = gi * 4 + j
                act(xnb[0:R, xi, :], src, AF.Square, regs, [r_xnb[xi], r_ssq[gi]], accum_out=ssq[0:R, sc:sc + 1])
                info.append((src, regs, gi, sc))
            c0_, c1_ = gi * 4, gi * 4 + len(grp_)
            act(rstd[0:Rm, c0_:c1_], ssq[0:Rm, c0_:c1_], AF.Ln, [r_ssq[gi], r_consts], [r_rstd[gi]], scale=1.0 / D, bias=consts[0:Rm, 0:1])
            act(rstd[0:Rm, c0_:c1_], rstd[0:Rm, c0_:c1_], AF.Exp, [r_rstd[gi]], [r_rstd[gi]], scale=-0.5)
            return info

        def norm_apply_gen(grp_, info, G_idx, SH_idx):
            for t, (src, regs, gi, sc) in zip(grp_, info):
                R = t["R"]
                xi = rot("xnb", NXN)
                ts(xnb[0:R, xi, :], src, rstd[0:R, sc:sc + 1], None, ALU.mult, None, regs + [r_rstd[gi]], [r_xnb[xi]])
                tslot = r_tmpA.index(regs[0]) if (len(regs) == 1 and regs[0] in r_tmpA) else None
                norm_post(t, xi, G_idx, SH_idx, tslot)
                yield

        def norm_gen(tiles, G_idx, SH_idx, batch=1):
            for g0 in range(0, len(tiles), batch):
                grp_ = tiles[g0:g0 + batch]
                info = norm_stats(grp_)
                yield from norm_apply_gen(grp_, info, G_idx, SH_idx)

        def gate_gen(ncols, BC, r_BC, EB, r_EB):
            gr = groups_of(ncols)
            for (c0, n) in gr:
                b = newbank()
                for k in range(KT):
                    mm(psf(b)[0:16, 0:n], wa[:, k, :], hT[:, k, c0:c0 + n], k == 0, k == KT - 1, [r_wa, r_hT[k]], [r_ps[b]])
                cp(aT[:, c0:c0 + n], psf(b)[0:16, 0:n], [r_ps[b]], [r_aT])
            yield
            for h in range(NH):
                ti = rot("tmpB", NT4)
                for (c0, n) in gr:
                    b = newbank()
                    mm(psf(b)[:, 0:n], walpha[:, h * 128:(h + 1) * 128], aT[:, c0:c0 + n], True, True, [r_walpha, r_aT], [r_ps[b]])
                    act(tmpB[:, ti, c0:c0 + n], psf(b)[:, 0:n], AF.Exp, [r_ps[b], r_nbal], [r_tmpB[ti]], scale=-1.0, bias=nbal[:, h:h + 1])
                act(tmpB[:, ti, 0:ncols], tmpB[:, ti, 0:ncols], AF.Ln, [r_tmpB[ti], r_consts], [r_tmpB[ti]], bias=consts[:, 1:2])
                P.op("dve", lambda e, h=h, ti=ti: e.tensor_tensor_scan(out=BC[:, h, 0:ncols], data0=rmask[:, 0:ncols], data1=tmpB[:, ti, 0:ncols],
                                                                        initial=0.0, op0=ALU.mult, op1=ALU.add),
                     reads=[r_rmask, r_tmpB[ti]], writes=[r_BC[h]])
                np_ = min(ncols, TP) // 128
                act(EB[:, h, 0:np_], BC[:, h, 127:np_ * 128:128], AF.Exp, [r_BC[h]], [r_EB[h]], scale=-1.0 / 16)
                if ncols > TP:
                    act(EB[:, h, TPB:TPB + 2], BC[:, h, TP + 31:TP + 64:32], AF.Exp, [r_BC[h]], [r_EB[h]], scale=-1.0 / 16)
                yield

        kpre = {}

        def prefetch_k():
            kpre[0] = wload1(w_in[:, O_K:O_K + 256])
            wreserved.add(kpre[0].slots[0])

        def qk_gen(w_off, ncols, BC, r_BC, OUT, r_OUT, is_q):
            used_pre = None
            for h in range(NH):
                if h % 2 == 0:
                    if w_off == O_K and h == 0 and 0 in kpre:
                        slot = kpre.pop(0)
                        used_pre = slot
                    else:
                        if used_pre is not None:
                            wreserved.discard(used_pre.slots[0])
                            used_pre = None
                        slot = wload1(w_in[:, w_off + h * 128:w_off + h * 128 + 256])
                for (c0, n) in groups_of(ncols):
                    b = newbank()
                    for k in range(KT):
                        mm(psf(b)[:, 0:n], slot.lhs(k, (h % 2) * 128, 128), hT[:, k, c0:c0 + n], k == 0, k == KT - 1,
                           slot.regs + [r_hT[k]], [r_ps[b]])
                    ti = rot("tmpB", NT4)
                    if is_q:
                        act(tmpB[:, ti, 0:n], BC[:, h, c0:c0 + n], AF.Exp, [r_BC[h], r_consts], [r_tmpB[ti]], scale=-1.0 / 16, bias=consts[:, 2:3])
                    else:
                        act(tmpB[:, ti, 0:n], BC[:, h, c0:c0 + n], AF.Exp, [r_BC[h]], [r_tmpB[ti]], scale=1.0 / 16)
                    tt(OUT[:, h, c0:c0 + n], psf(b)[:, 0:n], tmpB[:, ti, 0:n], ALU.mult, [r_ps[b], r_tmpB[ti]], [r_OUT[h]])
                    yield

        def tm_proj(slot, lhs, lhs_regs, col0, R, nk=KT, kbase=0, start=True, stop=True, bank=None):
            b = newbank() if bank is None else bank
            for k in range(nk):
                mm(psf(b)[0:R, :], lhs[:, kbase + k, col0:col0 + R], slot.rhs(k), start and k == 0, stop and k == nk - 1,
                   slot.regs + lhs_regs, [r_ps[b]])
            return b

        def v_gen(slot, half, gl_tiles, VT, r_VT):
            for ti_, t in enumerate(gl_tiles):
                R = t["R"]
                b = tm_proj(slot, hT, r_hT, t["col0"], R)
                if ti_ % 2 == 0:
                    act(VT[0:R, t["vi"], half * 512:(half + 1) * 512], psf(b)[0:R, :], AF.Copy, [r_ps[b]], [r_VT[t["vi"]]])
                else:
                    cp(VT[0:R, t["vi"], half * 512:(half + 1) * 512], psf(b)[0:R, :], [r_ps[b]], [r_VT[t["vi"]]])
                yield

        def g_gen(slot, half, gl_tiles):
            for t in gl_tiles:
                R = t["R"]
                b = tm_proj(slot, hT, r_hT, t["col0"], R)
                ta = rot("tmpA", NT3)
                act(tmpA[0:R, ta, 0:512], psf(b)[0:R, :], AF.Silu, [r_ps[b]], [r_tmpA[ta]])
                tt(gsg[0:R, t["vi"], half * 512:(half + 1) * 512], tmpA[0:R, ta, 0:512], gn4[0:R, :], ALU.mult,
                   [r_tmpA[ta], r_gn4], [r_gsg[t["vi"]]])
                yield

        def gla_A(t, KD, r_KD, full):
            R, c0 = t["R"], t["col0"]
            si = rot("sm", NSM)
            t["si"] = si
            if "samp" in t:
                s = t["samp"]
                ta = rot("tmpA", NT3)
                held["tmpA"].add(ta)
                t["s_ta"] = ta
                dma("sp", tmpA[:, ta, :].rearrange("p (h v) -> p h v", h=NH), s0_d[s].rearrange("h p v -> p h v"), [], [r_tmpA[ta]])
                cp(Sbfs[:, s, :, :], tmpA[:, ta, :].rearrange("p (h v) -> p h v", h=NH), [r_tmpA[ta]], [r_Sbfs[s]])
            bB = newbank()
            if full:
                bA = newbank()
            for h in range(NH):
                if full:
                    mm(psf(bA)[0:R, h * 128:h * 128 + R], KD[:, h, c0:c0 + R], qdT[:, h, c0:c0 + R], True, True,
                       [r_KD[h], r_qdT[h]], [r_ps[bA]])
                tr(psb(bB)[0:R, h * 128:(h + 1) * 128], KD[:, h, c0:c0 + R], identb[:, :], [r_KD[h], r_identb], [r_ps[bB]])
            if full:
                for h in range(NH):
                    tt(sTm[0:R, si, h, 0:R], psf(bA)[0:R, h * 128:h * 128 + R], maskT[0:R, 0:R], ALU.mult, [r_ps[bA], r_maskT], [r_sTm[si]])
            act(kdtm[0:R, si, :, :], psb(bB)[0:R, 0:512].rearrange("p (h d) -> p h d", h=NH), AF.Copy, [r_ps[bB]], [r_kdtm[si]])

        def gla_B(t, chunk_idx, VT, r_VT, EB, r_EB, full):
            R, c0, vi, si = t["R"], t["col0"], t["vi"], t["si"]
            samp = t.get("samp")
            bO = [newbank(), newbank()] if full else None
            bS = [newbank(), newbank()]
            for h in range(NH):
                if full:
                    ob = psf(bO[h // 2])[0:R, (h % 2) * 256:(h % 2 + 1) * 256]
                    mm(ob, sTm[0:R, si, h, 0:R], VT[0:R, vi, h * 256:(h + 1) * 256], True, False, [r_sTm[si], r_VT[vi]], [r_ps[bO[h // 2]]])
                    if samp is None:
                        mm(ob, qdT[:, h, c0:c0 + R], Sbf[:, h, :], False, True, [r_qdT[h], r_Sbf[h]], [r_ps[bO[h // 2]]])
                    else:
                        mm(ob, qdT[:, h, c0:c0 + R], Sbfs[:, samp, h, :], False, True, [r_qdT[h], r_Sbfs[samp]], [r_ps[bO[h // 2]]])
                mm(psf(bS[h // 2])[:, (h % 2) * 256:(h % 2 + 1) * 256], kdtm[0:R, si, h, :], VT[0:R, vi, h * 256:(h + 1) * 256], True, True,
                   [r_kdtm[si], r_VT[vi]], [r_ps[bS[h // 2]]])
            for h in range(NH):
                st_ = rot("st", 2)
                sp_ = psf(bS[h // 2])[:, (h % 2) * 256:(h % 2 + 1) * 256]
                e_ = EB[:, h, chunk_idx:chunk_idx + 1]
                if samp is None:
                    tt(stmp[:, st_, :], sp_, S32[:, h, :], ALU.add, [r_ps[bS[h // 2]], r_S32[h]], [r_stmp[st_]])
                    act(S32[:, h, :], stmp[:, st_, :], AF.Copy, [r_stmp[st_], r_EB[h]], [r_S32[h]], scale=e_)
                    if full:
                        ts(Sbf[:, h, :], stmp[:, st_, :], e_, None, ALU.mult, None, [r_stmp[st_], r_EB[h]], [r_Sbf[h]])
                else:
                    ta = t["s_ta"]
                    s32 = tmpA[:, ta, h * 256:(h + 1) * 256]
                    tt(stmp[:, st_, :], sp_, s32, ALU.add, [r_ps[bS[h // 2]], r_tmpA[ta]], [r_stmp[st_]])
                    act(s32, stmp[:, st_, :], AF.Copy, [r_stmp[st_], r_EB[h]], [r_tmpA[ta]], scale=e_)
            if samp is not None:
                ta = t["s_ta"]
                dma("sp", ss_o[samp].rearrange("h p v -> p h v"), tmpA[:, ta, :].rearrange("p (h v) -> p h v", h=NH), [r_tmpA[ta]], [])
                held["tmpA"].discard(ta)
            if not full:
                return
            so = rot("so", 2)
            og = rot("og", 2)
            t["og"] = og
            to = rot("tmpA", NT3)
            for j in range(2):
                act(tmpA[0:R, to, j * 512:(j + 1) * 512], psf(bO[j])[0:R, :], AF.Copy, [r_ps[bO[j]]], [r_tmpA[to]])
            for h in range(NH):
                act(ogb[0:R, og, h * 256:(h + 1) * 256], tmpA[0:R, to, h * 256:(h + 1) * 256], AF.Square, [r_tmpA[to]],
                    [r_ogb[og], r_ssqo[so]], accum_out=ssqo[0:R, so, h:h + 1])
            act(rstdo[0:R, so, :], ssqo[0:R, so, :], AF.Ln, [r_ssqo[so], r_consts], [r_rstdo[so]], scale=1.0 / 256, bias=consts[0:R, 0:1])
            act(rstdo[0:R, so, :], rstdo[0:R, so, :], AF.Exp, [r_rstdo[so]], [r_rstdo[so]], scale=-0.5)
            for h in range(NH):
                stt(ogb[0:R, og, h * 256:(h + 1) * 256], tmpA[0:R, to, h * 256:(h + 1) * 256], rstdo[0:R, so, h:h + 1],
                    gsg[0:R, vi, h * 256:(h + 1) * 256], ALU.mult, ALU.mult, [r_tmpA[to], r_rstdo[so], r_gsg[vi]], [r_ogb[og]])

        def gla_C(t):
            R, c0, og = t["R"], t["col0"], t["og"]
            bT = newbank()
            for k in range(KT):
                tr(psb(bT)[:, k * 128:k * 128 + R], ogb[0:R, og, k * 128:(k + 1) * 128], identb[0:R, 0:R], [r_ogb[og], r_identb], [r_ps[bT]])
            cp(ogT[:, :, c0:c0 + R], psb(bT).rearrange("p (k r) -> p k r", k=KT)[:, :, 0:R], [r_ps[bT]], [r_ogT])

        def gla_gen(gl_tiles, KD, r_KD, VT, r_VT, EB, r_EB, full):
            n = len(gl_tiles)
            for step in range(n + 2):
                if step < n:
                    gla_A(gl_tiles[step], KD, r_KD, full)
                    yield
                if 0 <= step - 1 < n:
                    t = gl_tiles[step - 1]
                    ci = (TPB + t["samp"]) if "samp" in t else t["ci"]
                    gla_B(t, ci, VT, r_VT, EB, r_EB, full)
                    yield
                if full and 0 <= step - 2 < n:
                    if step - 2 >= n - 2:
                        yield
                        yield
                    gla_C(gl_tiles[step - 2])
                    yield

        def conv_gen(has_s, ncol, ncolx):
            grp = groups_of(ncol)
            grpx = groups_of(ncolx)
            for c2 in range(4):
                s_cc = wload1(w_in[:, O_CC + c2 * 256:O_CC + (c2 + 1) * 256])
                s_ch = wload1(w_in[:, O_CH + c2 * 256:O_CH + (c2 + 1) * 256])
                s_cb = wload1(w_in[:, O_CB + c2 * 256:O_CB + (c2 + 1) * 256])
                for cc_ in range(2):
                    c = c2 * 2 + cc_
                    ui = rot("ue", 2)
                    w0 = vecs[:, V_CW + 0 * 8 + c:V_CW + 0 * 8 + c + 1]
                    w1 = vecs[:, V_CW + 1 * 8 + c:V_CW + 1 * 8 + c + 1]
                    w2 = vecs[:, V_CW + 2 * 8 + c:V_CW + 2 * 8 + c + 1]
                    bb = vecs[:, V_CBIAS + c:V_CBIAS + c + 1]
                    cp(ue[:, ui, 0:2], uprev[:, c, :], [r_uprev], [r_ue[ui]])
                    if has_s:
                        cp(ues[:, ui, :, 0:2], prevS[:, c, :].rearrange("p (s r) -> p s r", r=2), [r_prevS], [r_ues[ui]])
                    t1 = rot("tmpB", NT4)
                    for (c0, n) in grpx:
                        b1 = newbank()
                        for k in range(KT):
                            mm(psf(b1)[:, 0:n], s_cc.lhs(k, cc_ * 128, 128), hT[:, k, c0:c0 + n], k == 0, k == KT - 1,
                               s_cc.regs + [r_hT[k]], [r_ps[b1]])
                        act(tmpB[:, t1, c0:c0 + n], psf(b1)[:, 0:n], AF.Copy, [r_ps[b1]], [r_tmpB[t1]])
                    yield
                    for (c0, n) in grpx:
                        b2 = newbank()
                        for k in range(KT):
                            mm(psf(b2)[:, 0:n], s_ch.lhs(k, cc_ * 128, 128), hT[:, k, c0:c0 + n], k == 0, k == KT - 1,
                               s_ch.regs + [r_hT[k]], [r_ps[b2]])
                        if c0 < TP:
                            tt(ue[:, ui, 2 + c0:2 + c0 + n], psf(b2)[:, 0:n], tmpB[:, t1, c0:c0 + n], ALU.mult, [r_ps[b2], r_tmpB[t1]], [r_ue[ui]])
                        else:
                            tt(ues[:, ui, :, 2:34], psf(b2)[:, 0:64].rearrange("p (s t) -> p s t", s=2),
                               tmpB[:, t1, c0:c0 + 64].rearrange("p (s t) -> p s t", s=2), ALU.mult, [r_ps[b2], r_tmpB[t1]], [r_ues[ui]])
                            stt(ue[:, ui, 0:2], psf(b2)[:, 64:66], flag[:, 0:1], tmpB[:, t1, c0 + 64:c0 + 66], ALU.mult, ALU.mult,
                                [r_ps[b2], r_tmpB[t1], r_flag], [r_ue[ui]])
                    yield
                    t2 = rot("tmpB", NT4)
                    for (c0, n) in grp:
                        b3 = newbank()
                        for k in range(KT):
                            mm(psf(b3)[:, 0:n], s_cb.lhs(k, cc_ * 128, 128), hT[:, k, c0:c0 + n], k == 0, k == KT - 1,
                               s_cb.regs + [r_hT[k]], [r_ps[b3]])
                        act(tmpB[:, t2, c0:c0 + n], psf(b3)[:, 0:n], AF.Copy, [r_ps[b3]], [r_tmpB[t2]])
                    cp(uprev[:, c, :], ue[:, ui, TP:TP + 2], [r_ue[ui]], [r_uprev])
                    if has_s:
                        cp(ulastS[:, c, :].rearrange("p (s r) -> p s r", r=2), ues[:, ui, :, 32:34], [r_ues[ui]], [r_ulastS])
                    ca = 0
                    ts(cacc[:, ca, 0:TP], ue[:, ui, 0:TP], w0, bb, ALU.mult, ALU.add, [r_ue[ui], r_vecs], [r_cacc[ca]])
                    stt(cacc[:, ca, 0:TP], ue[:, ui, 1:TP + 1], w1, cacc[:, ca, 0:TP], ALU.mult, ALU.add, [r_ue[ui], r_vecs, r_cacc[ca]], [r_cacc[ca]])
                    stt(cacc[:, ca, 0:TP], ue[:, ui, 2:TP + 2], w2, cacc[:, ca, 0:TP], ALU.mult, ALU.add, [r_ue[ui], r_vecs, r_cacc[ca]], [r_cacc[ca]])
                    if has_s:
                        cs3 = cacc[:, ca, TP:TM].rearrange("p (s t) -> p s t", s=2)
                        ts(cs3, ues[:, ui, :, 0:32], w0, bb, ALU.mult, ALU.add, [r_ues[ui], r_vecs], [r_cacc[ca]])
                        stt(cs3, ues[:, ui, :, 1:33], w1, cs3, ALU.mult, ALU.add, [r_ues[ui], r_vecs, r_cacc[ca]], [r_cacc[ca]])
                        stt(cs3, ues[:, ui, :, 2:34], w2, cs3, ALU.mult, ALU.add, [r_ues[ui], r_vecs, r_cacc[ca]], [r_cacc[ca]])
                    tt(ucbT[:, c, 0:ncol], cacc[:, ca, 0:ncol], tmpB[:, t2, 0:ncol], ALU.mult, [r_cacc[ca], r_tmpB[t2]], [r_ucbT[c]])
                    yield

        def cache_out(src, nrow, regs_src, dst):
            co = rot("tmpA", NT3)
            for half in range(2):
                b = newbank()
                for cc_ in range(4):
                    c = half * 4 + cc_
                    tr(psf(b)[0:nrow, cc_ * 128:(cc_ + 1) * 128], src[:, c, :], identf[:, :], [regs_src, r_identf], [r_ps[b]])
                cp(tmpA[0:nrow, co, half * 512:(half + 1) * 512], psf(b)[0:nrow, :], [r_ps[b]], [r_tmpA[co]])
            dma("sp", dst, tmpA[0:nrow, co, :], [r_tmpA[co]], [])

        def gated_fm_gen(w_gate_off, w_y, ysrc, r_ysrc, ncol, finish):
            grp = groups_of(ncol)
            for c2 in range(4):
                s_y = wload1(w_y[:, c2 * 256:(c2 + 1) * 256])
                s_g = wload1(w_in[:, w_gate_off + c2 * 256:w_gate_off + (c2 + 1) * 256])
                for cc_ in range(2):
                    c = c2 * 2 + cc_
                    for (c0, n) in grp:
                        bg = newbank()
                        for k in range(KT):
                            mm(psf(bg)[:, 0:n], s_g.lhs(k, cc_ * 128, 128), hT[:, k, c0:c0 + n], k == 0, k == KT - 1,
                               s_g.regs + [r_hT[k]], [r_ps[bg]])
                        ti = rot("tmpB", NT4)
                        act(tmpB[:, ti, 0:n], psf(bg)[:, 0:n], AF.Sigmoid, [r_ps[bg]], [r_tmpB[ti]])
                        by = newbank()
                        for k in range(KT):
                            mm(psf(by)[:, 0:n], s_y.lhs(k, cc_ * 128, 128), ysrc[:, k, c0:c0 + n], k == 0, k == KT - 1,
                               s_y.regs + r_ysrc, [r_ps[by]])
                        finish(c, c0, n, by, ti)
                        yield

        def norm_pre(src, regs, R):
            gi = rot("ssq", 4)
            sc = gi * 4
            xi = rot("xnb", NXN)
            act(xnb[0:R, xi, :], src, AF.Square, regs, [r_xnb[xi], r_ssq[gi]], accum_out=ssq[0:R, sc:sc + 1])
            act(rstd[0:R, sc:sc + 1], ssq[0:R, sc:sc + 1], AF.Ln, [r_ssq[gi], r_consts], [r_rstd[gi]], scale=1.0 / D, bias=consts[0:R, 0:1])
            act(rstd[0:R, sc:sc + 1], rstd[0:R, sc:sc + 1], AF.Exp, [r_rstd[gi]], [r_rstd[gi]], scale=-0.5)
            ts(xnb[0:R, xi, :], src, rstd[0:R, sc:sc + 1], None, ALU.mult, None, regs + [r_rstd[gi]], [r_xnb[xi]])
            return xi

        def norm_post(t, xi, G_idx, SH_idx, tslot=None):
            R = t["R"]
            b = newbank()
            for k in range(KT):
                tr(psb(b)[:, k * 128:k * 128 + R], xnb[0:R, xi, k * 128:(k + 1) * 128], identb[0:R, 0:R], [r_xnb[xi], r_identb], [r_ps[b]])
            if tslot is None:
                tslot = rot("tmpA", NT3)
            src3 = psb(b).rearrange("p (k t) -> p k t", k=KT)
            tmp3 = tmpA[:, tslot, :].rearrange("p (k t) -> p k t", k=KT)
            for (co, nn, seq) in t["seqruns"]:
                tt(tmp3[:, :, co:co + nn], src3[:, :, co:co + nn], modT[:, G_idx, :, seq:seq + 1].broadcast_to([128, KT, nn]), ALU.mult,
                   [r_ps[b], r_modT[G_idx]], [r_tmpA[tslot]])
                tt(hT[:, :, t["col0"] + co:t["col0"] + co + nn], tmp3[:, :, co:co + nn],
                   modT[:, SH_idx, :, seq:seq + 1].broadcast_to([128, KT, nn]), ALU.add,
                   [r_tmpA[tslot], r_modT[SH_idx]], r_hT)

        def wo_norm2_gen(tiles):
            slots = [wload(w_o[:, half * 512:(half + 1) * 512]) for half in range(2)]
            pend = None
            for t in tiles:
                R = t["R"]
                for half in range(2):
                    ta = rot("tmpA", NT3)
                    b = tm_proj(slots[half], mT, r_mT, t["col0"], R)
                    xr = xres[0:R, t["xi"], half * 512:(half + 1) * 512]
                    rx = r_xres[t["xi"]][half]
                    tt(tmpA[0:R, ta, 0:512], psf(b)[0:R, :], gB[0:R, 0, t["gty"], half * 512:(half + 1) * 512], ALU.mult,
                       [r_ps[b], r_gB[0][t["gty"]]], [r_tmpA[ta]])
                    tt(xr, xr, tmpA[0:R, ta, 0:512], ALU.add, [rx, r_tmpA[ta]], [rx])
                xi = norm_pre(xres[0:R, t["xi"], :], r_xres[t["xi"]], R)
                if pend is not None:
                    norm_post(pend[0], pend[1], 3, 2)
                pend = (t, xi)
                yield
            norm_post(pend[0], pend[1], 3, 2)
            yield

        def ffn1_gen(ncol):
            grp = groups_of(ncol)
            for j2 in range(11):
                sg_ = wload1(w_ffn_in[:, j2 * 256:(j2 + 1) * 256])
                su_ = wload1(w_ffn_in[:, DFF + j2 * 256:DFF + (j2 + 1) * 256])
                for jj in range(2):
                    j = j2 * 2 + jj
                    for (c0, n) in grp:
                        bg = newbank()
                        for k in range(KT):
                            mm(psf(bg)[:, 0:n], sg_.lhs(k, jj * 128, 128), hT[:, k, c0:c0 + n], k == 0, k == KT - 1,
                               sg_.regs + [r_hT[k]], [r_ps[bg]])
                        ti = rot("tmpB", NT4)
                        act(tmpB[:, ti, 0:n], psf(bg)[:, 0:n], AF.Silu, [r_ps[bg]], [r_tmpB[ti]])
                        bu = newbank()
                        for k in range(KT):
                            mm(psf(bu)[:, 0:n], su_.lhs(k, jj * 128, 128), hT[:, k, c0:c0 + n], k == 0, k == KT - 1,
                               su_.regs + [r_hT[k]], [r_ps[bu]])
                        tt(actT[:, j, c0:c0 + n], psf(bu)[:, 0:n], tmpB[:, ti, 0:n], ALU.mult, [r_ps[bu], r_tmpB[ti]], [r_actT[j]])
                        yield

        def ffn2_gen(tiles, hook=None):
            kgs = [(0, 8), (8, 8), (16, 6)]
            for half in range(2):
                if half == 1 and hook is not None:
                    hook()
                banks = []
                for t in tiles:
                    b = newbank()
                    reserved.add(b)
                    banks.append(b)
                for gi_, (kb, nk) in enumerate(kgs):
                    slot = wload(w_ffn_out[kb * 128:(kb + nk) * 128, half * 512:(half + 1) * 512], nk=nk)
                    for t, b in zip(tiles, banks):
                        tm_proj(slot, actT, r_actT[kb:kb + nk], t["col0"], t["R"], nk=nk, kbase=kb, start=(gi_ == 0), stop=(gi_ == 2), bank=b)
                        yield
                for t, b in zip(tiles, banks):
                    R = t["R"]
                    ta = rot("tmpA", NT3)
                    xr = xres[0:R, t["xi"], half * 512:(half + 1) * 512]
                    rx = r_xres[t["xi"]][half]
                    tt(tmpA[0:R, ta, 0:512], psf(b)[0:R, :], gB[0:R, 1, t["gty"], half * 512:(half + 1) * 512], ALU.mult,
                       [r_ps[b], r_gB[1][t["gty"]]], [r_tmpA[ta]])
                    tt(xr, tmpA[0:R, ta, 0:512], xr, ALU.add, [r_tmpA[ta], rx], [rx])
                    reserved.discard(b)

        def final_gen(tiles):
            for t in tiles:
                R = t["R"]
                gi = rot("ssq", 4)
                sc = gi * 4
                xi = rot("xnb", NXN)
                rx = r_xres[t["xi"]]
                act(xnb[0:R, xi, :], xres[0:R, t["xi"], :], AF.Square, rx, [r_xnb[xi], r_ssq[gi]], accum_out=ssq[0:R, sc:sc + 1])
                act(rstd[0:R, sc:sc + 1], ssq[0:R, sc:sc + 1], AF.Ln, [r_ssq[gi], r_consts], [r_rstd[gi]], scale=1.0 / D, bias=consts[0:R, 0:1])
                act(rstd[0:R, sc:sc + 1], rstd[0:R, sc:sc + 1], AF.Exp, [r_rstd[gi]], [r_rstd[gi]], scale=-0.5)
                ta = rot("tmpA", NT3)
                stt(tmpA[0:R, ta, :], xres[0:R, t["xi"], :], rstd[0:R, sc:sc + 1], nfg[0:R, :], ALU.mult, ALU.mult,
                    rx + [r_rstd[gi], r_nfg], [r_tmpA[ta]])
                dma("sp", t["out"], tmpA[0:R, ta, :], [r_tmpA[ta]], [])
                yield

        P.tag = "prefix"
        npre = cfg.ntp // TPB

        def pre_tiles(pb):
            return [dict(dram=xp[(pb * TPB + i) * 128:(pb * TPB + i + 1) * 128, :], R=128, col0=i * 128, seqruns=[(0, 128, 0)], vi=i, ci=i)
                    for i in range(TPB)]

        pre_info = {}

        def pre_front(pb):
            st = pb % 2
            tiles = pre_tiles(pb)
            info = pre_info.pop(pb) if pb in pre_info else norm_stats(tiles)
            yield from norm_apply_gen(tiles, info, 1, 0)
            yield from gate_gen(TP, bcP[st], r_bcP[st], ebP[st], r_ebP[st])
            if pb + 1 < npre:
                nt_ = pre_tiles(pb + 1)
                pre_info[pb + 1] = norm_stats(nt_)
            yield from qk_gen(O_K, TP, bcP[st], r_bcP[st], kdP[st], r_kdP[st], False)
            for half in range(2):
                slot = wload(w_in[:, O_V + half * 512:O_V + (half + 1) * 512])
                yield from v_gen(slot, half, tiles, vtP[st], r_vtP[st])

        def pre_gla(pb):
            st = pb % 2
            yield from gla_gen(pre_tiles(pb), kdP[st], r_kdP[st], vtP[st], r_vtP[st], ebP[st], r_ebP[st], False)

        nblk = cfg.ntm // TPB

        def block_ctx(mb):
            has_s = (mb == 0)
            ctx = dict(mb=mb, has_s=has_s, last=(mb == nblk - 1), ncol=TM if has_s else TP, ncolx=T if has_s else TP)
            tiles = []
            for i in range(TPB):
                rows = slice((mb * TPB + i) * 128, (mb * TPB + i + 1) * 128)
                tiles.append(dict(dram=xm[rows, :], R=128, col0=i * 128, seqruns=[(0, 128, 0)], vi=i, ci=i, xi=i, gty=0, out=ym[rows, :]))
            gl_tiles = list(tiles)
            ntiles = list(tiles)
            if has_s:
                stile = dict(dram=xs[:, :], R=64, col0=TP, seqruns=[(0, 32, 1), (32, 32, 2)], xi=TPB, gty=1, out=ys[:, :])
                tiles.append(stile)
                ntiles.append(stile)
                for s in range(2):
                    gl_tiles.append(dict(R=32, col0=TP + 32 * s, vi=TPB + s, samp=s))
                ntiles.append(dict(dram=xh[:, :], R=2, col0=TM, seqruns=[(0, 2, 0)]))
            ctx.update(tiles=tiles, gl_tiles=gl_tiles, ntiles=ntiles)
            return ctx

        def front1(ctx):
            P.tag = "norm1"
            yield from norm_gen(ctx["ntiles"], 1, 0, batch=2)
            P.tag = "gate"
            yield from gate_gen(ctx["ncol"], bc, r_bc, eb, r_eb)
            P.tag = "qk"
            yield from qk_gen(O_Q, ctx["ncol"], bc, r_bc, qdT, r_qdT, True)

        def front2(ctx):
            P.tag = "qk"
            yield from qk_gen(O_K, ctx["ncol"], bc, r_bc, kdT, r_kdT, False)
            P.tag = "vg"
            for half in range(2):
                slot = wload(w_in[:, O_V + half * 512:O_V + (half + 1) * 512])
                yield from v_gen(slot, half, ctx["gl_tiles"], vtm, r_vtm)
            for half in range(2):
                slot = wload(w_in[:, O_G + half * 512:O_G + (half + 1) * 512])
                yield from g_gen(slot, half, ctx["gl_tiles"])

        interleave(tag_gen(pre_front(0), "prefix"), mod_rest_gen())
        for pb in range(npre):
            nxt = tag_gen(pre_front(pb + 1), "prefix") if pb + 1 < npre else tag_gen(front1(block_ctx(0)), "front1")
            interleave(tag_gen(pre_gla(pb), "prefix"), nxt)
        for h in range(NH):
            ts(S32[:, h, :], S32[:, h, :], flag[:, 0:1], None, ALU.mult, None, [r_S32[h], r_flag], [r_S32[h]])
            cp(Sbf[:, h, :], S32[:, h, :], [r_S32[h]], [r_Sbf[h]])

        def preload_x(ctx):
            for t in ctx["tiles"]:
                dma("sp", xres[0:t["R"], t["xi"], :], t["dram"], [], r_xres[t["xi"]])

        for mb in range(nblk):
            ctx = block_ctx(mb)
            cur_blk[0] = mb
            has_s, last, ncol, tiles = ctx["has_s"], ctx["last"], ctx["ncol"], ctx["tiles"]
            run(front2(ctx))
            interleave(tag_gen(gla_gen(ctx["gl_tiles"], kdT, r_kdT, vtm, r_vtm, eb, r_eb, True), "gla"),
                       tag_gen(conv_gen(has_s, ncol, ctx["ncolx"]), "conv"))
            P.tag = "gla"
            preload_x(ctx)
            if last:
                for h in range(NH):
                    dma("sp", sp_o[h, :, :], S32[:, h, :], [r_S32[h]], [])
            if has_s:
                cache_out(ulastS, 4, r_ulastS, cs_o[:, :])
            if last:
                cache_out(uprev, 2, r_uprev, cp_o[:, :])
            P.tag = "ya"

            def fin_a(c, c0, n, by, ti):
                tt(Pa[:, c, c0:c0 + n], psf(by)[:, 0:n], tmpB[:, ti, 0:n], ALU.mult, [r_ps[by], r_tmpB[ti]], [r_Pa[c]])
            run(gated_fm_gen(O_GA, w_gla_out, ogT, [r_ogT], ncol, fin_a))
            P.tag = "yb"

            def fin_b(c, c0, n, by, ti):
                tt(tmpB[:, ti, 0:n], psf(by)[:, 0:n], tmpB[:, ti, 0:n], ALU.mult, [r_ps[by], r_tmpB[ti]], [r_tmpB[ti]])
                tt(mT[:, c, c0:c0 + n], tmpB[:, ti, 0:n], Pa[:, c, c0:c0 + n], ALU.add, [r_tmpB[ti], r_Pa[c]], [r_mT[c]])
            run(gated_fm_gen(O_GB, w_conv_out, ucbT, r_ucbT, ncol, fin_b))
            P.tag = "wo"
            run(wo_norm2_gen(tiles))
            P.tag = "ffn1"
            run(ffn1_gen(ncol))
            nxt = tag_gen(front1(block_ctx(mb + 1)), "front1") if mb + 1 < nblk else None
            interleave(tag_gen(ffn2_gen(tiles, prefetch_k if mb + 1 < nblk else None), "ffn2"), nxt)
            P.tag = "final"
            run(final_gen(tiles))


        P.emit(nc, es)
    nc._prog = P
    return nc


def _pack_vec(v, nk):
    return np.ascontiguousarray(v.reshape(nk, 128).T)


def prepare_inputs(cfg, ncores, x_prompt, x_sample, c_prompt, c_sample, state_gla, cache_conv, w_mod, b_mod, norm1_g,
                   w_in, w_alpha, b_alpha, gla_norm_g, w_gla_out, conv_w, conv_b, w_conv_out, w_o,
                   norm2_g, w_ffn_in, w_ffn_out, norm_f_g):
    f = np.float32
    TPB = cfg.tpb
    TP = TPB * 128
    T = TP + 66
    half_tok = cfg.ntm * 128
    vecs = np.zeros((128, NV), f)
    vecs[:, V_N1:V_N1 + 8] = _pack_vec(np.asarray(norm1_g[0], f), 8)
    vecs[:, V_N2:V_N2 + 8] = _pack_vec(np.asarray(norm2_g[0], f), 8)
    vecs[:, V_BMOD:V_BMOD + 48] = _pack_vec(np.asarray(b_mod[0], f), 48)
    vecs[:, V_BAL:V_BAL + 4] = _pack_vec(np.asarray(b_alpha[0], f), 4)
    vecs[:, V_CW:V_CW + 24] = np.concatenate([_pack_vec(np.asarray(conv_w[0, r], f), 8) for r in range(3)], axis=1)
    vecs[:, V_CBIAS:V_CBIAS + 8] = _pack_vec(np.asarray(conv_b[0], f), 8)
    bcp = np.zeros((128, 4 * D), f)
    bcp[:, 0:D] = np.broadcast_to(np.tile(np.asarray(gla_norm_g[0], f), 4), (128, D))
    bcp[:, D:2 * D] = np.broadcast_to(np.asarray(norm_f_g, f), (128, D))
    bcp[:, 2 * D:3 * D] = np.broadcast_to(np.asarray(b_mod[0, 2 * D:3 * D], f), (128, D))
    bcp[:, 3 * D:4 * D] = np.broadcast_to(np.asarray(b_mod[0, 5 * D:6 * D], f), (128, D))
    ident = np.eye(128, dtype=f)
    maskT = np.triu(np.ones((128, 128), f))
    rmask = np.ones((128, T), f)
    for c in list(range(0, TP, 128)) + [TP, TP + 32, TP + 64]:
        rmask[:, c] = 0.0
    shared = dict(w_mod=np.asarray(w_mod[0], f), w_in=np.asarray(w_in[0], f), w_alpha=np.asarray(w_alpha[0], f),
                  w_gla_out=np.asarray(w_gla_out[0], f), w_conv_out=np.asarray(w_conv_out[0], f), w_o=np.asarray(w_o[0], f),
                  w_ffn_in=np.asarray(w_ffn_in[0], f), w_ffn_out=np.asarray(w_ffn_out[0], f),
                  vecs=vecs, bcp=bcp, ident=ident, maskT=maskT, rmask=rmask)
    in_maps = []
    for c in range(ncores):
        sq, hf = c // 2, c % 2
        xseq = np.asarray(x_prompt[sq], f)
        m = dict(shared)
        m["xm"] = np.ascontiguousarray(xseq[hf * half_tok:(hf + 1) * half_tok])
        m["xp"] = np.ascontiguousarray(xseq[0:cfg.ntp * 128])
        m["xs"] = np.ascontiguousarray(np.asarray(x_sample[2 * c:2 * c + 2], f).reshape(64, D))
        m["xh"] = np.ascontiguousarray(xseq[half_tok - 2:half_tok]) if hf == 1 else np.ascontiguousarray(xseq[0:2])
        cs = np.stack([np.asarray(c_prompt[sq], f), np.asarray(c_sample[2 * c], f), np.asarray(c_sample[2 * c + 1], f)], axis=0)
        m["cT"] = np.ascontiguousarray(cs.reshape(3, KT, 128).transpose(2, 1, 0).reshape(128, KT * 3))
        m["flag"] = np.full((128, 1), float(hf), f)
        m["s0"] = np.ascontiguousarray(np.asarray(state_gla[0, 2 * c:2 * c + 2], f))
        m["cc0"] = np.ascontiguousarray(np.asarray(cache_conv[0, 2 * c:2 * c + 2], f).reshape(4, D))
        in_maps.append(m)
    return in_maps


def assemble(cfg, ncores, res):
    f = np.float32
    nseq = ncores // 2
    half_tok = cfg.ntm * 128
    y_prompt = np.zeros((nseq, 2 * half_tok, D), f)
    y_sample = np.zeros((2 * ncores, 32, D), f)
    sgp = np.zeros((1, nseq, NH, 128, 256), f)
    ccp = np.zeros((1, nseq, 2, D), f)
    sgs = np.zeros((1, 2 * ncores, NH, 128, 256), f)
    ccs = np.zeros((1, 2 * ncores, 2, D), f)
    for c in range(ncores):
        r = res[c]
        sq, hf = c // 2, c % 2
        y_prompt[sq, hf * half_tok:(hf + 1) * half_tok] = r["ym"]
        y_sample[2 * c:2 * c + 2] = np.asarray(r["ys"]).reshape(2, 32, D)
        if hf == 1:
            sgp[0, sq] = r["sp"]
            ccp[0, sq] = r["cp"]
        sgs[0, 2 * c:2 * c + 2] = r["ss"]
        ccs[0, 2 * c:2 * c + 2] = np.asarray(r["cs"]).reshape(2, 2, D)
    return (y_prompt, y_sample, sgp, ccp, sgs, ccs)


_NC_CACHE = {}


def kernel(**inputs):
    cfg = Cfg(16, 16, 4)
    ncores = 8
    in_maps = prepare_inputs(cfg, ncores, **inputs)
    if "nc" not in _NC_CACHE:
        _NC_CACHE["nc"] = build(cfg)
    nc = _NC_CACHE["nc"]
    res = run_bass_kernel_spmd(nc, in_maps, core_ids=list(range(ncores)))
    return assemble(cfg, ncores, res.results)
```

```python
import numpy as np
from contextlib import ExitStack
import concourse.bass as bass
import concourse.mybir as mybir
from concourse.bass_utils import run_bass_kernel_spmd

F32 = mybir.dt.float32
BF16 = mybir.dt.bfloat16
AF = mybir.ActivationFunctionType
ALU = mybir.AluOpType

D = 1024
KT = 8
IN_DIM = 8208
DFF = 2816
NH = 4
EPS = 1e-6
O_Q, O_K, O_V, O_G, O_A, O_CB, O_CC, O_CH, O_GA, O_GB = 0, 512, 1024, 2048, 3072, 3088, 4112, 5136, 6160, 7184

V_N1, V_N2, V_BMOD, V_BAL, V_CW, V_CBIAS = 0, 8, 16, 64, 68, 92
NV = 100


class Reg:
    __slots__ = ("name", "lw", "rd", "ov")

    def __init__(self, name):
        self.name = name
        self.lw = None
        self.rd = []
        self.ov = []


def overlap(a, b):
    a.ov.append(b)
    b.ov.append(a)


class Op:
    __slots__ = ("eng", "fn", "deps", "signal", "count", "dma", "dsem", "dval", "idx", "tag")

    def __init__(self, eng, fn, dma):
        self.eng = eng
        self.fn = fn
        self.deps = []
        self.signal = False
        self.count = 0
        self.dma = dma
        self.dsem = None
        self.dval = 0


class Prog:
    ENGS = ("pe", "act", "dve", "pool", "sp")
    NDMA = {"pe": 8, "act": 8, "dve": 8, "pool": 6, "sp": 8}

    def __init__(self):
        self.ops = {e: [] for e in self.ENGS}
        self.ndma = {e: 0 for e in self.ENGS}
        self.dma_ops = {e: [] for e in self.ENGS}
        self.tag = ""

    def op(self, eng, fn, reads=(), writes=(), dma=False, pe_acc=False):
        o = Op(eng, fn, dma)
        o.tag = self.tag
        deps = []
        for r in reads:
            if r.lw is not None:
                deps.append(r.lw)
            for b in r.ov:
                if b.lw is not None:
                    deps.append(b.lw)
        for w in writes:
            if w.lw is not None:
                deps.append(w.lw)
            deps.extend(w.rd)
            for b in w.ov:
                if b.lw is not None:
                    deps.append(b.lw)
                deps.extend(b.rd)
        seen = set()
        for d in deps:
            if id(d) in seen or d is o:
                continue
            seen.add(id(d))
            if eng == "pe" and d.eng == "pe" and not d.dma:
                continue
            d.signal = True
            o.deps.append(d)
        for r in reads:
            r.rd.append(o)
        for w in writes:
            w.lw = o
            w.rd = []
        if dma:
            i = self.ndma[eng]
            self.ndma[eng] += 1
            o.idx = i
            if i >= self.NDMA[eng]:
                prev = self.dma_ops[eng][i - self.NDMA[eng]]
                o.deps.append(prev)
            self.dma_ops[eng].append(o)
            o.signal = True
        self.ops[eng].append(o)
        return o

    def emit(self, nc, es, final_wait_eng="sp"):
        sems = {e: es.enter_context(nc.semaphore("s_" + e)) for e in self.ENGS}
        dsems = {e: [es.enter_context(nc.semaphore("d_%s%d" % (e, i))) for i in range(self.NDMA[e])]
                 for e in self.ENGS if self.ndma[e] > 0}
        for e in self.ENGS:
            c = 0
            for o in self.ops[e]:
                if o.dma:
                    o.dsem = dsems[e][o.idx % self.NDMA[e]]
                    o.dval = 16 * (o.idx // self.NDMA[e] + 1)
                elif o.signal:
                    c += 1
                    o.count = c
        block = es.enter_context(nc.Block())
        handles = {"pe": block.tensor, "act": block.scalar, "dve": block.vector, "pool": block.gpsimd, "sp": block.sync}
        prog = self

        def make(e):
            def body(eng):
                waited = {}
                for o in prog.ops[e]:
                    need = {}
                    for d in o.deps:
                        if d.dma:
                            s, v = d.dsem, d.dval
                        else:
                            s, v = sems[d.eng], d.count
                        k = id(s)
                        if waited.get(k, 0) >= v:
                            continue
                        if k not in need or need[k][1] < v:
                            need[k] = (s, v)
                    for k, (s, v) in need.items():
                        eng.wait_ge(s, v)
                        waited[k] = v
                    ins = o.fn(eng)
                    if o.dma:
                        ins.then_inc(o.dsem, 16)
                    elif o.signal:
                        ins.then_inc(sems[e], 1)
                if e == final_wait_eng:
                    for e2 in prog.ENGS:
                        n = prog.ndma[e2]
                        for i in range(min(n, prog.NDMA[e2])):
                            cnt = (n - 1 - i) // prog.NDMA[e2] + 1
                            eng.wait_ge(dsems[e2][i], 16 * cnt)
            return body

        for e in self.ENGS:
            handles[e](make(e))


class Cfg:
    def __init__(self, ntm=16, ntp=16, tpb=4):
        self.ntm = ntm
        self.ntp = ntp
        self.tpb = tpb
        assert ntm % tpb == 0 and ntp % tpb == 0


def build(cfg):
    nc = bass.Bass("TRN2", target_bir_lowering=False)
    P = Prog()
    TPB = cfg.tpb
    TP = TPB * 128
    TM = TP + 64
    T = TM + 2

    def din(name, shape):
        return nc.dram_tensor(name, list(shape), F32, kind="ExternalInput").ap()

    def dout(name, shape):
        return nc.dram_tensor(name, list(shape), F32, kind="ExternalOutput").ap()

    xm = din("xm", (cfg.ntm * 128, D))
    xp = din("xp", (cfg.ntp * 128, D))
    xs = din("xs", (64, D))
    xh = din("xh", (2, D))
    cT_d = din("cT", (128, KT * 3))
    flag_d = din("flag", (128, 1))
    s0_d = din("s0", (2, NH, 128, 256))
    cc0_d = din("cc0", (4, D))
    w_mod = din("w_mod", (D, 6 * D))
    w_in = din("w_in", (D, IN_DIM))
    w_alpha = din("w_alpha", (16, 512))
    w_gla_out = din("w_gla_out", (D, D))
    w_conv_out = din("w_conv_out", (D, D))
    w_o = din("w_o", (D, D))
    w_ffn_in = din("w_ffn_in", (D, 2 * DFF))
    w_ffn_out = din("w_ffn_out", (DFF, D))
    vecs_d = din("vecs", (128, NV))
    bcp_d = din("bcp", (128, 4 * D))
    ident_d = din("ident", (128, 128))
    maskT_d = din("maskT", (128, 128))
    rmask_d = din("rmask", (128, T))

    ym = dout("ym", (cfg.ntm * 128, D))
    ys = dout("ys", (64, D))
    sp_o = dout("sp", (NH, 128, 256))
    cp_o = dout("cp", (2, D))
    ss_o = dout("ss", (2, NH, 128, 256))
    cs_o = dout("cs", (4, D))

    es = ExitStack()
    with es:
        def sb(name, shape, dt=F32):
            return es.enter_context(nc.sbuf_tensor("sb_" + name, list(shape), dt))

        NW = 8
        wring = sb("wring", (128, NW // 2, KT, 512), BF16)
        wregs = [Reg("w%d" % i) for i in range(NW)]
        wa = sb("wa", (128, KT, 16), BF16); r_wa = Reg("wa")
        walpha = sb("walpha", (16, 512), BF16); r_walpha = Reg("walpha")
        vecs = sb("vecs", (128, NV)); r_vecs = Reg("vecs")
        gn4 = sb("gn4", (128, 512)); r_gn4 = Reg("gn4")
        nfg = sb("nfg", (128, D)); r_nfg = Reg("nfg")
        identf = sb("identf", (128, 128)); r_identf = Reg("identf")
        identb = sb("identb", (128, 128), BF16); r_identb = Reg("identb")
        maskT = sb("maskT", (128, 128)); r_maskT = Reg("maskT")
        rmask = sb("rmask", (128, T)); r_rmask = Reg("rmask")
        flag = sb("flag", (128, 1)); r_flag = Reg("flag")
        cTf = sb("cTf", (128, KT, 3)); r_cTf = Reg("cTf")
        cTb = sb("cTb", (128, KT, 3), BF16); r_cTb = Reg("cTb")
        r_cB = Reg("cB")
        gB = sb("gB", (128, 2, 2, D)); r_gB = [[Reg("gB%d%d" % (a, b)) for b in range(2)] for a in range(2)]
        modT = sb("modT", (128, 4, KT, 3)); r_modT = [Reg("modT%d" % i) for i in range(4)]
        consts = sb("consts", (128, 4)); r_consts = Reg("consts")
        nbal = sb("nbal", (128, 4)); r_nbal = Reg("nbal")
        hT = sb("hT", (128, KT, T), BF16); r_hT = [Reg("hT%d" % k) for k in range(KT)]
        NXT = TPB + 1
        xres = sb("xres", (128, NXT, D)); r_xres = [[Reg("xres%d_%d" % (i, hf)) for hf in range(2)] for i in range(NXT)]
        ssq = sb("ssq", (128, 16)); r_ssq = [Reg("ssq%d" % i) for i in range(4)]
        rstd = sb("rstd", (128, 16)); r_rstd = [Reg("rstd%d" % i) for i in range(4)]
        NXN = 2
        xnb = sb("xnb", (128, NXN, D), BF16); r_xnb = [Reg("xnb%d" % i) for i in range(NXN)]

        NT3 = 4
        tmpA = sb("tmpA", (128, NT3, D)); r_tmpA = [Reg("tmpA%d" % i) for i in range(NT3)]
        NT4 = 4
        tmpB = sb("tmpB", (128, NT4, T)); r_tmpB = [Reg("tmpB%d" % i) for i in range(NT4)]
        aT = sb("aT", (16, T), BF16); r_aT = Reg("aT")
        S32 = sb("S32", (128, NH, 256)); r_S32 = [Reg("S32_%d" % h) for h in range(NH)]
        Sbf = sb("Sbf", (128, NH, 256), BF16); r_Sbf = [Reg("Sbf_%d" % h) for h in range(NH)]
        Sbfs = sb("Sbfs", (128, 2, NH, 256), BF16); r_Sbfs = [Reg("Sbfs_%d" % s) for s in range(2)]
        eb = sb("eb", (128, NH, TPB + 2)); r_eb = [Reg("eb%d" % h) for h in range(NH)]
        eb2 = sb("eb2", (128, NH, TPB + 2)); r_eb2 = [Reg("eb2_%d" % h) for h in range(NH)]
        ucbF = sb("ucbT", (128, max(KT * TM, 2 * KT * 128)), BF16); r_ucbT = [Reg("ucbT%d" % k) for k in range(KT)]
        ucbT = ucbF[:, 0:KT * TM].rearrange("p (k t) -> p k t", k=KT)
        cB = ucbF[:, 0:2 * KT * 128].rearrange("p (a k r) -> p a k r", a=2, k=KT)
        for rj in r_ucbT:
            overlap(r_cB, rj)
        NSM = 2
        sTm = sb("sTm", (128, NSM, NH, 128), BF16); r_sTm = [Reg("sTm%d" % i) for i in range(NSM)]
        kdtm = sb("kdtm", (128, NSM, NH, 128), BF16); r_kdtm = [Reg("kdtm%d" % i) for i in range(NSM)]
        stmp = sb("stmp", (128, 2, 256)); r_stmp = [Reg("stmp%d" % i) for i in range(2)]
        ssqo = sb("ssqo", (128, 2, 4)); r_ssqo = [Reg("ssqo%d" % i) for i in range(2)]
        rstdo = sb("rstdo", (128, 2, 4)); r_rstdo = [Reg("rstdo%d" % i) for i in range(2)]
        ogb = sb("ogb", (128, 2, D), BF16); r_ogb = [Reg("ogb%d" % i) for i in range(2)]
        ogT = sb("ogT", (128, KT, TM), BF16); r_ogT = Reg("ogT")
        ue = sb("ue", (128, 2, TP + 2)); r_ue = [Reg("ue%d" % i) for i in range(2)]
        ues = sb("ues", (128, 2, 2, 34)); r_ues = [Reg("ues%d" % i) for i in range(2)]
        uprev = sb("uprev", (128, KT, 2)); r_uprev = Reg("uprev")
        prevS = sb("prevS", (128, KT, 4)); r_prevS = Reg("prevS")
        ulastS = sb("ulastS", (128, KT, 4)); r_ulastS = Reg("ulastS")
        cacc = sb("cacc", (128, 1, TM)); r_cacc = [Reg("cacc0")]
        lay = {}
        off = 0
        for nm, sz in (("bc", NH * TM * 2), ("qdT", NH * TM), ("kdT", NH * TM), ("vtm", (TPB + 2) * D), ("gsg", (TPB + 2) * D)):
            lay[nm] = (off, sz, "G1")
            off += sz
        NA = off
        lay["Pa"] = (0, KT * TM, "G2")
        lay["mT"] = (KT * TM, KT * TM, "G2")
        lay["actT"] = (NA - 22 * TM, 22 * TM, "G3")
        lay["kdP1"] = (lay["qdT"][0], NH * TP, "P1")
        o1 = lay["vtm"][0] + TPB * D
        lay["bcP1"] = (o1, NH * TP * 2, "P1")
        lay["vtP1"] = (o1 + NH * TP * 2, TPB * D, "P1")
        assert o1 + NH * TP * 2 + TPB * D <= NA and NH * TP <= lay["qdT"][1] and lay["actT"][0] >= lay["kdT"][0]
        arena = sb("arena", (128, NA), BF16)

        def aview(nm):
            lo, sz, _ = lay[nm]
            return arena[:, lo:lo + sz]

        bc = aview("bc").bitcast(F32).rearrange("p (h t) -> p h t", h=NH)
        qdT = aview("qdT").rearrange("p (h t) -> p h t", h=NH)
        kdT = aview("kdT").rearrange("p (h t) -> p h t", h=NH)
        vtm = aview("vtm").rearrange("p (n d) -> p n d", d=D)
        gsg = aview("gsg").rearrange("p (n d) -> p n d", d=D)
        Pa = aview("Pa").rearrange("p (k t) -> p k t", k=KT)
        mT = aview("mT").rearrange("p (k t) -> p k t", k=KT)
        actT = aview("actT").rearrange("p (k t) -> p k t", k=22)
        kdP1 = aview("kdP1").rearrange("p (h t) -> p h t", h=NH)
        bcP1 = aview("bcP1").bitcast(F32).rearrange("p (h t) -> p h t", h=NH)
        vtP1 = aview("vtP1").rearrange("p (n d) -> p n d", d=D)
        areg = []

        def mk(nm, nparts):
            lo, sz, grp_ = lay[nm]
            step = sz // nparts
            lst = []
            for i in range(nparts):
                r = Reg("%s_%d" % (nm, i))
                areg.append((r, lo + i * step, lo + (i + 1) * step, grp_))
                lst.append(r)
            return lst

        r_bc = mk("bc", NH)
        r_qdT = mk("qdT", NH)
        r_kdT = mk("kdT", NH)
        r_vtm = mk("vtm", TPB + 2)
        r_gsg = mk("gsg", TPB + 2)
        r_Pa = mk("Pa", KT)
        r_mT = mk("mT", KT)
        r_actT = mk("actT", 22)
        r_kdP1 = mk("kdP1", NH)
        r_bcP1 = mk("bcP1", NH)
        r_vtP1 = mk("vtP1", TPB)
        for ia in range(len(areg)):
            for ib in range(ia + 1, len(areg)):
                (ra, lo1, hi1, g1), (rb, lo2, hi2, g2) = areg[ia], areg[ib]
                if g1 != g2 and lo1 < hi2 and lo2 < hi1:
                    overlap(ra, rb)
        bcP, r_bcP = [bc, bcP1], [r_bc, r_bcP1]
        kdP, r_kdP = [kdT, kdP1], [r_kdT, r_kdP1]
        vtP, r_vtP = [vtm, vtP1], [r_vtm, r_vtP1]
        ebP, r_ebP = [eb, eb2], [r_eb, r_eb2]

        psum = es.enter_context(nc.psum_tensor("psum", [128, 8, 512], F32))
        r_ps = [Reg("ps%d" % i) for i in range(8)]
        ps_ctr = [0]

        reserved = set()

        def newbank():
            while True:
                i = ps_ctr[0] % 8
                ps_ctr[0] += 1
                if i not in reserved:
                    return i

        def psf(b):
            return psum[:, b, :]

        def psb(b):
            return psum[:, b, :].bitcast(BF16)

        obanks = [[0, 1], [2, 3]]
        ctr = {"ob": 0, "xnb": 0, "ssq": 0, "tmpA": 0, "tmpB": 0, "w": 0, "sm": 0, "st": 0, "so": 0, "og": 0, "ue": 0, "ca": 0}

        held = {"tmpA": set()}

        def rot(name, n):
            while True:
                i = ctr[name] % n
                ctr[name] += 1
                if i not in held.get(name, ()):
                    return i

        def dma(eng, out, in_, reads, writes):
            return P.op(eng, lambda e: e.dma_start(out=out, in_=in_), reads=reads, writes=writes, dma=True)

        def act(out, in_, func, reads, writes, bias=None, scale=None, accum_out=None):
            kw = {}
            if bias is not None:
                kw["bias"] = bias
            if scale is not None:
                kw["scale"] = scale
            if accum_out is not None:
                kw["accum_out"] = accum_out
            return P.op("act", lambda e: e.activation(out=out, in_=in_, func=func, **kw), reads=reads, writes=writes)

        def tt(out, in0, in1, op, reads, writes, eng="dve"):
            return P.op(eng, lambda e: e.tensor_tensor(out=out, in0=in0, in1=in1, op=op), reads=reads, writes=writes)

        def ts(out, in0, s1, s2, op0, op1, reads, writes, eng="dve"):
            if s2 is None:
                return P.op(eng, lambda e: e.tensor_scalar(out=out, in0=in0, scalar1=s1, scalar2=None, op0=op0), reads=reads, writes=writes)
            return P.op(eng, lambda e: e.tensor_scalar(out=out, in0=in0, scalar1=s1, scalar2=s2, op0=op0, op1=op1), reads=reads, writes=writes)

        def stt(out, in0, scalar, in1, op0, op1, reads, writes, eng="dve"):
            return P.op(eng, lambda e: e.scalar_tensor_tensor(out=out, in0=in0, scalar=scalar, in1=in1, op0=op0, op1=op1), reads=reads, writes=writes)

        def cp(out, in_, reads, writes, eng="dve"):
            return P.op(eng, lambda e: e.tensor_copy(out=out, in_=in_), reads=reads, writes=writes)

        def mm(out, lhsT, rhs, start, stop, reads, writes):
            return P.op("pe", lambda e: e.matmul(out, lhsT=lhsT, rhs=rhs, start=start, stop=stop), reads=reads, writes=writes)

        def tr(out, in_, ident, reads, writes):
            return P.op("pe", lambda e: e.transpose(out=out, in_=in_, identity=ident), reads=reads, writes=writes)

        class W:
            def __init__(self, slots):
                self.slots = slots
                self.regs = [wregs[i] for i in slots]

            def lhs(self, k, col, m):
                sl = self.slots[col // 256]
                c = (sl % 2) * 256 + col % 256
                return wring[:, sl // 2, k, c:c + m]

            def rhs(self, k):
                return wring[:, self.slots[0] // 2, k, :]

        wreserved = set()
        wcache = {}
        cur_blk = [0]
        CACHE_W = True

        def wfetch(src_ap, nk, dst, dregs, ncols):
            key = (src_ap.name, src_ap.offset, tuple(src_ap.shape))
            cacheable = CACHE_W and src_ap.name != "w_mod" and (cur_blk[0] >= 1 or not src_ap.name.startswith("w_ffn"))
            if cacheable and key in wcache:
                sc_ap, sreg = wcache[key]
                dma("pool", dst, sc_ap.rearrange("p (k c) -> p k c", k=nk), reads=[sreg], writes=dregs)
                wflush(2)
                return
            dma("pool", dst, src_ap.rearrange("(k p) c -> p k c", p=128), reads=[], writes=dregs)
            if cacheable and key not in wpend_keys:
                wpend.append((key, nk, ncols, dst, dregs, ctr["w"]))
                wpend_keys.add(key)
            wflush(2)

        wpend = []
        wpend_keys = set()
        wnum = [0]
        WB_Q = "pool"

        def wflush(keep):
            while wpend and (len(wpend) > keep or ctr["w"] - wpend[0][5] >= 3):
                key, nk, ncols, dst, dregs, _ = wpend.pop(0)
                sc_ap = nc.dram_tensor("wsc%d" % wnum[0], [128, nk * ncols], BF16, kind="Internal").ap()
                sreg = Reg("wsc%d" % wnum[0])
                wnum[0] += 1
                dma(WB_Q, sc_ap.rearrange("p (k c) -> p k c", k=nk), dst, reads=dregs, writes=[sreg])
                wcache[key] = (sc_ap, sreg)
                wpend_keys.discard(key)

        def wload(src_ap, nk=KT):
            while True:
                if ctr["w"] % 2:
                    ctr["w"] += 1
                s0 = ctr["w"] % NW
                ctr["w"] += 2
                if s0 not in wreserved and s0 + 1 not in wreserved:
                    break
            wfetch(src_ap, nk, wring[:, s0 // 2, 0:nk, :], [wregs[s0], wregs[s0 + 1]], 512)
            return W([s0, s0 + 1])

        def wload1(src_ap, nk=KT):
            while True:
                s0 = ctr["w"] % NW
                ctr["w"] += 1
                if s0 not in wreserved:
                    break
            wfetch(src_ap, nk, wring[:, s0 // 2, 0:nk, (s0 % 2) * 256:(s0 % 2 + 1) * 256], [wregs[s0]], 256)
            return W([s0])

        def groups_of(ncols):
            g = []
            c = 0
            while c < min(ncols, TP):
                n = min(512, TP - c)
                g.append((c, n))
                c += n
            if ncols > TP:
                g.append((TP, ncols - TP))
            return g

        P.tag = "setup"
        dma("sp", vecs[:], vecs_d[:, :], [], [r_vecs])
        dma("sp", identf[:], ident_d[:, :], [], [r_identf])
        dma("sp", maskT[:], maskT_d[:, :], [], [r_maskT])
        dma("sp", rmask[:], rmask_d[:, :], [], [r_rmask])
        dma("sp", flag[:], flag_d[:, :], [], [r_flag])
        dma("sp", cTf[:], cT_d.rearrange("p (k s) -> p k s", s=3), [], [r_cTf])
        dma("sp", gn4[:], bcp_d[:, 0:512], [], [r_gn4])
        dma("sp", nfg[:], bcp_d[:, D:2 * D], [], [r_nfg])
        dma("pool", wa[:], w_in[:, O_A:O_A + 16].rearrange("(k p) c -> p k c", p=128), [], [r_wa])
        dma("pool", walpha[:], w_alpha[:, :], [], [r_walpha])
        P.op("dve", lambda e: e.memset(consts[:, 0:1], EPS), writes=[r_consts])
        P.op("dve", lambda e: e.memset(consts[:, 1:2], 1.0), writes=[r_consts])
        P.op("dve", lambda e: e.memset(consts[:, 2:3], float(np.log(128.0 ** -0.5))), writes=[r_consts])
        P.op("dve", lambda e: e.memset(consts[:, 3:4], 0.0), writes=[r_consts])
        ts(nbal[:], vecs[:, V_BAL:V_BAL + 4], -1.0, None, ALU.mult, None, [r_vecs], [r_nbal])
        P.op("dve", lambda e: e.memset(ssq[:], 1.0), writes=r_ssq)
        P.op("dve", lambda e: e.memset(ssqo[:], 1.0), writes=r_ssqo)
        cp(identb[:], identf[:], [r_identf], [r_identb])
        cp(cTb[:], cTf[:], [r_cTf], [r_cTb])
        for k in range(KT):
            cp(cB[:, 0, k, :], cTf[:, k, 0:1].broadcast_to([128, 128]), [r_cTf], [r_cB])
            cp(cB[:, 1, k, 0:32], cTf[:, k, 1:2].broadcast_to([128, 32]), [r_cTf], [r_cB])
            cp(cB[:, 1, k, 32:64], cTf[:, k, 2:3].broadcast_to([128, 32]), [r_cTf], [r_cB])
        for h in range(NH):
            P.op("dve", lambda e, h=h: e.memset(S32[:, h, :], 0.0), writes=[r_S32[h]])
        P.op("dve", lambda e: e.memset(uprev[:], 0.0), writes=[r_uprev])
        b = newbank()
        tc0 = rot("tmpA", NT3)
        dma("sp", tmpA[0:4, tc0, :], cc0_d[:, :], [], [r_tmpA[tc0]])
        for c in range(KT):
            tr(psf(b)[:, c * 4:(c + 1) * 4], tmpA[0:4, tc0, c * 128:(c + 1) * 128], identf[0:4, 0:4], [r_tmpA[tc0], r_identf], [r_ps[b]])
        cp(prevS[:], psf(b)[:, 0:KT * 4].rearrange("p (c f) -> p c f", f=4), [r_ps[b]], [r_prevS])

        P.tag = "mod"
        def mod_fm_gen(parts):
            for (six, mi) in parts:
                for half in range(2):
                    c0 = six * D + half * 512
                    slot = wload(w_mod[:, c0:c0 + 512])
                    b = newbank()
                    for sub in range(4):
                        for k in range(KT):
                            mm(psf(b)[:, sub * 3:(sub + 1) * 3], slot.lhs(k, sub * 128, 128), cTb[:, k, :],
                               k == 0, k == KT - 1, slot.regs + [r_cTb], [r_ps[b]])
                    for sub in range(4):
                        kk = half * 4 + sub
                        col = V_BMOD + six * 8 + kk
                        ts(modT[:, mi, kk, :], psf(b)[:, sub * 3:(sub + 1) * 3], vecs[:, col:col + 1], None, ALU.add, None,
                           [r_ps[b], r_vecs], [r_modT[mi]])
                    yield
                if mi in (1, 3):
                    voff = V_N1 if mi == 1 else V_N2
                    for s_ in range(3):
                        stt(modT[:, mi, :, s_], modT[:, mi, :, s_], 1.0, vecs[:, voff:voff + 8], ALU.add, ALU.mult,
                            [r_modT[mi], r_vecs], [r_modT[mi]])

        def mod_tm_gen():
            for gi, six in enumerate((2, 5)):
                tb = rot("tmpA", NT3)
                dma("sp", tmpA[:, tb, :], bcp_d[:, (2 + gi) * D:(3 + gi) * D], [], [r_tmpA[tb]])
                for half in range(2):
                    c0 = six * D + half * 512
                    slot = wload(w_mod[:, c0:c0 + 512])
                    for ty in range(2):
                        b = newbank()
                        R = 128 if ty == 0 else 64
                        for k in range(KT):
                            mm(psf(b)[0:R, :], cB[:, ty, k, 0:R], slot.rhs(k), k == 0, k == KT - 1,
                               [r_cB] + slot.regs, [r_ps[b]])
                        tt(gB[0:R, gi, ty, half * 512:(half + 1) * 512], psf(b)[0:R, :], tmpA[0:R, tb, half * 512:(half + 1) * 512], ALU.add,
                           [r_ps[b], r_tmpA[tb]], [r_gB[gi][ty]])
                    yield

        def mod_rest_gen():
            P.tag = "mod"
            yield from mod_fm_gen([(3, 2), (4, 3)])
            yield from mod_tm_gen()

        PRE0 = {}
        for _ in mod_fm_gen([(0, 0), (1, 1)]):
            pass

        def run(g):
            for _ in g:
                pass

        def tag_gen(g, tag):
            while True:
                P.tag = tag
                try:
                    next(g)
                except StopIteration:
                    return
                yield

        def interleave(*gens):
            gens = [g for g in gens if g is not None]
            while gens:
                for g in list(gens):
                    try:
                        next(g)
                    except StopIteration:
                        gens.remove(g)

        def norm_stats(grp_):
            gi = rot("ssq", 4)
            info = []
            Rm = max(t["R"] for t in grp_)
            for j, t in enumerate(grp_):
                R = t["R"]
                if "dram" in t:
                    ta = rot("tmpA", NT3)
                    dma("sp", tmpA[0:R, ta, :], t["dram"], [], [r_tmpA[ta]])
                    src, regs = tmpA[0:R, ta, :], [r_tmpA[ta]]
                else:
                    src, regs = t["src"], t["regs"]
                xi = rot("xnb", NXN)
                sc = gi * 4 + j
                act(xnb[0:R, xi, :], src, AF.Square, regs, [r_xnb[xi], r_ssq[gi]], accum_out=ssq[0:R, sc:sc + 1])
                info.append((src, regs, gi, sc))
            c0_, c1_ = gi * 4, gi * 4 + len(grp_)
            act(rstd[0:Rm, c0_:c1_], ssq[0:Rm, c0_:c1_], AF.Ln, [r_ssq[gi], r_consts], [r_rstd[gi]], scale=1.0 / D, bias=consts[0:Rm, 0:1])
            act(rstd[0:Rm, c0_:c1_], rstd[0:Rm, c0_:c1_], AF.Exp, [r_rstd[gi]], [r_rstd[gi]], scale=-0.5)
            return info

        def norm_apply_gen(grp_, info, G_idx, SH_idx, use_dve=False):
            for t, (src, regs, gi, sc) in zip(grp_, info):
                R = t["R"]
                xi = rot("xnb", NXN)
                ts(xnb[0:R, xi, :], src, rstd[0:R, sc:sc + 1], None, ALU.mult, None, regs + [r_rstd[gi]], [r_xnb[xi]])
                tslot = r_tmpA.index(regs[0]) if (len(regs) == 1 and regs[0] in r_tmpA) else None
                norm_post(t, xi, G_idx, SH_idx, tslot, use_dve)
                yield

        def norm_gen(tiles, G_idx, SH_idx, batch=1):
            for g0 in range(0, len(tiles), batch):
                grp_ = tiles[g0:g0 + batch]
                info = norm_stats(grp_)
                yield from norm_apply_gen(grp_, info, G_idx, SH_idx)

        def gate_gen(ncols, BC, r_BC, EB, r_EB):
            gr = groups_of(ncols)
            for (c0, n) in gr:
                b = newbank()
                for k in range(KT):
                    mm(psf(b)[0:16, 0:n], wa[:, k, :], hT[:, k, c0:c0 + n], k == 0, k == KT - 1, [r_wa, r_hT[k]], [r_ps[b]])
                cp(aT[:, c0:c0 + n], psf(b)[0:16, 0:n], [r_ps[b]], [r_aT])
            yield
            for h in range(NH):
                ti = rot("tmpB", NT4)
                for (c0, n) in gr:
                    b = newbank()
                    mm(psf(b)[:, 0:n], walpha[:, h * 128:(h + 1) * 128], aT[:, c0:c0 + n], True, True, [r_walpha, r_aT], [r_ps[b]])
                    act(tmpB[:, ti, c0:c0 + n], psf(b)[:, 0:n], AF.Exp, [r_ps[b], r_nbal], [r_tmpB[ti]], scale=-1.0, bias=nbal[:, h:h + 1])
                act(tmpB[:, ti, 0:ncols], tmpB[:, ti, 0:ncols], AF.Ln, [r_tmpB[ti], r_consts], [r_tmpB[ti]], bias=consts[:, 1:2])
                P.op("dve", lambda e, h=h, ti=ti: e.tensor_tensor_scan(out=BC[:, h, 0:ncols], data0=rmask[:, 0:ncols], data1=tmpB[:, ti, 0:ncols],
                                                                        initial=0.0, op0=ALU.mult, op1=ALU.add),
                     reads=[r_rmask, r_tmpB[ti]], writes=[r_BC[h]])
                np_ = min(ncols, TP) // 128
                act(EB[:, h, 0:np_], BC[:, h, 127:np_ * 128:128], AF.Exp, [r_BC[h]], [r_EB[h]], scale=-1.0 / 16)
                if ncols > TP:
                    act(EB[:, h, TPB:TPB + 2], BC[:, h, TP + 31:TP + 64:32], AF.Exp, [r_BC[h]], [r_EB[h]], scale=-1.0 / 16)
                yield

        kpre = {}

        def prefetch_k():
            kpre[0] = wload1(w_in[:, O_K:O_K + 256])
            wreserved.add(kpre[0].slots[0])

        def qk_gen(w_off, ncols, BC, r_BC, OUT, r_OUT, is_q):
            used_pre = None
            for h in range(NH):
                if h % 2 == 0:
                    if w_off == O_K and h == 0 and 0 in kpre:
                        slot = kpre.pop(0)
                        used_pre = slot
                    else:
                        if used_pre is not None:
                            wreserved.discard(used_pre.slots[0])
                            used_pre = None
                        slot = wload1(w_in[:, w_off + h * 128:w_off + h * 128 + 256])
                for (c0, n) in groups_of(ncols):
                    b = newbank()
                    for k in range(KT):
                        mm(psf(b)[:, 0:n], slot.lhs(k, (h % 2) * 128, 128), hT[:, k, c0:c0 + n], k == 0, k == KT - 1,
                           slot.regs + [r_hT[k]], [r_ps[b]])
                    ti = rot("tmpB", NT4)
                    if is_q:
                        act(tmpB[:, ti, 0:n], BC[:, h, c0:c0 + n], AF.Exp, [r_BC[h], r_consts], [r_tmpB[ti]], scale=-1.0 / 16, bias=consts[:, 2:3])
                    else:
                        act(tmpB[:, ti, 0:n], BC[:, h, c0:c0 + n], AF.Exp, [r_BC[h]], [r_tmpB[ti]], scale=1.0 / 16)
                    tt(OUT[:, h, c0:c0 + n], psf(b)[:, 0:n], tmpB[:, ti, 0:n], ALU.mult, [r_ps[b], r_tmpB[ti]], [r_OUT[h]])
                    yield

        def tm_proj(slot, lhs, lhs_regs, col0, R, nk=KT, kbase=0, start=True, stop=True, bank=None):
            b = newbank() if bank is None else bank
            for k in range(nk):
                mm(psf(b)[0:R, :], lhs[:, kbase + k, col0:col0 + R], slot.rhs(k), start and k == 0, stop and k == nk - 1,
                   slot.regs + lhs_regs, [r_ps[b]])
            return b

        def v_gen(slot, half, gl_tiles, VT, r_VT):
            for ti_, t in enumerate(gl_tiles):
                R = t["R"]
                b = tm_proj(slot, hT, r_hT, t["col0"], R)
                if ti_ % 2 == 0:
                    act(VT[0:R, t["vi"], half * 512:(half + 1) * 512], psf(b)[0:R, :], AF.Copy, [r_ps[b]], [r_VT[t["vi"]]])
                else:
                    cp(VT[0:R, t["vi"], half * 512:(half + 1) * 512], psf(b)[0:R, :], [r_ps[b]], [r_VT[t["vi"]]])
                yield

        def g_gen(slot, half, gl_tiles):
            for t in gl_tiles:
                R = t["R"]
                b = tm_proj(slot, hT, r_hT, t["col0"], R)
                ta = rot("tmpA", NT3)
                act(tmpA[0:R, ta, 0:512], psf(b)[0:R, :], AF.Silu, [r_ps[b]], [r_tmpA[ta]])
                tt(gsg[0:R, t["vi"], half * 512:(half + 1) * 512], tmpA[0:R, ta, 0:512], gn4[0:R, :], ALU.mult,
                   [r_tmpA[ta], r_gn4], [r_gsg[t["vi"]]])
                yield

        def gla_A(t, KD, r_KD, full):
            R, c0 = t["R"], t["col0"]
            si = rot("sm", NSM)
            t["si"] = si
            if "samp" in t:
                s = t["samp"]
                ta = rot("tmpA", NT3)
                held["tmpA"].add(ta)
                t["s_ta"] = ta
                dma("sp", tmpA[:, ta, :].rearrange("p (h v) -> p h v", h=NH), s0_d[s].rearrange("h p v -> p h v"), [], [r_tmpA[ta]])
                cp(Sbfs[:, s, :, :], tmpA[:, ta, :].rearrange("p (h v) -> p h v", h=NH), [r_tmpA[ta]], [r_Sbfs[s]])
            bB = newbank()
            if full:
                bA = newbank()
            for h in range(NH):
                if full:
                    mm(psf(bA)[0:R, h * 128:h * 128 + R], KD[:, h, c0:c0 + R], qdT[:, h, c0:c0 + R], True, True,
                       [r_KD[h], r_qdT[h]], [r_ps[bA]])
                tr(psb(bB)[0:R, h * 128:(h + 1) * 128], KD[:, h, c0:c0 + R], identb[:, :], [r_KD[h], r_identb], [r_ps[bB]])
            if full:
                for h in range(NH):
                    tt(sTm[0:R, si, h, 0:R], psf(bA)[0:R, h * 128:h * 128 + R], maskT[0:R, 0:R], ALU.mult, [r_ps[bA], r_maskT], [r_sTm[si]])
            act(kdtm[0:R, si, :, :], psb(bB)[0:R, 0:512].rearrange("p (h d) -> p h d", h=NH), AF.Copy, [r_ps[bB]], [r_kdtm[si]])

        def gla_B(t, chunk_idx, VT, r_VT, EB, r_EB, full):
            R, c0, vi, si = t["R"], t["col0"], t["vi"], t["si"]
            samp = t.get("samp")
            bO = [newbank(), newbank()] if full else None
            bS = [newbank(), newbank()]
            for h in range(NH):
                if full:
                    ob = psf(bO[h // 2])[0:R, (h % 2) * 256:(h % 2 + 1) * 256]
                    mm(ob, sTm[0:R, si, h, 0:R], VT[0:R, vi, h * 256:(h + 1) * 256], True, False, [r_sTm[si], r_VT[vi]], [r_ps[bO[h // 2]]])
                    if samp is None:
                        mm(ob, qdT[:, h, c0:c0 + R], Sbf[:, h, :], False, True, [r_qdT[h], r_Sbf[h]], [r_ps[bO[h // 2]]])
                    else:
                        mm(ob, qdT[:, h, c0:c0 + R], Sbfs[:, samp, h, :], False, True, [r_qdT[h], r_Sbfs[samp]], [r_ps[bO[h // 2]]])
                mm(psf(bS[h // 2])[:, (h % 2) * 256:(h % 2 + 1) * 256], kdtm[0:R, si, h, :], VT[0:R, vi, h * 256:(h + 1) * 256], True, True,
                   [r_kdtm[si], r_VT[vi]], [r_ps[bS[h // 2]]])
            for h in range(NH):
                st_ = rot("st", 2)
                sp_ = psf(bS[h // 2])[:, (h % 2) * 256:(h % 2 + 1) * 256]
                e_ = EB[:, h, chunk_idx:chunk_idx + 1]
                if samp is None:
                    tt(stmp[:, st_, :], sp_, S32[:, h, :], ALU.add, [r_ps[bS[h // 2]], r_S32[h]], [r_stmp[st_]])
                    act(S32[:, h, :], stmp[:, st_, :], AF.Copy, [r_stmp[st_], r_EB[h]], [r_S32[h]], scale=e_)
                    if full:
                        ts(Sbf[:, h, :], stmp[:, st_, :], e_, None, ALU.mult, None, [r_stmp[st_], r_EB[h]], [r_Sbf[h]])
                else:
                    ta = t["s_ta"]
                    s32 = tmpA[:, ta, h * 256:(h + 1) * 256]
                    tt(stmp[:, st_, :], sp_, s32, ALU.add, [r_ps[bS[h // 2]], r_tmpA[ta]], [r_stmp[st_]])
                    act(s32, stmp[:, st_, :], AF.Copy, [r_stmp[st_], r_EB[h]], [r_tmpA[ta]], scale=e_)
            if samp is not None:
                ta = t["s_ta"]
                dma("sp", ss_o[samp].rearrange("h p v -> p h v"), tmpA[:, ta, :].rearrange("p (h v) -> p h v", h=NH), [r_tmpA[ta]], [])
                held["tmpA"].discard(ta)
            if not full:
                return
            so = rot("so", 2)
            og = rot("og", 2)
            t["og"] = og
            to = rot("tmpA", NT3)
            for j in range(2):
                act(tmpA[0:R, to, j * 512:(j + 1) * 512], psf(bO[j])[0:R, :], AF.Copy, [r_ps[bO[j]]], [r_tmpA[to]])
            for h in range(NH):
                act(ogb[0:R, og, h * 256:(h + 1) * 256], tmpA[0:R, to, h * 256:(h + 1) * 256], AF.Square, [r_tmpA[to]],
                    [r_ogb[og], r_ssqo[so]], accum_out=ssqo[0:R, so, h:h + 1])
            act(rstdo[0:R, so, :], ssqo[0:R, so, :], AF.Ln, [r_ssqo[so], r_consts], [r_rstdo[so]], scale=1.0 / 256, bias=consts[0:R, 0:1])
            act(rstdo[0:R, so, :], rstdo[0:R, so, :], AF.Exp, [r_rstdo[so]], [r_rstdo[so]], scale=-0.5)
            for h in range(NH):
                stt(ogb[0:R, og, h * 256:(h + 1) * 256], tmpA[0:R, to, h * 256:(h + 1) * 256], rstdo[0:R, so, h:h + 1],
                    gsg[0:R, vi, h * 256:(h + 1) * 256], ALU.mult, ALU.mult, [r_tmpA[to], r_rstdo[so], r_gsg[vi]], [r_ogb[og]])

        def gla_C(t):
            R, c0, og = t["R"], t["col0"], t["og"]
            bT = newbank()
            for k in range(KT):
                tr(psb(bT)[:, k * 128:k * 128 + R], ogb[0:R, og, k * 128:(k + 1) * 128], identb[0:R, 0:R], [r_ogb[og], r_identb], [r_ps[bT]])
            cp(ogT[:, :, c0:c0 + R], psb(bT).rearrange("p (k r) -> p k r", k=KT)[:, :, 0:R], [r_ps[bT]], [r_ogT])

        def gla_gen(gl_tiles, KD, r_KD, VT, r_VT, EB, r_EB, full):
            n = len(gl_tiles)
            for step in range(n + 2):
                if step < n:
                    gla_A(gl_tiles[step], KD, r_KD, full)
                    yield
                if 0 <= step - 1 < n:
                    t = gl_tiles[step - 1]
                    ci = (TPB + t["samp"]) if "samp" in t else t["ci"]
                    gla_B(t, ci, VT, r_VT, EB, r_EB, full)
                    yield
                if full and 0 <= step - 2 < n:
                    if step - 2 >= n - 2:
                        yield
                        yield
                    gla_C(gl_tiles[step - 2])
                    yield

        def conv_gen(has_s, ncol, ncolx):
            grp = groups_of(ncol)
            grpx = groups_of(ncolx)
            for c2 in range(4):
                s_cc = wload1(w_in[:, O_CC + c2 * 256:O_CC + (c2 + 1) * 256])
                s_ch = wload1(w_in[:, O_CH + c2 * 256:O_CH + (c2 + 1) * 256])
                s_cb = wload1(w_in[:, O_CB + c2 * 256:O_CB + (c2 + 1) * 256])
                for cc_ in range(2):
                    c = c2 * 2 + cc_
                    ui = rot("ue", 2)
                    w0 = vecs[:, V_CW + 0 * 8 + c:V_CW + 0 * 8 + c + 1]
                    w1 = vecs[:, V_CW + 1 * 8 + c:V_CW + 1 * 8 + c + 1]
                    w2 = vecs[:, V_CW + 2 * 8 + c:V_CW + 2 * 8 + c + 1]
                    bb = vecs[:, V_CBIAS + c:V_CBIAS + c + 1]
                    cp(ue[:, ui, 0:2], uprev[:, c, :], [r_uprev], [r_ue[ui]])
                    if has_s:
                        cp(ues[:, ui, :, 0:2], prevS[:, c, :].rearrange("p (s r) -> p s r", r=2), [r_prevS], [r_ues[ui]])
                    t1 = rot("tmpB", NT4)
                    for (c0, n) in grpx:
                        b1 = newbank()
                        for k in range(KT):
                            mm(psf(b1)[:, 0:n], s_cc.lhs(k, cc_ * 128, 128), hT[:, k, c0:c0 + n], k == 0, k == KT - 1,
                               s_cc.regs + [r_hT[k]], [r_ps[b1]])
                        act(tmpB[:, t1, c0:c0 + n], psf(b1)[:, 0:n], AF.Copy, [r_ps[b1]], [r_tmpB[t1]])
                    yield
                    for (c0, n) in grpx:
                        b2 = newbank()
                        for k in range(KT):
                            mm(psf(b2)[:, 0:n], s_ch.lhs(k, cc_ * 128, 128), hT[:, k, c0:c0 + n], k == 0, k == KT - 1,
                               s_ch.regs + [r_hT[k]], [r_ps[b2]])
                        if c0 < TP:
                            tt(ue[:, ui, 2 + c0:2 + c0 + n], psf(b2)[:, 0:n], tmpB[:, t1, c0:c0 + n], ALU.mult, [r_ps[b2], r_tmpB[t1]], [r_ue[ui]])
                        else:
                            tt(ues[:, ui, :, 2:34], psf(b2)[:, 0:64].rearrange("p (s t) -> p s t", s=2),
                               tmpB[:, t1, c0:c0 + 64].rearrange("p (s t) -> p s t", s=2), ALU.mult, [r_ps[b2], r_tmpB[t1]], [r_ues[ui]])
                            stt(ue[:, ui, 0:2], psf(b2)[:, 64:66], flag[:, 0:1], tmpB[:, t1, c0 + 64:c0 + 66], ALU.mult, ALU.mult,
                                [r_ps[b2], r_tmpB[t1], r_flag], [r_ue[ui]])
                    yield
                    t2 = rot("tmpB", NT4)
                    for (c0, n) in grp:
                        b3 = newbank()
                        for k in range(KT):
                            mm(psf(b3)[:, 0:n], s_cb.lhs(k, cc_ * 128, 128), hT[:, k, c0:c0 + n], k == 0, k == KT - 1,
                               s_cb.regs + [r_hT[k]], [r_ps[b3]])
                        act(tmpB[:, t2, c0:c0 + n], psf(b3)[:, 0:n], AF.Copy, [r_ps[b3]], [r_tmpB[t2]])
                    cp(uprev[:, c, :], ue[:, ui, TP:TP + 2], [r_ue[ui]], [r_uprev])
                    if has_s:
                        cp(ulastS[:, c, :].rearrange("p (s r) -> p s r", r=2), ues[:, ui, :, 32:34], [r_ues[ui]], [r_ulastS])
                    ca = 0
                    ts(cacc[:, ca, 0:TP], ue[:, ui, 0:TP], w0, bb, ALU.mult, ALU.add, [r_ue[ui], r_vecs], [r_cacc[ca]])
                    stt(cacc[:, ca, 0:TP], ue[:, ui, 1:TP + 1], w1, cacc[:, ca, 0:TP], ALU.mult, ALU.add, [r_ue[ui], r_vecs, r_cacc[ca]], [r_cacc[ca]])
                    stt(cacc[:, ca, 0:TP], ue[:, ui, 2:TP + 2], w2, cacc[:, ca, 0:TP], ALU.mult, ALU.add, [r_ue[ui], r_vecs, r_cacc[ca]], [r_cacc[ca]])
                    if has_s:
                        cs3 = cacc[:, ca, TP:TM].rearrange("p (s t) -> p s t", s=2)
                        ts(cs3, ues[:, ui, :, 0:32], w0, bb, ALU.mult, ALU.add, [r_ues[ui], r_vecs], [r_cacc[ca]])
                        stt(cs3, ues[:, ui, :, 1:33], w1, cs3, ALU.mult, ALU.add, [r_ues[ui], r_vecs, r_cacc[ca]], [r_cacc[ca]])
                        stt(cs3, ues[:, ui, :, 2:34], w2, cs3, ALU.mult, ALU.add, [r_ues[ui], r_vecs, r_cacc[ca]], [r_cacc[ca]])
                    tt(ucbT[:, c, 0:ncol], cacc[:, ca, 0:ncol], tmpB[:, t2, 0:ncol], ALU.mult, [r_cacc[ca], r_tmpB[t2]], [r_ucbT[c]])
                    yield

        def cache_out(src, nrow, regs_src, dst):
            co = rot("tmpA", NT3)
            for half in range(2):
                b = newbank()
                for cc_ in range(4):
                    c = half * 4 + cc_
                    tr(psf(b)[0:nrow, cc_ * 128:(cc_ + 1) * 128], src[:, c, :], identf[:, :], [regs_src, r_identf], [r_ps[b]])
                cp(tmpA[0:nrow, co, half * 512:(half + 1) * 512], psf(b)[0:nrow, :], [r_ps[b]], [r_tmpA[co]])
            dma("sp", dst, tmpA[0:nrow, co, :], [r_tmpA[co]], [])

        def gated_fm_gen(w_gate_off, w_y, ysrc, r_ysrc, ncol, finish):
            grp = groups_of(ncol)
            for c2 in range(4):
                s_y = wload1(w_y[:, c2 * 256:(c2 + 1) * 256])
                s_g = wload1(w_in[:, w_gate_off + c2 * 256:w_gate_off + (c2 + 1) * 256])
                for cc_ in range(2):
                    c = c2 * 2 + cc_
                    for (c0, n) in grp:
                        bg = newbank()
                        for k in range(KT):
                            mm(psf(bg)[:, 0:n], s_g.lhs(k, cc_ * 128, 128), hT[:, k, c0:c0 + n], k == 0, k == KT - 1,
                               s_g.regs + [r_hT[k]], [r_ps[bg]])
                        ti = rot("tmpB", NT4)
                        act(tmpB[:, ti, 0:n], psf(bg)[:, 0:n], AF.Sigmoid, [r_ps[bg]], [r_tmpB[ti]])
                        by = newbank()
                        for k in range(KT):
                            mm(psf(by)[:, 0:n], s_y.lhs(k, cc_ * 128, 128), ysrc[:, k, c0:c0 + n], k == 0, k == KT - 1,
                               s_y.regs + r_ysrc, [r_ps[by]])
                        finish(c, c0, n, by, ti)
                        yield

        def norm_pre(src, regs, R):
            gi = rot("ssq", 4)
            sc = gi * 4
            xi = rot("xnb", NXN)
            act(xnb[0:R, xi, :], src, AF.Square, regs, [r_xnb[xi], r_ssq[gi]], accum_out=ssq[0:R, sc:sc + 1])
            act(rstd[0:R, sc:sc + 1], ssq[0:R, sc:sc + 1], AF.Ln, [r_ssq[gi], r_consts], [r_rstd[gi]], scale=1.0 / D, bias=consts[0:R, 0:1])
            act(rstd[0:R, sc:sc + 1], rstd[0:R, sc:sc + 1], AF.Exp, [r_rstd[gi]], [r_rstd[gi]], scale=-0.5)
            ts(xnb[0:R, xi, :], src, rstd[0:R, sc:sc + 1], None, ALU.mult, None, regs + [r_rstd[gi]], [r_xnb[xi]])
            return xi

        def norm_post(t, xi, G_idx, SH_idx, tslot=None, use_dve=False):
            R = t["R"]
            b = newbank()
            for k in range(KT):
                tr(psb(b)[:, k * 128:k * 128 + R], xnb[0:R, xi, k * 128:(k + 1) * 128], identb[0:R, 0:R], [r_xnb[xi], r_identb], [r_ps[b]])
            if not use_dve:
                for k in range(KT):
                    for (co, nn, seq) in t["seqruns"]:
                        act(hT[:, k, t["col0"] + co:t["col0"] + co + nn], psb(b)[:, k * 128 + co:k * 128 + co + nn], AF.Identity,
                            [r_ps[b], r_modT[G_idx], r_modT[SH_idx]], [r_hT[k]], scale=modT[:, G_idx, k, seq:seq + 1], bias=modT[:, SH_idx, k, seq:seq + 1])
                return
            if tslot is None:
                tslot = rot("tmpA", NT3)
            src3 = psb(b).rearrange("p (k t) -> p k t", k=KT)
            tmp3 = tmpA[:, tslot, :].rearrange("p (k t) -> p k t", k=KT)
            for (co, nn, seq) in t["seqruns"]:
                tt(tmp3[:, :, co:co + nn], src3[:, :, co:co + nn], modT[:, G_idx, :, seq:seq + 1].broadcast_to([128, KT, nn]), ALU.mult,
                   [r_ps[b], r_modT[G_idx]], [r_tmpA[tslot]])
                tt(hT[:, :, t["col0"] + co:t["col0"] + co + nn], tmp3[:, :, co:co + nn],
                   modT[:, SH_idx, :, seq:seq + 1].broadcast_to([128, KT, nn]), ALU.add,
                   [r_tmpA[tslot], r_modT[SH_idx]], r_hT)

        def wo_norm2_gen(tiles):
            slots = [wload(w_o[:, half * 512:(half + 1) * 512]) for half in range(2)]
            pend = None
            for t in tiles:
                R = t["R"]
                for half in range(2):
                    ta = rot("tmpA", NT3)
                    b = tm_proj(slots[half], mT, r_mT, t["col0"], R)
                    xr = xres[0:R, t["xi"], half * 512:(half + 1) * 512]
                    rx = r_xres[t["xi"]][half]
                    tt(tmpA[0:R, ta, 0:512], psf(b)[0:R, :], gB[0:R, 0, t["gty"], half * 512:(half + 1) * 512], ALU.mult,
                       [r_ps[b], r_gB[0][t["gty"]]], [r_tmpA[ta]])
                    tt(xr, xr, tmpA[0:R, ta, 0:512], ALU.add, [rx, r_tmpA[ta]], [rx])
                xi = norm_pre(xres[0:R, t["xi"], :], r_xres[t["xi"]], R)
                if pend is not None:
                    norm_post(pend[0], pend[1], 3, 2)
                pend = (t, xi)
                yield
            norm_post(pend[0], pend[1], 3, 2)
            yield

        def ffn1_gen(ncol):
            grp = groups_of(ncol)
            for j2 in range(11):
                sg_ = wload1(w_ffn_in[:, j2 * 256:(j2 + 1) * 256])
                su_ = wload1(w_ffn_in[:, DFF + j2 * 256:DFF + (j2 + 1) * 256])
                for jj in range(2):
                    j = j2 * 2 + jj
                    for (c0, n) in grp:
                        bg = newbank()
                        for k in range(KT):
                            mm(psf(bg)[:, 0:n], sg_.lhs(k, jj * 128, 128), hT[:, k, c0:c0 + n], k == 0, k == KT - 1,
                               sg_.regs + [r_hT[k]], [r_ps[bg]])
                        ti = rot("tmpB", NT4)
                        act(tmpB[:, ti, 0:n], psf(bg)[:, 0:n], AF.Silu, [r_ps[bg]], [r_tmpB[ti]])
                        bu = newbank()
                        for k in range(KT):
                            mm(psf(bu)[:, 0:n], su_.lhs(k, jj * 128, 128), hT[:, k, c0:c0 + n], k == 0, k == KT - 1,
                               su_.regs + [r_hT[k]], [r_ps[bu]])
                        tt(actT[:, j, c0:c0 + n], psf(bu)[:, 0:n], tmpB[:, ti, 0:n], ALU.mult, [r_ps[bu], r_tmpB[ti]], [r_actT[j]])
                        yield

        def ffn2_gen(tiles, hook=None):
            kgs = [(0, 8), (8, 8), (16, 6)]
            for half in range(2):
                if half == 1 and hook is not None:
                    hook()
                banks = []
                for t in tiles:
                    b = newbank()
                    reserved.add(b)
                    banks.append(b)
                for gi_, (kb, nk) in enumerate(kgs):
                    slot = wload(w_ffn_out[kb * 128:(kb + nk) * 128, half * 512:(half + 1) * 512], nk=nk)
                    for t, b in zip(tiles, banks):
                        tm_proj(slot, actT, r_actT[kb:kb + nk], t["col0"], t["R"], nk=nk, kbase=kb, start=(gi_ == 0), stop=(gi_ == 2), bank=b)
                        yield
                for t, b in zip(tiles, banks):
                    R = t["R"]
                    ta = rot("tmpA", NT3)
                    xr = xres[0:R, t["xi"], half * 512:(half + 1) * 512]
                    rx = r_xres[t["xi"]][half]
                    tt(tmpA[0:R, ta, 0:512], psf(b)[0:R, :], gB[0:R, 1, t["gty"], half * 512:(half + 1) * 512], ALU.mult,
                       [r_ps[b], r_gB[1][t["gty"]]], [r_tmpA[ta]])
                    tt(xr, tmpA[0:R, ta, 0:512], xr, ALU.add, [r_tmpA[ta], rx], [rx])
                    reserved.discard(b)

        def final_gen(tiles):
            for t in tiles:
                R = t["R"]
                gi = rot("ssq", 4)
                sc = gi * 4
                xi = rot("xnb", NXN)
                rx = r_xres[t["xi"]]
                act(xnb[0:R, xi, :], xres[0:R, t["xi"], :], AF.Square, rx, [r_xnb[xi], r_ssq[gi]], accum_out=ssq[0:R, sc:sc + 1])
                act(rstd[0:R, sc:sc + 1], ssq[0:R, sc:sc + 1], AF.Ln, [r_ssq[gi], r_consts], [r_rstd[gi]], scale=1.0 / D, bias=consts[0:R, 0:1])
                act(rstd[0:R, sc:sc + 1], rstd[0:R, sc:sc + 1], AF.Exp, [r_rstd[gi]], [r_rstd[gi]], scale=-0.5)
                ta = rot("tmpA", NT3)
                stt(tmpA[0:R, ta, :], xres[0:R, t["xi"], :], rstd[0:R, sc:sc + 1], nfg[0:R, :], ALU.mult, ALU.mult,
                    rx + [r_rstd[gi], r_nfg], [r_tmpA[ta]])
                dma("sp", t["out"], tmpA[0:R, ta, :], [r_tmpA[ta]], [])
                yield

        P.tag = "prefix"
        npre = cfg.ntp // TPB

        def pre_tiles(pb):
            return [dict(dram=xp[(pb * TPB + i) * 128:(pb * TPB + i + 1) * 128, :], R=128, col0=i * 128, seqruns=[(0, 128, 0)], vi=i, ci=i)
                    for i in range(TPB)]

        pre_info = {}

        def pre_front(pb):
            st = pb % 2
            tiles = pre_tiles(pb)
            info = pre_info.pop(pb) if pb in pre_info else norm_stats(tiles)
            yield from norm_apply_gen(tiles, info, 1, 0, use_dve=True)
            yield from gate_gen(TP, bcP[st], r_bcP[st], ebP[st], r_ebP[st])
            if pb + 1 < npre:
                nt_ = pre_tiles(pb + 1)
                pre_info[pb + 1] = norm_stats(nt_)
            yield from qk_gen(O_K, TP, bcP[st], r_bcP[st], kdP[st], r_kdP[st], False)
            for half in range(2):
                slot = wload(w_in[:, O_V + half * 512:O_V + (half + 1) * 512])
                yield from v_gen(slot, half, tiles, vtP[st], r_vtP[st])

        def pre_gla(pb):
            st = pb % 2
            yield from gla_gen(pre_tiles(pb), kdP[st], r_kdP[st], vtP[st], r_vtP[st], ebP[st], r_ebP[st], False)

        nblk = cfg.ntm // TPB

        def block_ctx(mb):
            has_s = (mb == 0)
            ctx = dict(mb=mb, has_s=has_s, last=(mb == nblk - 1), ncol=TM if has_s else TP, ncolx=T if has_s else TP)
            tiles = []
            for i in range(TPB):
                rows = slice((mb * TPB + i) * 128, (mb * TPB + i + 1) * 128)
                tiles.append(dict(dram=xm[rows, :], R=128, col0=i * 128, seqruns=[(0, 128, 0)], vi=i, ci=i, xi=i, gty=0, out=ym[rows, :]))
            gl_tiles = list(tiles)
            ntiles = list(tiles)
            if has_s:
                stile = dict(dram=xs[:, :], R=64, col0=TP, seqruns=[(0, 32, 1), (32, 32, 2)], xi=TPB, gty=1, out=ys[:, :])
                tiles.append(stile)
                ntiles.append(stile)
                for s in range(2):
                    gl_tiles.append(dict(R=32, col0=TP + 32 * s, vi=TPB + s, samp=s))
                ntiles.append(dict(dram=xh[:, :], R=2, col0=TM, seqruns=[(0, 2, 0)]))
            ctx.update(tiles=tiles, gl_tiles=gl_tiles, ntiles=ntiles)
            return ctx

        def front1(ctx):
            P.tag = "norm1"
            yield from norm_gen(ctx["ntiles"], 1, 0, batch=2)
            P.tag = "gate"
            yield from gate_gen(ctx["ncol"], bc, r_bc, eb, r_eb)
            P.tag = "qk"
            yield from qk_gen(O_Q, ctx["ncol"], bc, r_bc, qdT, r_qdT, True)

        def front2(ctx):
            P.tag = "qk"
            yield from qk_gen(O_K, ctx["ncol"], bc, r_bc, kdT, r_kdT, False)
            P.tag = "vg"
            for half in range(2):
                slot = wload(w_in[:, O_V + half * 512:O_V + (half + 1) * 512])
                yield from v_gen(slot, half, ctx["gl_tiles"], vtm, r_vtm)
            for half in range(2):
                slot = wload(w_in[:, O_G + half * 512:O_G + (half + 1) * 512])
                yield from g_gen(slot, half, ctx["gl_tiles"])

        interleave(tag_gen(pre_front(0), "prefix"), mod_rest_gen())
        for pb in range(npre):
            nxt = tag_gen(pre_front(pb + 1), "prefix") if pb + 1 < npre else tag_gen(front1(block_ctx(0)), "front1")
            interleave(tag_gen(pre_gla(pb), "prefix"), nxt)
        for h in range(NH):
            ts(S32[:, h, :], S32[:, h, :], flag[:, 0:1], None, ALU.mult, None, [r_S32[h], r_flag], [r_S32[h]])
            cp(Sbf[:, h, :], S32[:, h, :], [r_S32[h]], [r_Sbf[h]])

        def preload_x(ctx):
            for t in ctx["tiles"]:
                dma("sp", xres[0:t["R"], t["xi"], :], t["dram"], [], r_xres[t["xi"]])

        for mb in range(nblk):
            ctx = block_ctx(mb)
            cur_blk[0] = mb
            has_s, last, ncol, tiles = ctx["has_s"], ctx["last"], ctx["ncol"], ctx["tiles"]
            run(front2(ctx))
            interleave(tag_gen(gla_gen(ctx["gl_tiles"], kdT, r_kdT, vtm, r_vtm, eb, r_eb, True), "gla"),
                       tag_gen(conv_gen(has_s, ncol, ctx["ncolx"]), "conv"))
            P.tag = "gla"
            preload_x(ctx)
            if last:
                for h in range(NH):
                    dma("sp", sp_o[h, :, :], S32[:, h, :], [r_S32[h]], [])
            if has_s:
                cache_out(ulastS, 4, r_ulastS, cs_o[:, :])
            if last:
                cache_out(uprev, 2, r_uprev, cp_o[:, :])
            P.tag = "ya"

            def fin_a(c, c0, n, by, ti):
                tt(Pa[:, c, c0:c0 + n], psf(by)[:, 0:n], tmpB[:, ti, 0:n], ALU.mult, [r_ps[by], r_tmpB[ti]], [r_Pa[c]])
            run(gated_fm_gen(O_GA, w_gla_out, ogT, [r_ogT], ncol, fin_a))
            P.tag = "yb"

            def fin_b(c, c0, n, by, ti):
                tt(tmpB[:, ti, 0:n], psf(by)[:, 0:n], tmpB[:, ti, 0:n], ALU.mult, [r_ps[by], r_tmpB[ti]], [r_tmpB[ti]])
                tt(mT[:, c, c0:c0 + n], tmpB[:, ti, 0:n], Pa[:, c, c0:c0 + n], ALU.add, [r_tmpB[ti], r_Pa[c]], [r_mT[c]])
            run(gated_fm_gen(O_GB, w_conv_out, ucbT, r_ucbT, ncol, fin_b))
            P.tag = "wo"
            run(wo_norm2_gen(tiles))
            P.tag = "ffn1"
            run(ffn1_gen(ncol))
            nxt = tag_gen(front1(block_ctx(mb + 1)), "front1") if mb + 1 < nblk else None
            interleave(tag_gen(ffn2_gen(tiles, prefetch_k if mb + 1 < nblk else None), "ffn2"), nxt)
            P.tag = "final"
            run(final_gen(tiles))


        P.emit(nc, es)
    nc._prog = P
    return nc


def _pack_vec(v, nk):
    return np.ascontiguousarray(v.reshape(nk, 128).T)


def prepare_inputs(cfg, ncores, x_prompt, x_sample, c_prompt, c_sample, state_gla, cache_conv, w_mod, b_mod, norm1_g,
                   w_in, w_alpha, b_alpha, gla_norm_g, w_gla_out, conv_w, conv_b, w_conv_out, w_o,
                   norm2_g, w_ffn_in, w_ffn_out, norm_f_g):
    f = np.float32
    TPB = cfg.tpb
    TP = TPB * 128
    T = TP + 66
    half_tok = cfg.ntm * 128
    vecs = np.zeros((128, NV), f)
    vecs[:, V_N1:V_N1 + 8] = _pack_vec(np.asarray(norm1_g[0], f), 8)
    vecs[:, V_N2:V_N2 + 8] = _pack_vec(np.asarray(norm2_g[0], f), 8)
    vecs[:, V_BMOD:V_BMOD + 48] = _pack_vec(np.asarray(b_mod[0], f), 48)
    vecs[:, V_BAL:V_BAL + 4] = _pack_vec(np.asarray(b_alpha[0], f), 4)
    vecs[:, V_CW:V_CW + 24] = np.concatenate([_pack_vec(np.asarray(conv_w[0, r], f), 8) for r in range(3)], axis=1)
    vecs[:, V_CBIAS:V_CBIAS + 8] = _pack_vec(np.asarray(conv_b[0], f), 8)
    bcp = np.zeros((128, 4 * D), f)
    bcp[:, 0:D] = np.broadcast_to(np.tile(np.asarray(gla_norm_g[0], f), 4), (128, D))
    bcp[:, D:2 * D] = np.broadcast_to(np.asarray(norm_f_g, f), (128, D))
    bcp[:, 2 * D:3 * D] = np.broadcast_to(np.asarray(b_mod[0, 2 * D:3 * D], f), (128, D))
    bcp[:, 3 * D:4 * D] = np.broadcast_to(np.asarray(b_mod[0, 5 * D:6 * D], f), (128, D))
    ident = np.eye(128, dtype=f)
    maskT = np.triu(np.ones((128, 128), f))
    rmask = np.ones((128, T), f)
    for c in list(range(0, TP, 128)) + [TP, TP + 32, TP + 64]:
        rmask[:, c] = 0.0
    shared = dict(w_mod=np.asarray(w_mod[0], f), w_in=np.asarray(w_in[0], f), w_alpha=np.asarray(w_alpha[0], f),
                  w_gla_out=np.asarray(w_gla_out[0], f), w_conv_out=np.asarray(w_conv_out[0], f), w_o=np.asarray(w_o[0], f),
                  w_ffn_in=np.asarray(w_ffn_in[0], f), w_ffn_out=np.asarray(w_ffn_out[0], f),
                  vecs=vecs, bcp=bcp, ident=ident, maskT=maskT, rmask=rmask)
    in_maps = []
    for c in range(ncores):
        sq, hf = c // 2, c % 2
        xseq = np.asarray(x_prompt[sq], f)
        m = dict(shared)
        m["xm"] = np.ascontiguousarray(xseq[hf * half_tok:(hf + 1) * half_tok])
        m["xp"] = np.ascontiguousarray(xseq[0:cfg.ntp * 128])
        m["xs"] = np.ascontiguousarray(np.asarray(x_sample[2 * c:2 * c + 2], f).reshape(64, D))
        m["xh"] = np.ascontiguousarray(xseq[half_tok - 2:half_tok]) if hf == 1 else np.ascontiguousarray(xseq[0:2])
        cs = np.stack([np.asarray(c_prompt[sq], f), np.asarray(c_sample[2 * c], f), np.asarray(c_sample[2 * c + 1], f)], axis=0)
        m["cT"] = np.ascontiguousarray(cs.reshape(3, KT, 128).transpose(2, 1, 0).reshape(128, KT * 3))
        m["flag"] = np.full((128, 1), float(hf), f)
        m["s0"] = np.ascontiguousarray(np.asarray(state_gla[0, 2 * c:2 * c + 2], f))
        m["cc0"] = np.ascontiguousarray(np.asarray(cache_conv[0, 2 * c:2 * c + 2], f).reshape(4, D))
        in_maps.append(m)
    return in_maps


def assemble(cfg, ncores, res):
    f = np.float32
    nseq = ncores // 2
    half_tok = cfg.ntm * 128
    y_prompt = np.zeros((nseq, 2 * half_tok, D), f)
    y_sample = np.zeros((2 * ncores, 32, D), f)
    sgp = np.zeros((1, nseq, NH, 128, 256), f)
    ccp = np.zeros((1, nseq, 2, D), f)
    sgs = np.zeros((1, 2 * ncores, NH, 128, 256), f)
    ccs = np.zeros((1, 2 * ncores, 2, D), f)
    for c in range(ncores):
        r = res[c]
        sq, hf = c // 2, c % 2
        y_prompt[sq, hf * half_tok:(hf + 1) * half_tok] = r["ym"]
        y_sample[2 * c:2 * c + 2] = np.asarray(r["ys"]).reshape(2, 32, D)
        if hf == 1:
            sgp[0, sq] = r["sp"]
            ccp[0, sq] = r["cp"]
        sgs[0, 2 * c:2 * c + 2] = r["ss"]
        ccs[0, 2 * c:2 * c + 2] = np.asarray(r["cs"]).reshape(2, 2, D)
    return (y_prompt, y_sample, sgp, ccp, sgs, ccs)


_NC_CACHE = {}


def kernel(**inputs):
    cfg = Cfg(16, 16, 4)
    ncores = 8
    in_maps = prepare_inputs(cfg, ncores, **inputs)
    if "nc" not in _NC_CACHE:
        _NC_CACHE["nc"] = build(cfg)
    nc = _NC_CACHE["nc"]
    res = run_bass_kernel_spmd(nc, in_maps, core_ids=list(range(ncores)))
    return assemble(cfg, ncores, res.results)
```

```python
import numpy as np
from contextlib import ExitStack
import concourse.bass as bass
import concourse.mybir as mybir
from concourse.bass_utils import run_bass_kernel_spmd

F32 = mybir.dt.float32
BF16 = mybir.dt.bfloat16
AF = mybir.ActivationFunctionType
ALU = mybir.AluOpType

D = 1024
KT = 8
IN_DIM = 8208
DFF = 2816
NH = 4
EPS = 1e-6
O_Q, O_K, O_V, O_G, O_A, O_CB, O_CC, O_CH, O_GA, O_GB = 0, 512, 1024, 2048, 3072, 3088, 4112, 5136, 6160, 7184

V_N1, V_N2, V_BMOD, V_BAL, V_CW, V_CBIAS = 0, 8, 16, 64, 68, 92
NV = 100


class Reg:
    __slots__ = ("name", "lw", "rd", "ov")

    def __init__(self, name):
        self.name = name
        self.lw = None
        self.rd = []
        self.ov = []


def overlap(a, b):
    a.ov.append(b)
    b.ov.append(a)


class Op:
    __slots__ = ("eng", "fn", "deps", "signal", "count", "dma", "dsem", "dval", "idx", "tag")

    def __init__(self, eng, fn, dma):
        self.eng = eng
        self.fn = fn
        self.deps = []
        self.signal = False
        self.count = 0
        self.dma = dma
        self.dsem = None
        self.dval = 0


class Prog:
    ENGS = ("pe", "act", "dve", "pool", "sp")
    NDMA = {"pe": 8, "act": 8, "dve": 8, "pool": 6, "sp": 8}

    def __init__(self):
        self.ops = {e: [] for e in self.ENGS}
        self.ndma = {e: 0 for e in self.ENGS}
        self.dma_ops = {e: [] for e in self.ENGS}
        self.tag = ""

    def op(self, eng, fn, reads=(), writes=(), dma=False, pe_acc=False):
        o = Op(eng, fn, dma)
        o.tag = self.tag
        deps = []
        for r in reads:
            if r.lw is not None:
                deps.append(r.lw)
            for b in r.ov:
                if b.lw is not None:
                    deps.append(b.lw)
        for w in writes:
            if w.lw is not None:
                deps.append(w.lw)
            deps.extend(w.rd)
            for b in w.ov:
                if b.lw is not None:
                    deps.append(b.lw)
                deps.extend(b.rd)
        seen = set()
        for d in deps:
            if id(d) in seen or d is o:
                continue
            seen.add(id(d))
            if eng == "pe" and d.eng == "pe" and not d.dma:
                continue
            d.signal = True
            o.deps.append(d)
        for r in reads:
            r.rd.append(o)
        for w in writes:
            w.lw = o
            w.rd = []
        if dma:
            i = self.ndma[eng]
            self.ndma[eng] += 1
            o.idx = i
            if i >= self.NDMA[eng]:
                prev = self.dma_ops[eng][i - self.NDMA[eng]]
                o.deps.append(prev)
            self.dma_ops[eng].append(o)
            o.signal = True
        self.ops[eng].append(o)
        return o

    def emit(self, nc, es, final_wait_eng="sp"):
        sems = {e: es.enter_context(nc.semaphore("s_" + e)) for e in self.ENGS}
        dsems = {e: [es.enter_context(nc.semaphore("d_%s%d" % (e, i))) for i in range(self.NDMA[e])]
                 for e in self.ENGS if self.ndma[e] > 0}
        for e in self.ENGS:
            c = 0
            for o in self.ops[e]:
                if o.dma:
                    o.dsem = dsems[e][o.idx % self.NDMA[e]]
                    o.dval = 16 * (o.idx // self.NDMA[e] + 1)
                elif o.signal:
                    c += 1
                    o.count = c
        block = es.enter_context(nc.Block())
        handles = {"pe": block.tensor, "act": block.scalar, "dve": block.vector, "pool": block.gpsimd, "sp": block.sync}
        prog = self

        def make(e):
            def body(eng):
                waited = {}
                for o in prog.ops[e]:
                    need = {}
                    for d in o.deps:
                        if d.dma:
                            s, v = d.dsem, d.dval
                        else:
                            s, v = sems[d.eng], d.count
                        k = id(s)
                        if waited.get(k, 0) >= v:
                            continue
                        if k not in need or need[k][1] < v:
                            need[k] = (s, v)
                    for k, (s, v) in need.items():
                        eng.wait_ge(s, v)
                        waited[k] = v
                    ins = o.fn(eng)
                    if o.dma:
                        ins.then_inc(o.dsem, 16)
                    elif o.signal:
                        ins.then_inc(sems[e], 1)
                if e == final_wait_eng:
                    for e2 in prog.ENGS:
                        n = prog.ndma[e2]
                        for i in range(min(n, prog.NDMA[e2])):
                            cnt = (n - 1 - i) // prog.NDMA[e2] + 1
                            eng.wait_ge(dsems[e2][i], 16 * cnt)
            return body

        for e in self.ENGS:
            handles[e](make(e))


class Cfg:
    def __init__(self, ntm=16, ntp=16, tpb=4):
        self.ntm = ntm
        self.ntp = ntp
        self.tpb = tpb
        assert ntm % tpb == 0 and ntp % tpb == 0


def build(cfg):
    nc = bass.Bass("TRN2", target_bir_lowering=False)
    P = Prog()
    TPB = cfg.tpb
    TP = TPB * 128
    TM = TP + 64
    T = TM + 2

    def din(name, shape):
        return nc.dram_tensor(name, list(shape), F32, kind="ExternalInput").ap()

    def dout(name, shape):
        return nc.dram_tensor(name, list(shape), F32, kind="ExternalOutput").ap()

    xm = din("xm", (cfg.ntm * 128, D))
    xp = din("xp", (cfg.ntp * 128, D))
    xs = din("xs", (64, D))
    xh = din("xh", (2, D))
    cT_d = din("cT", (128, KT * 3))
    flag_d = din("flag", (128, 1))
    s0_d = din("s0", (2, NH, 128, 256))
    cc0_d = din("cc0", (4, D))
    w_mod = din("w_mod", (D, 6 * D))
    w_in = din("w_in", (D, IN_DIM))
    w_alpha = din("w_alpha", (16, 512))
    w_gla_out = din("w_gla_out", (D, D))
    w_conv_out = din("w_conv_out", (D, D))
    w_o = din("w_o", (D, D))
    w_ffn_in = din("w_ffn_in", (D, 2 * DFF))
    w_ffn_out = din("w_ffn_out", (DFF, D))
    vecs_d = din("vecs", (128, NV))
    bcp_d = din("bcp", (128, 4 * D))
    ident_d = din("ident", (128, 128))
    maskT_d = din("maskT", (128, 128))
    rmask_d = din("rmask", (128, T))

    ym = dout("ym", (cfg.ntm * 128, D))
    ys = dout("ys", (64, D))
    sp_o = dout("sp", (NH, 128, 256))
    cp_o = dout("cp", (2, D))
    ss_o = dout("ss", (2, NH, 128, 256))
    cs_o = dout("cs", (4, D))

    es = ExitStack()
    with es:
        def sb(name, shape, dt=F32):
            return es.enter_context(nc.sbuf_tensor("sb_" + name, list(shape), dt))

        NW = 8
        wring = sb("wring", (128, NW // 2, KT, 512), BF16)
        wregs = [Reg("w%d" % i) for i in range(NW)]
        wa = sb("wa", (128, KT, 16), BF16); r_wa = Reg("wa")
        walpha = sb("walpha", (16, 512), BF16); r_walpha = Reg("walpha")
        vecs = sb("vecs", (128, NV)); r_vecs = Reg("vecs")
        gn4 = sb("gn4", (128, 512)); r_gn4 = Reg("gn4")
        nfg = sb("nfg", (128, D)); r_nfg = Reg("nfg")
        identf = sb("identf", (128, 128)); r_identf = Reg("identf")
        identb = sb("identb", (128, 128), BF16); r_identb = Reg("identb")
        maskT = sb("maskT", (128, 128)); r_maskT = Reg("maskT")
        rmask = sb("rmask", (128, T)); r_rmask = Reg("rmask")
        flag = sb("flag", (128, 1)); r_flag = Reg("flag")
        cTf = sb("cTf", (128, KT, 3)); r_cTf = Reg("cTf")
        cTb = sb("cTb", (128, KT, 3), BF16); r_cTb = Reg("cTb")
        r_cB = Reg("cB")
        gB = sb("gB", (128, 2, 2, D)); r_gB = [[Reg("gB%d%d" % (a, b)) for b in range(2)] for a in range(2)]
        modT = sb("modT", (128, 4, KT, 3)); r_modT = [Reg("modT%d" % i) for i in range(4)]
        consts = sb("consts", (128, 4)); r_consts = Reg("consts")
        nbal = sb("nbal", (128, 4)); r_nbal = Reg("nbal")
        hT = sb("hT", (128, KT, T), BF16); r_hT = [Reg("hT%d" % k) for k in range(KT)]
        NXT = TPB + 1
        xres = sb("xres", (128, NXT, D)); r_xres = [[Reg("xres%d_%d" % (i, hf)) for hf in range(2)] for i in range(NXT)]
        ssq = sb("ssq", (128, 16)); r_ssq = [Reg("ssq%d" % i) for i in range(4)]
        rstd = sb("rstd", (128, 16)); r_rstd = [Reg("rstd%d" % i) for i in range(4)]
        NXN = 2
        xnb = sb("xnb", (128, NXN, D), BF16); r_xnb = [Reg("xnb%d" % i) for i in range(NXN)]

        NT3 = 4
        tmpA = sb("tmpA", (128, NT3, D)); r_tmpA = [Reg("tmpA%d" % i) for i in range(NT3)]
        NT4 = 4
        tmpB = sb("tmpB", (128, NT4, T)); r_tmpB = [Reg("tmpB%d" % i) for i in range(NT4)]
        aT = sb("aT", (16, T), BF16); r_aT = Reg("aT")
        S32 = sb("S32", (128, NH, 256)); r_S32 = [Reg("S32_%d" % h) for h in range(NH)]
        Sbf = sb("Sbf", (128, NH, 256), BF16); r_Sbf = [Reg("Sbf_%d" % h) for h in range(NH)]
        Sbfs = sb("Sbfs", (128, 2, NH, 256), BF16); r_Sbfs = [Reg("Sbfs_%d" % s) for s in range(2)]
        eb = sb("eb", (128, NH, TPB + 2)); r_eb = [Reg("eb%d" % h) for h in range(NH)]
        eb2 = sb("eb2", (128, NH, TPB + 2)); r_eb2 = [Reg("eb2_%d" % h) for h in range(NH)]
        ucbF = sb("ucbT", (128, max(KT * TM, 2 * KT * 128)), BF16); r_ucbT = [Reg("ucbT%d" % k) for k in range(KT)]
        ucbT = ucbF[:, 0:KT * TM].rearrange("p (k t) -> p k t", k=KT)
        cB = ucbF[:, 0:2 * KT * 128].rearrange("p (a k r) -> p a k r", a=2, k=KT)
        for rj in r_ucbT:
            overlap(r_cB, rj)
        NSM = 2
        sTm = sb("sTm", (128, NSM, NH, 128), BF16); r_sTm = [Reg("sTm%d" % i) for i in range(NSM)]
        kdtm = sb("kdtm", (128, NSM, NH, 128), BF16); r_kdtm = [Reg("kdtm%d" % i) for i in range(NSM)]
        stmp = sb("stmp", (128, 2, 256)); r_stmp = [Reg("stmp%d" % i) for i in range(2)]
        ssqo = sb("ssqo", (128, 2, 4)); r_ssqo = [Reg("ssqo%d" % i) for i in range(2)]
        rstdo = sb("rstdo", (128, 2, 4)); r_rstdo = [Reg("rstdo%d" % i) for i in range(2)]
        ogb = sb("ogb", (128, 2, D), BF16); r_ogb = [Reg("ogb%d" % i) for i in range(2)]
        ogT = sb("ogT", (128, KT, TM), BF16); r_ogT = Reg("ogT")
        ue = sb("ue", (128, 2, TP + 2)); r_ue = [Reg("ue%d" % i) for i in range(2)]
        ues = sb("ues", (128, 2, 2, 34)); r_ues = [Reg("ues%d" % i) for i in range(2)]
        uprev = sb("uprev", (128, KT, 2)); r_uprev = Reg("uprev")
        prevS = sb("prevS", (128, KT, 4)); r_prevS = Reg("prevS")
        ulastS = sb("ulastS", (128, KT, 4)); r_ulastS = Reg("ulastS")
        cacc = sb("cacc", (128, 1, TM)); r_cacc = [Reg("cacc0")]
        lay = {}
        off = 0
        for nm, sz in (("bc", NH * TM * 2), ("qdT", NH * TM), ("kdT", NH * TM), ("vtm", (TPB + 2) * D), ("gsg", (TPB + 2) * D)):
            lay[nm] = (off, sz, "G1")
            off += sz
        NA = off
        lay["Pa"] = (0, KT * TM, "G2")
        lay["mT"] = (KT * TM, KT * TM, "G2")
        lay["actT"] = (NA - 22 * TM, 22 * TM, "G3")
        lay["kdP1"] = (lay["qdT"][0], NH * TP, "P1")
        o1 = lay["vtm"][0] + TPB * D
        lay["bcP1"] = (o1, NH * TP * 2, "P1")
        lay["vtP1"] = (o1 + NH * TP * 2, TPB * D, "P1")
        assert o1 + NH * TP * 2 + TPB * D <= NA and NH * TP <= lay["qdT"][1] and lay["actT"][0] >= lay["kdT"][0]
        arena = sb("arena", (128, NA), BF16)

        def aview(nm):
            lo, sz, _ = lay[nm]
            return arena[:, lo:lo + sz]

        bc = aview("bc").bitcast(F32).rearrange("p (h t) -> p h t", h=NH)
        qdT = aview("qdT").rearrange("p (h t) -> p h t", h=NH)
        kdT = aview("kdT").rearrange("p (h t) -> p h t", h=NH)
        vtm = aview("vtm").rearrange("p (n d) -> p n d", d=D)
        gsg = aview("gsg").rearrange("p (n d) -> p n d", d=D)
        Pa = aview("Pa").rearrange("p (k t) -> p k t", k=KT)
        mT = aview("mT").rearrange("p (k t) -> p k t", k=KT)
        actT = aview("actT").rearrange("p (k t) -> p k t", k=22)
        kdP1 = aview("kdP1").rearrange("p (h t) -> p h t", h=NH)
        bcP1 = aview("bcP1").bitcast(F32).rearrange("p (h t) -> p h t", h=NH)
        vtP1 = aview("vtP1").rearrange("p (n d) -> p n d", d=D)
        areg = []

        def mk(nm, nparts):
            lo, sz, grp_ = lay[nm]
            step = sz // nparts
            lst = []
            for i in range(nparts):
                r = Reg("%s_%d" % (nm, i))
                areg.append((r, lo + i * step, lo + (i + 1) * step, grp_))
                lst.append(r)
            return lst

        r_bc = mk("bc", NH)
        r_qdT = mk("qdT", NH)
        r_kdT = mk("kdT", NH)
        r_vtm = mk("vtm", TPB + 2)
        r_gsg = mk("gsg", TPB + 2)
        r_Pa = mk("Pa", KT)
        r_mT = mk("mT", KT)
        r_actT = mk("actT", 22)
        r_kdP1 = mk("kdP1", NH)
        r_bcP1 = mk("bcP1", NH)
        r_vtP1 = mk("vtP1", TPB)
        for ia in range(len(areg)):
            for ib in range(ia + 1, len(areg)):
                (ra, lo1, hi1, g1), (rb, lo2, hi2, g2) = areg[ia], areg[ib]
                if g1 != g2 and lo1 < hi2 and lo2 < hi1:
                    overlap(ra, rb)
        bcP, r_bcP = [bc, bcP1], [r_bc, r_bcP1]
        kdP, r_kdP = [kdT, kdP1], [r_kdT, r_kdP1]
        vtP, r_vtP = [vtm, vtP1], [r_vtm, r_vtP1]
        ebP, r_ebP = [eb, eb2], [r_eb, r_eb2]

        psum = es.enter_context(nc.psum_tensor("psum", [128, 8, 512], F32))
        r_ps = [Reg("ps%d" % i) for i in range(8)]
        ps_ctr = [0]

        reserved = set()

        def newbank():
            while True:
                i = ps_ctr[0] % 8
                ps_ctr[0] += 1
                if i not in reserved:
                    return i

        def psf(b):
            return psum[:, b, :]

        def psb(b):
            return psum[:, b, :].bitcast(BF16)

        obanks = [[0, 1], [2, 3]]
        ctr = {"ob": 0, "xnb": 0, "ssq": 0, "tmpA": 0, "tmpB": 0, "w": 0, "sm": 0, "st": 0, "so": 0, "og": 0, "ue": 0, "ca": 0}

        held = {"tmpA": set()}

        def rot(name, n):
            while True:
                i = ctr[name] % n
                ctr[name] += 1
                if i not in held.get(name, ()):
                    return i

        def dma(eng, out, in_, reads, writes):
            return P.op(eng, lambda e: e.dma_start(out=out, in_=in_), reads=reads, writes=writes, dma=True)

        def act(out, in_, func, reads, writes, bias=None, scale=None, accum_out=None):
            kw = {}
            if bias is not None:
                kw["bias"] = bias
            if scale is not None:
                kw["scale"] = scale
            if accum_out is not None:
                kw["accum_out"] = accum_out
            return P.op("act", lambda e: e.activation(out=out, in_=in_, func=func, **kw), reads=reads, writes=writes)

        def tt(out, in0, in1, op, reads, writes, eng="dve"):
            return P.op(eng, lambda e: e.tensor_tensor(out=out, in0=in0, in1=in1, op=op), reads=reads, writes=writes)

        def ts(out, in0, s1, s2, op0, op1, reads, writes, eng="dve"):
            if s2 is None:
                return P.op(eng, lambda e: e.tensor_scalar(out=out, in0=in0, scalar1=s1, scalar2=None, op0=op0), reads=reads, writes=writes)
            return P.op(eng, lambda e: e.tensor_scalar(out=out, in0=in0, scalar1=s1, scalar2=s2, op0=op0, op1=op1), reads=reads, writes=writes)

        def stt(out, in0, scalar, in1, op0, op1, reads, writes, eng="dve"):
            return P.op(eng, lambda e: e.scalar_tensor_tensor(out=out, in0=in0, scalar=scalar, in1=in1, op0=op0, op1=op1), reads=reads, writes=writes)

        def cp(out, in_, reads, writes, eng="dve"):
            return P.op(eng, lambda e: e.tensor_copy(out=out, in_=in_), reads=reads, writes=writes)

        def mm(out, lhsT, rhs, start, stop, reads, writes):
            return P.op("pe", lambda e: e.matmul(out, lhsT=lhsT, rhs=rhs, start=start, stop=stop), reads=reads, writes=writes)

        def tr(out, in_, ident, reads, writes):
            return P.op("pe", lambda e: e.transpose(out=out, in_=in_, identity=ident), reads=reads, writes=writes)

        class W:
            def __init__(self, slots):
                self.slots = slots
                self.regs = [wregs[i] for i in slots]

            def lhs(self, k, col, m):
                sl = self.slots[col // 256]
                c = (sl % 2) * 256 + col % 256
                return wring[:, sl // 2, k, c:c + m]

            def rhs(self, k):
                return wring[:, self.slots[0] // 2, k, :]

        wreserved = set()
        wcache = {}
        cur_blk = [0]
        CACHE_W = True

        def wfetch(src_ap, nk, dst, dregs, ncols):
            key = (src_ap.name, src_ap.offset, tuple(src_ap.shape))
            cacheable = CACHE_W and src_ap.name != "w_mod" and (cur_blk[0] >= 1 or not src_ap.name.startswith("w_ffn"))
            if cacheable and key in wcache:
                sc_ap, sreg = wcache[key]
                dma("pool", dst, sc_ap.rearrange("p (k c) -> p k c", k=nk), reads=[sreg], writes=dregs)
                wflush(2)
                return
            dma("pool", dst, src_ap.rearrange("(k p) c -> p k c", p=128), reads=[], writes=dregs)
            if cacheable and key not in wpend_keys:
                wpend.append((key, nk, ncols, dst, dregs, ctr["w"]))
                wpend_keys.add(key)
            wflush(2)

        wpend = []
        wpend_keys = set()
        wnum = [0]
        WB_Q = "pool"

        def wflush(keep):
            while wpend and (len(wpend) > keep or ctr["w"] - wpend[0][5] >= 3):
                key, nk, ncols, dst, dregs, _ = wpend.pop(0)
                sc_ap = nc.dram_tensor("wsc%d" % wnum[0], [128, nk * ncols], BF16, kind="Internal").ap()
                sreg = Reg("wsc%d" % wnum[0])
                wnum[0] += 1
                dma(WB_Q, sc_ap.rearrange("p (k c) -> p k c", k=nk), dst, reads=dregs, writes=[sreg])
                wcache[key] = (sc_ap, sreg)
                wpend_keys.discard(key)

        def wload(src_ap, nk=KT):
            while True:
                if ctr["w"] % 2:
                    ctr["w"] += 1
                s0 = ctr["w"] % NW
                ctr["w"] += 2
                if s0 not in wreserved and s0 + 1 not in wreserved:
                    break
            wfetch(src_ap, nk, wring[:, s0 // 2, 0:nk, :], [wregs[s0], wregs[s0 + 1]], 512)
            return W([s0, s0 + 1])

        def wload1(src_ap, nk=KT):
            while True:
                s0 = ctr["w"] % NW
                ctr["w"] += 1
                if s0 not in wreserved:
                    break
            wfetch(src_ap, nk, wring[:, s0 // 2, 0:nk, (s0 % 2) * 256:(s0 % 2 + 1) * 256], [wregs[s0]], 256)
            return W([s0])

        def groups_of(ncols):
            g = []
            c = 0
            while c < min(ncols, TP):
                n = min(512, TP - c)
                g.append((c, n))
                c += n
            if ncols > TP:
                g.append((TP, ncols - TP))
            return g

        P.tag = "setup"
        dma("sp", vecs[:], vecs_d[:, :], [], [r_vecs])
        dma("sp", identf[:], ident_d[:, :], [], [r_identf])
        dma("sp", maskT[:], maskT_d[:, :], [], [r_maskT])
        dma("sp", rmask[:], rmask_d[:, :], [], [r_rmask])
        dma("sp", flag[:], flag_d[:, :], [], [r_flag])
        dma("sp", cTf[:], cT_d.rearrange("p (k s) -> p k s", s=3), [], [r_cTf])
        dma("sp", gn4[:], bcp_d[:, 0:512], [], [r_gn4])
        dma("sp", nfg[:], bcp_d[:, D:2 * D], [], [r_nfg])
        dma("pool", wa[:], w_in[:, O_A:O_A + 16].rearrange("(k p) c -> p k c", p=128), [], [r_wa])
        dma("pool", walpha[:], w_alpha[:, :], [], [r_walpha])
        P.op("dve", lambda e: e.memset(consts[:, 0:1], EPS), writes=[r_consts])
        P.op("dve", lambda e: e.memset(consts[:, 1:2], 1.0), writes=[r_consts])
        P.op("dve", lambda e: e.memset(consts[:, 2:3], float(np.log(128.0 ** -0.5))), writes=[r_consts])
        P.op("dve", lambda e: e.memset(consts[:, 3:4], 0.0), writes=[r_consts])
        ts(nbal[:], vecs[:, V_BAL:V_BAL + 4], -1.0, None, ALU.mult, None, [r_vecs], [r_nbal])
        P.op("dve", lambda e: e.memset(ssq[:], 1.0), writes=r_ssq)
        P.op("dve", lambda e: e.memset(ssqo[:], 1.0), writes=r_ssqo)
        cp(identb[:], identf[:], [r_identf], [r_identb])
        cp(cTb[:], cTf[:], [r_cTf], [r_cTb])
        for k in range(KT):
            cp(cB[:, 0, k, :], cTf[:, k, 0:1].broadcast_to([128, 128]), [r_cTf], [r_cB])
            cp(cB[:, 1, k, 0:32], cTf[:, k, 1:2].broadcast_to([128, 32]), [r_cTf], [r_cB])
            cp(cB[:, 1, k, 32:64], cTf[:, k, 2:3].broadcast_to([128, 32]), [r_cTf], [r_cB])
        for h in range(NH):
            P.op("dve", lambda e, h=h: e.memset(S32[:, h, :], 0.0), writes=[r_S32[h]])
        P.op("dve", lambda e: e.memset(uprev[:], 0.0), writes=[r_uprev])
        b = newbank()
        tc0 = rot("tmpA", NT3)
        dma("sp", tmpA[0:4, tc0, :], cc0_d[:, :], [], [r_tmpA[tc0]])
        for c in range(KT):
            tr(psf(b)[:, c * 4:(c + 1) * 4], tmpA[0:4, tc0, c * 128:(c + 1) * 128], identf[0:4, 0:4], [r_tmpA[tc0], r_identf], [r_ps[b]])
        cp(prevS[:], psf(b)[:, 0:KT * 4].rearrange("p (c f) -> p c f", f=4), [r_ps[b]], [r_prevS])

        P.tag = "mod"
        def mod_fm_gen(parts):
            for (six, mi) in parts:
                for half in range(2):
                    c0 = six * D + half * 512
                    slot = wload(w_mod[:, c0:c0 + 512])
                    b = newbank()
                    for sub in range(4):
                        for k in range(KT):
                            mm(psf(b)[:, sub * 3:(sub + 1) * 3], slot.lhs(k, sub * 128, 128), cTb[:, k, :],
                               k == 0, k == KT - 1, slot.regs + [r_cTb], [r_ps[b]])
                    for sub in range(4):
                        kk = half * 4 + sub
                        col = V_BMOD + six * 8 + kk
                        ts(modT[:, mi, kk, :], psf(b)[:, sub * 3:(sub + 1) * 3], vecs[:, col:col + 1], None, ALU.add, None,
                           [r_ps[b], r_vecs], [r_modT[mi]])
                    yield
                if mi in (1, 3):
                    voff = V_N1 if mi == 1 else V_N2
                    for s_ in range(3):
                        stt(modT[:, mi, :, s_], modT[:, mi, :, s_], 1.0, vecs[:, voff:voff + 8], ALU.add, ALU.mult,
                            [r_modT[mi], r_vecs], [r_modT[mi]])

        def mod_tm_gen():
            for gi, six in enumerate((2, 5)):
                tb = rot("tmpA", NT3)
                dma("sp", tmpA[:, tb, :], bcp_d[:, (2 + gi) * D:(3 + gi) * D], [], [r_tmpA[tb]])
                for half in range(2):
                    c0 = six * D + half * 512
                    slot = wload(w_mod[:, c0:c0 + 512])
                    for ty in range(2):
                        b = newbank()
                        R = 128 if ty == 0 else 64
                        for k in range(KT):
                            mm(psf(b)[0:R, :], cB[:, ty, k, 0:R], slot.rhs(k), k == 0, k == KT - 1,
                               [r_cB] + slot.regs, [r_ps[b]])
                        tt(gB[0:R, gi, ty, half * 512:(half + 1) * 512], psf(b)[0:R, :], tmpA[0:R, tb, half * 512:(half + 1) * 512], ALU.add,
                           [r_ps[b], r_tmpA[tb]], [r_gB[gi][ty]])
                    yield

        def mod_rest_gen():
            P.tag = "mod"
            yield from mod_fm_gen([(3, 2), (4, 3)])
            yield from mod_tm_gen()

        PRE0 = {}
        for _ in mod_fm_gen([(0, 0), (1, 1)]):
            pass

        def run(g):
            for _ in g:
                pass

        def tag_gen(g, tag):
            while True:
                P.tag = tag
                try:
                    next(g)
                except StopIteration:
                    return
                yield

        def interleave_ratio(side, main, ratio):
            acc = 0.0
            side_live, main_live = side is not None, main is not None
            while side_live or main_live:
                if side_live:
                    try:
                        next(side)
                    except StopIteration:
                        side_live = False
                acc += ratio if side_live else 1.0
                while main_live and acc >= 1.0:
                    acc -= 1.0
                    try:
                        next(main)
                    except StopIteration:
                        main_live = False
                if not main_live:
                    acc = 0.0

        def interleave(*gens):
            gens = [g for g in gens if g is not None]
            while gens:
                for g in list(gens):
                    try:
                        next(g)
                    except StopIteration:
                        gens.remove(g)

        def norm_stats(grp_):
            gi = rot("ssq", 4)
            info = []
            Rm = max(t["R"] for t in grp_)
            for j, t in enumerate(grp_):
                R = t["R"]
                if "dram" in t:
                    ta = rot("tmpA", NT3)
                    dma("sp", tmpA[0:R, ta, :], t["dram"], [], [r_tmpA[ta]])
                    src, regs = tmpA[0:R, ta, :], [r_tmpA[ta]]
                else:
                    src, regs = t["src"], t["regs"]
                xi = rot("xnb", NXN)
                sc = gi * 4 + j
                act(xnb[0:R, xi, :], src, AF.Square, regs, [r_xnb[xi], r_ssq[gi]], accum_out=ssq[0:R, sc:sc + 1])
                info.append((src, regs, gi, sc))
            c0_, c1_ = gi * 4, gi * 4 + len(grp_)
            act(rstd[0:Rm, c0_:c1_], ssq[0:Rm, c0_:c1_], AF.Ln, [r_ssq[gi], r_consts], [r_rstd[gi]], scale=1.0 / D, bias=consts[0:Rm, 0:1])
            act(rstd[0:Rm, c0_:c1_], rstd[0:Rm, c0_:c1_], AF.Exp, [r_rstd[gi]], [r_rstd[gi]], scale=-0.5)
            return info

        def norm_apply_gen(grp_, info, G_idx, SH_idx):
            for t, (src, regs, gi, sc) in zip(grp_, info):
                R = t["R"]
                xi = rot("xnb", NXN)
                ts(xnb[0:R, xi, :], src, rstd[0:R, sc:sc + 1], None, ALU.mult, None, regs + [r_rstd[gi]], [r_xnb[xi]])
                norm_post(t, xi, G_idx, SH_idx)
                yield

        def norm_gen(tiles, G_idx, SH_idx, batch=1):
            for g0 in range(0, len(tiles), batch):
                grp_ = tiles[g0:g0 + batch]
                info = norm_stats(grp_)
                yield from norm_apply_gen(grp_, info, G_idx, SH_idx)

        def gate_gen(ncols, BC, r_BC, EB, r_EB):
            gr = groups_of(ncols)
            for (c0, n) in gr:
                b = newbank()
                for k in range(KT):
                    mm(psf(b)[0:16, 0:n], wa[:, k, :], hT[:, k, c0:c0 + n], k == 0, k == KT - 1, [r_wa, r_hT[k]], [r_ps[b]])
                cp(aT[:, c0:c0 + n], psf(b)[0:16, 0:n], [r_ps[b]], [r_aT])
            yield
            for h in range(NH):
                ti = rot("tmpB", NT4)
                for (c0, n) in gr:
                    b = newbank()
                    mm(psf(b)[:, 0:n], walpha[:, h * 128:(h + 1) * 128], aT[:, c0:c0 + n], True, True, [r_walpha, r_aT], [r_ps[b]])
                    act(tmpB[:, ti, c0:c0 + n], psf(b)[:, 0:n], AF.Exp, [r_ps[b], r_nbal], [r_tmpB[ti]], scale=-1.0, bias=nbal[:, h:h + 1])
                act(tmpB[:, ti, 0:ncols], tmpB[:, ti, 0:ncols], AF.Ln, [r_tmpB[ti], r_consts], [r_tmpB[ti]], bias=consts[:, 1:2])
                P.op("dve", lambda e, h=h, ti=ti: e.tensor_tensor_scan(out=BC[:, h, 0:ncols], data0=rmask[:, 0:ncols], data1=tmpB[:, ti, 0:ncols],
                                                                        initial=0.0, op0=ALU.mult, op1=ALU.add),
                     reads=[r_rmask, r_tmpB[ti]], writes=[r_BC[h]])
                np_ = min(ncols, TP) // 128
                act(EB[:, h, 0:np_], BC[:, h, 127:np_ * 128:128], AF.Exp, [r_BC[h]], [r_EB[h]], scale=-1.0 / 16)
                if ncols > TP:
                    act(EB[:, h, TPB:TPB + 2], BC[:, h, TP + 31:TP + 64:32], AF.Exp, [r_BC[h]], [r_EB[h]], scale=-1.0 / 16)
                yield

        kpre = {}

        def prefetch_k():
            kpre[0] = wload1(w_in[:, O_K:O_K + 256])
            wreserved.add(kpre[0].slots[0])

        def qk_gen(w_off, ncols, BC, r_BC, OUT, r_OUT, is_q):
            used_pre = None
            for h in range(NH):
                if h % 2 == 0:
                    if w_off == O_K and h == 0 and 0 in kpre:
                        slot = kpre.pop(0)
                        used_pre = slot
                    else:
                        if used_pre is not None:
                            wreserved.discard(used_pre.slots[0])
                            used_pre = None
                        slot = wload1(w_in[:, w_off + h * 128:w_off + h * 128 + 256])
                for (c0, n) in groups_of(ncols):
                    b = newbank()
                    for k in range(KT):
                        mm(psf(b)[:, 0:n], slot.lhs(k, (h % 2) * 128, 128), hT[:, k, c0:c0 + n], k == 0, k == KT - 1,
                           slot.regs + [r_hT[k]], [r_ps[b]])
                    ti = rot("tmpB", NT4)
                    if is_q:
                        act(tmpB[:, ti, 0:n], BC[:, h, c0:c0 + n], AF.Exp, [r_BC[h], r_consts], [r_tmpB[ti]], scale=-1.0 / 16, bias=consts[:, 2:3])
                    else:
                        act(tmpB[:, ti, 0:n], BC[:, h, c0:c0 + n], AF.Exp, [r_BC[h]], [r_tmpB[ti]], scale=1.0 / 16)
                    tt(OUT[:, h, c0:c0 + n], psf(b)[:, 0:n], tmpB[:, ti, 0:n], ALU.mult, [r_ps[b], r_tmpB[ti]], [r_OUT[h]])
                    yield

        def tm_proj(slot, lhs, lhs_regs, col0, R, nk=KT, kbase=0, start=True, stop=True, bank=None):
            b = newbank() if bank is None else bank
            for k in range(nk):
                mm(psf(b)[0:R, :], lhs[:, kbase + k, col0:col0 + R], slot.rhs(k), start and k == 0, stop and k == nk - 1,
                   slot.regs + lhs_regs, [r_ps[b]])
            return b

        def v_gen(slot, half, gl_tiles, VT, r_VT):
            for ti_, t in enumerate(gl_tiles):
                R = t["R"]
                b = tm_proj(slot, hT, r_hT, t["col0"], R)
                if ti_ % 2 == 0:
                    act(VT[0:R, t["vi"], half * 512:(half + 1) * 512], psf(b)[0:R, :], AF.Copy, [r_ps[b]], [r_VT[t["vi"]]])
                else:
                    cp(VT[0:R, t["vi"], half * 512:(half + 1) * 512], psf(b)[0:R, :], [r_ps[b]], [r_VT[t["vi"]]])
                yield

        def g_gen(slot, half, gl_tiles):
            for t in gl_tiles:
                R = t["R"]
                b = tm_proj(slot, hT, r_hT, t["col0"], R)
                ta = rot("tmpA", NT3)
                act(tmpA[0:R, ta, 0:512], psf(b)[0:R, :], AF.Silu, [r_ps[b]], [r_tmpA[ta]])
                tt(gsg[0:R, t["vi"], half * 512:(half + 1) * 512], tmpA[0:R, ta, 0:512], gn4[0:R, :], ALU.mult,
                   [r_tmpA[ta], r_gn4], [r_gsg[t["vi"]]])
                yield

        def gla_A(t, KD, r_KD, full):
            R, c0 = t["R"], t["col0"]
            si = rot("sm", NSM)
            t["si"] = si
            if "samp" in t:
                s = t["samp"]
                ta = rot("tmpA", NT3)
                held["tmpA"].add(ta)
                t["s_ta"] = ta
                dma("sp", tmpA[:, ta, :].rearrange("p (h v) -> p h v", h=NH), s0_d[s].rearrange("h p v -> p h v"), [], [r_tmpA[ta]])
                cp(Sbfs[:, s, :, :], tmpA[:, ta, :].rearrange("p (h v) -> p h v", h=NH), [r_tmpA[ta]], [r_Sbfs[s]])
            bB = newbank()
            if full:
                bA = newbank()
            for h in range(NH):
                if full:
                    mm(psf(bA)[0:R, h * 128:h * 128 + R], KD[:, h, c0:c0 + R], qdT[:, h, c0:c0 + R], True, True,
                       [r_KD[h], r_qdT[h]], [r_ps[bA]])
                tr(psb(bB)[0:R, h * 128:(h + 1) * 128], KD[:, h, c0:c0 + R], identb[:, :], [r_KD[h], r_identb], [r_ps[bB]])
            if full:
                for h in range(NH):
                    tt(sTm[0:R, si, h, 0:R], psf(bA)[0:R, h * 128:h * 128 + R], maskT[0:R, 0:R], ALU.mult, [r_ps[bA], r_maskT], [r_sTm[si]])
            act(kdtm[0:R, si, :, :], psb(bB)[0:R, 0:512].rearrange("p (h d) -> p h d", h=NH), AF.Copy, [r_ps[bB]], [r_kdtm[si]])

        def gla_B(t, chunk_idx, VT, r_VT, EB, r_EB, full):
            R, c0, vi, si = t["R"], t["col0"], t["vi"], t["si"]
            samp = t.get("samp")
            bO = [newbank(), newbank()] if full else None
            bS = [newbank(), newbank()]
            for h in range(NH):
                if full:
                    ob = psf(bO[h // 2])[0:R, (h % 2) * 256:(h % 2 + 1) * 256]
                    mm(ob, sTm[0:R, si, h, 0:R], VT[0:R, vi, h * 256:(h + 1) * 256], True, False, [r_sTm[si], r_VT[vi]], [r_ps[bO[h // 2]]])
                    if samp is None:
                        mm(ob, qdT[:, h, c0:c0 + R], Sbf[:, h, :], False, True, [r_qdT[h], r_Sbf[h]], [r_ps[bO[h // 2]]])
                    else:
                        mm(ob, qdT[:, h, c0:c0 + R], Sbfs[:, samp, h, :], False, True, [r_qdT[h], r_Sbfs[samp]], [r_ps[bO[h // 2]]])
                mm(psf(bS[h // 2])[:, (h % 2) * 256:(h % 2 + 1) * 256], kdtm[0:R, si, h, :], VT[0:R, vi, h * 256:(h + 1) * 256], True, True,
                   [r_kdtm[si], r_VT[vi]], [r_ps[bS[h // 2]]])
            for h in range(NH):
                st_ = rot("st", 2)
                sp_ = psf(bS[h // 2])[:, (h % 2) * 256:(h % 2 + 1) * 256]
                e_ = EB[:, h, chunk_idx:chunk_idx + 1]
                if samp is None:
                    tt(stmp[:, st_, :], sp_, S32[:, h, :], ALU.add, [r_ps[bS[h // 2]], r_S32[h]], [r_stmp[st_]])
                    act(S32[:, h, :], stmp[:, st_, :], AF.Copy, [r_stmp[st_], r_EB[h]], [r_S32[h]], scale=e_)
                    if full:
                        ts(Sbf[:, h, :], stmp[:, st_, :], e_, None, ALU.mult, None, [r_stmp[st_], r_EB[h]], [r_Sbf[h]])
                else:
                    ta = t["s_ta"]
                    s32 = tmpA[:, ta, h * 256:(h + 1) * 256]
                    tt(stmp[:, st_, :], sp_, s32, ALU.add, [r_ps[bS[h // 2]], r_tmpA[ta]], [r_stmp[st_]])
                    act(s32, stmp[:, st_, :], AF.Copy, [r_stmp[st_], r_EB[h]], [r_tmpA[ta]], scale=e_)
            if samp is not None:
                ta = t["s_ta"]
                dma("sp", ss_o[samp].rearrange("h p v -> p h v"), tmpA[:, ta, :].rearrange("p (h v) -> p h v", h=NH), [r_tmpA[ta]], [])
                held["tmpA"].discard(ta)
            if not full:
                return
            so = rot("so", 2)
            og = rot("og", 2)
            t["og"] = og
            to = rot("tmpA", NT3)
            for j in range(2):
                act(tmpA[0:R, to, j * 512:(j + 1) * 512], psf(bO[j])[0:R, :], AF.Copy, [r_ps[bO[j]]], [r_tmpA[to]])
            for h in range(NH):
                act(ogb[0:R, og, h * 256:(h + 1) * 256], tmpA[0:R, to, h * 256:(h + 1) * 256], AF.Square, [r_tmpA[to]],
                    [r_ogb[og], r_ssqo[so]], accum_out=ssqo[0:R, so, h:h + 1])
            act(rstdo[0:R, so, :], ssqo[0:R, so, :], AF.Ln, [r_ssqo[so], r_consts], [r_rstdo[so]], scale=1.0 / 256, bias=consts[0:R, 0:1])
            act(rstdo[0:R, so, :], rstdo[0:R, so, :], AF.Exp, [r_rstdo[so]], [r_rstdo[so]], scale=-0.5)
            for h in range(NH):
                stt(ogb[0:R, og, h * 256:(h + 1) * 256], tmpA[0:R, to, h * 256:(h + 1) * 256], rstdo[0:R, so, h:h + 1],
                    gsg[0:R, vi, h * 256:(h + 1) * 256], ALU.mult, ALU.mult, [r_tmpA[to], r_rstdo[so], r_gsg[vi]], [r_ogb[og]])

        def gla_C(t):
            R, c0, og = t["R"], t["col0"], t["og"]
            bT = newbank()
            for k in range(KT):
                tr(psb(bT)[:, k * 128:k * 128 + R], ogb[0:R, og, k * 128:(k + 1) * 128], identb[0:R, 0:R], [r_ogb[og], r_identb], [r_ps[bT]])
            cp(ogT[:, :, c0:c0 + R], psb(bT).rearrange("p (k r) -> p k r", k=KT)[:, :, 0:R], [r_ps[bT]], [r_ogT])

        def gla_gen(gl_tiles, KD, r_KD, VT, r_VT, EB, r_EB, full):
            n = len(gl_tiles)
            for step in range(n + 2):
                if step < n:
                    gla_A(gl_tiles[step], KD, r_KD, full)
                    yield
                if 0 <= step - 1 < n:
                    t = gl_tiles[step - 1]
                    ci = (TPB + t["samp"]) if "samp" in t else t["ci"]
                    gla_B(t, ci, VT, r_VT, EB, r_EB, full)
                    yield
                if full and 0 <= step - 2 < n:
                    if step - 2 >= n - 2:
                        yield
                        yield
                    gla_C(gl_tiles[step - 2])
                    yield

        def conv_gen(has_s, ncol, ncolx):
            grp = groups_of(ncol)
            grpx = groups_of(ncolx)
            for c2 in range(4):
                s_cc = wload1(w_in[:, O_CC + c2 * 256:O_CC + (c2 + 1) * 256])
                s_ch = wload1(w_in[:, O_CH + c2 * 256:O_CH + (c2 + 1) * 256])
                s_cb = wload1(w_in[:, O_CB + c2 * 256:O_CB + (c2 + 1) * 256])
                for cc_ in range(2):
                    c = c2 * 2 + cc_
                    ui = rot("ue", 2)
                    w0 = vecs[:, V_CW + 0 * 8 + c:V_CW + 0 * 8 + c + 1]
                    w1 = vecs[:, V_CW + 1 * 8 + c:V_CW + 1 * 8 + c + 1]
                    w2 = vecs[:, V_CW + 2 * 8 + c:V_CW + 2 * 8 + c + 1]
                    bb = vecs[:, V_CBIAS + c:V_CBIAS + c + 1]
                    cp(ue[:, ui, 0:2], uprev[:, c, :], [r_uprev], [r_ue[ui]])
                    if has_s:
                        cp(ues[:, ui, :, 0:2], prevS[:, c, :].rearrange("p (s r) -> p s r", r=2), [r_prevS], [r_ues[ui]])
                    t1 = rot("tmpB", NT4)
                    for (c0, n) in grpx:
                        b1 = newbank()
                        for k in range(KT):
                            mm(psf(b1)[:, 0:n], s_cc.lhs(k, cc_ * 128, 128), hT[:, k, c0:c0 + n], k == 0, k == KT - 1,
                               s_cc.regs + [r_hT[k]], [r_ps[b1]])
                        act(tmpB[:, t1, c0:c0 + n], psf(b1)[:, 0:n], AF.Copy, [r_ps[b1]], [r_tmpB[t1]])
                    yield
                    for (c0, n) in grpx:
                        b2 = newbank()
                        for k in range(KT):
                            mm(psf(b2)[:, 0:n], s_ch.lhs(k, cc_ * 128, 128), hT[:, k, c0:c0 + n], k == 0, k == KT - 1,
                               s_ch.regs + [r_hT[k]], [r_ps[b2]])
                        if c0 < TP:
                            tt(ue[:, ui, 2 + c0:2 + c0 + n], psf(b2)[:, 0:n], tmpB[:, t1, c0:c0 + n], ALU.mult, [r_ps[b2], r_tmpB[t1]], [r_ue[ui]])
                        else:
                            tt(ues[:, ui, :, 2:34], psf(b2)[:, 0:64].rearrange("p (s t) -> p s t", s=2),
                               tmpB[:, t1, c0:c0 + 64].rearrange("p (s t) -> p s t", s=2), ALU.mult, [r_ps[b2], r_tmpB[t1]], [r_ues[ui]])
                            stt(ue[:, ui, 0:2], psf(b2)[:, 64:66], flag[:, 0:1], tmpB[:, t1, c0 + 64:c0 + 66], ALU.mult, ALU.mult,
                                [r_ps[b2], r_tmpB[t1], r_flag], [r_ue[ui]])
                    yield
                    t2 = rot("tmpB", NT4)
                    for (c0, n) in grp:
                        b3 = newbank()
                        for k in range(KT):
                            mm(psf(b3)[:, 0:n], s_cb.lhs(k, cc_ * 128, 128), hT[:, k, c0:c0 + n], k == 0, k == KT - 1,
                               s_cb.regs + [r_hT[k]], [r_ps[b3]])
                        act(tmpB[:, t2, c0:c0 + n], psf(b3)[:, 0:n], AF.Copy, [r_ps[b3]], [r_tmpB[t2]])
                    cp(uprev[:, c, :], ue[:, ui, TP:TP + 2], [r_ue[ui]], [r_uprev])
                    if has_s:
                        cp(ulastS[:, c, :].rearrange("p (s r) -> p s r", r=2), ues[:, ui, :, 32:34], [r_ues[ui]], [r_ulastS])
                    ca = 0
                    ts(cacc[:, ca, 0:TP], ue[:, ui, 0:TP], w0, bb, ALU.mult, ALU.add, [r_ue[ui], r_vecs], [r_cacc[ca]])
                    stt(cacc[:, ca, 0:TP], ue[:, ui, 1:TP + 1], w1, cacc[:, ca, 0:TP], ALU.mult, ALU.add, [r_ue[ui], r_vecs, r_cacc[ca]], [r_cacc[ca]])
                    stt(cacc[:, ca, 0:TP], ue[:, ui, 2:TP + 2], w2, cacc[:, ca, 0:TP], ALU.mult, ALU.add, [r_ue[ui], r_vecs, r_cacc[ca]], [r_cacc[ca]])
                    if has_s:
                        cs3 = cacc[:, ca, TP:TM].rearrange("p (s t) -> p s t", s=2)
                        ts(cs3, ues[:, ui, :, 0:32], w0, bb, ALU.mult, ALU.add, [r_ues[ui], r_vecs], [r_cacc[ca]])
                        stt(cs3, ues[:, ui, :, 1:33], w1, cs3, ALU.mult, ALU.add, [r_ues[ui], r_vecs, r_cacc[ca]], [r_cacc[ca]])
                        stt(cs3, ues[:, ui, :, 2:34], w2, cs3, ALU.mult, ALU.add, [r_ues[ui], r_vecs, r_cacc[ca]], [r_cacc[ca]])
                    tt(ucbT[:, c, 0:ncol], cacc[:, ca, 0:ncol], tmpB[:, t2, 0:ncol], ALU.mult, [r_cacc[ca], r_tmpB[t2]], [r_ucbT[c]])
                    yield

        def cache_out(src, nrow, regs_src, dst):
            co = rot("tmpA", NT3)
            for half in range(2):
                b = newbank()
                for cc_ in range(4):
                    c = half * 4 + cc_
                    tr(psf(b)[0:nrow, cc_ * 128:(cc_ + 1) * 128], src[:, c, :], identf[:, :], [regs_src, r_identf], [r_ps[b]])
                cp(tmpA[0:nrow, co, half * 512:(half + 1) * 512], psf(b)[0:nrow, :], [r_ps[b]], [r_tmpA[co]])
            dma("sp", dst, tmpA[0:nrow, co, :], [r_tmpA[co]], [])

        def gated_fm_gen(w_gate_off, w_y, ysrc, r_ysrc, ncol, finish):
            grp = groups_of(ncol)
            for c2 in range(4):
                s_y = wload1(w_y[:, c2 * 256:(c2 + 1) * 256])
                s_g = wload1(w_in[:, w_gate_off + c2 * 256:w_gate_off + (c2 + 1) * 256])
                for cc_ in range(2):
                    c = c2 * 2 + cc_
                    for (c0, n) in grp:
                        bg = newbank()
                        for k in range(KT):
                            mm(psf(bg)[:, 0:n], s_g.lhs(k, cc_ * 128, 128), hT[:, k, c0:c0 + n], k == 0, k == KT - 1,
                               s_g.regs + [r_hT[k]], [r_ps[bg]])
                        ti = rot("tmpB", NT4)
                        act(tmpB[:, ti, 0:n], psf(bg)[:, 0:n], AF.Sigmoid, [r_ps[bg]], [r_tmpB[ti]])
                        by = newbank()
                        for k in range(KT):
                            mm(psf(by)[:, 0:n], s_y.lhs(k, cc_ * 128, 128), ysrc[:, k, c0:c0 + n], k == 0, k == KT - 1,
                               s_y.regs + r_ysrc, [r_ps[by]])
                        finish(c, c0, n, by, ti)
                        yield

        def norm_pre(src, regs, R):
            gi = rot("ssq", 4)
            sc = gi * 4
            xi = rot("xnb", NXN)
            act(xnb[0:R, xi, :], src, AF.Square, regs, [r_xnb[xi], r_ssq[gi]], accum_out=ssq[0:R, sc:sc + 1])
            act(rstd[0:R, sc:sc + 1], ssq[0:R, sc:sc + 1], AF.Ln, [r_ssq[gi], r_consts], [r_rstd[gi]], scale=1.0 / D, bias=consts[0:R, 0:1])
            act(rstd[0:R, sc:sc + 1], rstd[0:R, sc:sc + 1], AF.Exp, [r_rstd[gi]], [r_rstd[gi]], scale=-0.5)
            ts(xnb[0:R, xi, :], src, rstd[0:R, sc:sc + 1], None, ALU.mult, None, regs + [r_rstd[gi]], [r_xnb[xi]])
            return xi

        def norm_post(t, xi, G_idx, SH_idx):
            R = t["R"]
            b = newbank()
            for k in range(KT):
                tr(psb(b)[:, k * 128:k * 128 + R], xnb[0:R, xi, k * 128:(k + 1) * 128], identb[0:R, 0:R], [r_xnb[xi], r_identb], [r_ps[b]])
            for k in range(KT):
                for (co, nn, seq) in t["seqruns"]:
                    act(hT[:, k, t["col0"] + co:t["col0"] + co + nn], psb(b)[:, k * 128 + co:k * 128 + co + nn], AF.Identity,
                        [r_ps[b], r_modT[G_idx], r_modT[SH_idx]], [r_hT[k]], scale=modT[:, G_idx, k, seq:seq + 1], bias=modT[:, SH_idx, k, seq:seq + 1])

        def wo_norm2_gen(tiles):
            slots = [wload(w_o[:, half * 512:(half + 1) * 512]) for half in range(2)]
            pend = None
            for t in tiles:
                R = t["R"]
                for half in range(2):
                    ta = rot("tmpA", NT3)
                    b = tm_proj(slots[half], mT, r_mT, t["col0"], R)
                    xr = xres[0:R, t["xi"], half * 512:(half + 1) * 512]
                    rx = r_xres[t["xi"]][half]
                    tt(tmpA[0:R, ta, 0:512], psf(b)[0:R, :], gB[0:R, 0, t["gty"], half * 512:(half + 1) * 512], ALU.mult,
                       [r_ps[b], r_gB[0][t["gty"]]], [r_tmpA[ta]])
                    tt(xr, xr, tmpA[0:R, ta, 0:512], ALU.add, [rx, r_tmpA[ta]], [rx])
                xi = norm_pre(xres[0:R, t["xi"], :], r_xres[t["xi"]], R)
                if pend is not None:
                    norm_post(pend[0], pend[1], 3, 2)
                pend = (t, xi)
                yield
            norm_post(pend[0], pend[1], 3, 2)
            yield

        def ffn1_gen(ncol):
            grp = groups_of(ncol)
            for j2 in range(11):
                sg_ = wload1(w_ffn_in[:, j2 * 256:(j2 + 1) * 256])
                su_ = wload1(w_ffn_in[:, DFF + j2 * 256:DFF + (j2 + 1) * 256])
                for jj in range(2):
                    j = j2 * 2 + jj
                    for (c0, n) in grp:
                        bg = newbank()
                        for k in range(KT):
                            mm(psf(bg)[:, 0:n], sg_.lhs(k, jj * 128, 128), hT[:, k, c0:c0 + n], k == 0, k == KT - 1,
                               sg_.regs + [r_hT[k]], [r_ps[bg]])
                        ti = rot("tmpB", NT4)
                        act(tmpB[:, ti, 0:n], psf(bg)[:, 0:n], AF.Silu, [r_ps[bg]], [r_tmpB[ti]])
                        bu = newbank()
                        for k in range(KT):
                            mm(psf(bu)[:, 0:n], su_.lhs(k, jj * 128, 128), hT[:, k, c0:c0 + n], k == 0, k == KT - 1,
                               su_.regs + [r_hT[k]], [r_ps[bu]])
                        tt(actT[:, j, c0:c0 + n], psf(bu)[:, 0:n], tmpB[:, ti, 0:n], ALU.mult, [r_ps[bu], r_tmpB[ti]], [r_actT[j]])
                        yield

        def ffn2_gen(tiles, hook=None):
            kgs = [(0, 8), (8, 8), (16, 6)]
            for half in range(2):
                if half == 1 and hook is not None:
                    hook()
                banks = []
                for t in tiles:
                    b = newbank()
                    reserved.add(b)
                    banks.append(b)
                for gi_, (kb, nk) in enumerate(kgs):
                    slot = wload(w_ffn_out[kb * 128:(kb + nk) * 128, half * 512:(half + 1) * 512], nk=nk)
                    for t, b in zip(tiles, banks):
                        tm_proj(slot, actT, r_actT[kb:kb + nk], t["col0"], t["R"], nk=nk, kbase=kb, start=(gi_ == 0), stop=(gi_ == 2), bank=b)
                        yield
                for t, b in zip(tiles, banks):
                    R = t["R"]
                    ta = rot("tmpA", NT3)
                    xr = xres[0:R, t["xi"], half * 512:(half + 1) * 512]
                    rx = r_xres[t["xi"]][half]
                    tt(tmpA[0:R, ta, 0:512], psf(b)[0:R, :], gB[0:R, 1, t["gty"], half * 512:(half + 1) * 512], ALU.mult,
                       [r_ps[b], r_gB[1][t["gty"]]], [r_tmpA[ta]])
                    tt(xr, tmpA[0:R, ta, 0:512], xr, ALU.add, [r_tmpA[ta], rx], [rx])
                    reserved.discard(b)

        def final_gen(tiles):
            for t in tiles:
                R = t["R"]
                gi = rot("ssq", 4)
                sc = gi * 4
                xi = rot("xnb", NXN)
                rx = r_xres[t["xi"]]
                act(xnb[0:R, xi, :], xres[0:R, t["xi"], :], AF.Square, rx, [r_xnb[xi], r_ssq[gi]], accum_out=ssq[0:R, sc:sc + 1])
                act(rstd[0:R, sc:sc + 1], ssq[0:R, sc:sc + 1], AF.Ln, [r_ssq[gi], r_consts], [r_rstd[gi]], scale=1.0 / D, bias=consts[0:R, 0:1])
                act(rstd[0:R, sc:sc + 1], rstd[0:R, sc:sc + 1], AF.Exp, [r_rstd[gi]], [r_rstd[gi]], scale=-0.5)
                ta = rot("tmpA", NT3)
                stt(tmpA[0:R, ta, :], xres[0:R, t["xi"], :], rstd[0:R, sc:sc + 1], nfg[0:R, :], ALU.mult, ALU.mult,
                    rx + [r_rstd[gi], r_nfg], [r_tmpA[ta]])
                dma("sp", t["out"], tmpA[0:R, ta, :], [r_tmpA[ta]], [])
                yield

        P.tag = "prefix"
        npre = cfg.ntp // TPB

        def pre_tiles(pb):
            return [dict(dram=xp[(pb * TPB + i) * 128:(pb * TPB + i + 1) * 128, :], R=128, col0=i * 128, seqruns=[(0, 128, 0)], vi=i, ci=i)
                    for i in range(TPB)]

        pre_info = {}

        def pre_front(pb):
            st = pb % 2
            tiles = pre_tiles(pb)
            info = pre_info.pop(pb) if pb in pre_info else norm_stats(tiles)
            yield from norm_apply_gen(tiles, info, 1, 0)
            yield from gate_gen(TP, bcP[st], r_bcP[st], ebP[st], r_ebP[st])
            if pb + 1 < npre:
                nt_ = pre_tiles(pb + 1)
                pre_info[pb + 1] = norm_stats(nt_)
            yield from qk_gen(O_K, TP, bcP[st], r_bcP[st], kdP[st], r_kdP[st], False)
            for half in range(2):
                slot = wload(w_in[:, O_V + half * 512:O_V + (half + 1) * 512])
                yield from v_gen(slot, half, tiles, vtP[st], r_vtP[st])

        def pre_gla(pb):
            st = pb % 2
            yield from gla_gen(pre_tiles(pb), kdP[st], r_kdP[st], vtP[st], r_vtP[st], ebP[st], r_ebP[st], False)

        nblk = cfg.ntm // TPB

        def block_ctx(mb):
            has_s = (mb == 0)
            ctx = dict(mb=mb, has_s=has_s, last=(mb == nblk - 1), ncol=TM if has_s else TP, ncolx=T if has_s else TP)
            tiles = []
            for i in range(TPB):
                rows = slice((mb * TPB + i) * 128, (mb * TPB + i + 1) * 128)
                tiles.append(dict(dram=xm[rows, :], R=128, col0=i * 128, seqruns=[(0, 128, 0)], vi=i, ci=i, xi=i, gty=0, out=ym[rows, :]))
            gl_tiles = list(tiles)
            ntiles = list(tiles)
            if has_s:
                stile = dict(dram=xs[:, :], R=64, col0=TP, seqruns=[(0, 32, 1), (32, 32, 2)], xi=TPB, gty=1, out=ys[:, :])
                tiles.append(stile)
                ntiles.append(stile)
                for s in range(2):
                    gl_tiles.append(dict(R=32, col0=TP + 32 * s, vi=TPB + s, samp=s))
                ntiles.append(dict(dram=xh[:, :], R=2, col0=TM, seqruns=[(0, 2, 0)]))
            ctx.update(tiles=tiles, gl_tiles=gl_tiles, ntiles=ntiles)
            return ctx

        def front1(ctx):
            P.tag = "norm1"
            yield from norm_gen(ctx["ntiles"], 1, 0, batch=2)
            P.tag = "gate"
            yield from gate_gen(ctx["ncol"], bc, r_bc, eb, r_eb)
            P.tag = "qk"
            yield from qk_gen(O_Q, ctx["ncol"], bc, r_bc, qdT, r_qdT, True)

        def front2(ctx):
            P.tag = "qk"
            yield from qk_gen(O_K, ctx["ncol"], bc, r_bc, kdT, r_kdT, False)
            P.tag = "vg"
            for half in range(2):
                slot = wload(w_in[:, O_V + half * 512:O_V + (half + 1) * 512])
                yield from v_gen(slot, half, ctx["gl_tiles"], vtm, r_vtm)
            for half in range(2):
                slot = wload(w_in[:, O_G + half * 512:O_G + (half + 1) * 512])
                yield from g_gen(slot, half, ctx["gl_tiles"])

        interleave(tag_gen(pre_front(0), "prefix"), mod_rest_gen())
        for pb in range(npre):
            nxt = tag_gen(pre_front(pb + 1), "prefix") if pb + 1 < npre else tag_gen(front1(block_ctx(0)), "front1")
            interleave(tag_gen(pre_gla(pb), "prefix"), nxt)
        for h in range(NH):
            ts(S32[:, h, :], S32[:, h, :], flag[:, 0:1], None, ALU.mult, None, [r_S32[h], r_flag], [r_S32[h]])
            cp(Sbf[:, h, :], S32[:, h, :], [r_S32[h]], [r_Sbf[h]])

        def preload_x(ctx):
            for t in ctx["tiles"]:
                dma("sp", xres[0:t["R"], t["xi"], :], t["dram"], [], r_xres[t["xi"]])

        for mb in range(nblk):
            ctx = block_ctx(mb)
            cur_blk[0] = mb
            has_s, last, ncol, tiles = ctx["has_s"], ctx["last"], ctx["ncol"], ctx["tiles"]
            run(front2(ctx))
            n_gla = 3 * len(ctx["gl_tiles"]) + 4
            interleave_ratio(tag_gen(gla_gen(ctx["gl_tiles"], kdT, r_kdT, vtm, r_vtm, eb, r_eb, True), "gla"),
                             tag_gen(conv_gen(has_s, ncol, ctx["ncolx"]), "conv"), 24.0 / n_gla)
            P.tag = "gla"
            preload_x(ctx)
            if last:
                for h in range(NH):
                    dma("sp", sp_o[h, :, :], S32[:, h, :], [r_S32[h]], [])
            if has_s:
                cache_out(ulastS, 4, r_ulastS, cs_o[:, :])
            if last:
                cache_out(uprev, 2, r_uprev, cp_o[:, :])
            P.tag = "ya"

            def fin_a(c, c0, n, by, ti):
                tt(Pa[:, c, c0:c0 + n], psf(by)[:, 0:n], tmpB[:, ti, 0:n], ALU.mult, [r_ps[by], r_tmpB[ti]], [r_Pa[c]])
            run(gated_fm_gen(O_GA, w_gla_out, ogT, [r_ogT], ncol, fin_a))
            P.tag = "yb"

            def fin_b(c, c0, n, by, ti):
                tt(tmpB[:, ti, 0:n], psf(by)[:, 0:n], tmpB[:, ti, 0:n], ALU.mult, [r_ps[by], r_tmpB[ti]], [r_tmpB[ti]])
                tt(mT[:, c, c0:c0 + n], tmpB[:, ti, 0:n], Pa[:, c, c0:c0 + n], ALU.add, [r_tmpB[ti], r_Pa[c]], [r_mT[c]])
            run(gated_fm_gen(O_GB, w_conv_out, ucbT, r_ucbT, ncol, fin_b))
            P.tag = "wo"
            run(wo_norm2_gen(tiles))
            P.tag = "ffn1"
            run(ffn1_gen(ncol))
            nxt = tag_gen(front1(block_ctx(mb + 1)), "front1") if mb + 1 < nblk else None
            interleave_ratio(nxt, tag_gen(ffn2_gen(tiles, prefetch_k if mb + 1 < nblk else None), "ffn2"), 6.0 * len(tiles) / 13.0)
            P.tag = "final"
            run(final_gen(tiles))


        P.emit(nc, es)
    nc._prog = P
    return nc


def _pack_vec(v, nk):
    return np.ascontiguousarray(v.reshape(nk, 128).T)


def prepare_inputs(cfg, ncores, x_prompt, x_sample, c_prompt, c_sample, state_gla, cache_conv, w_mod, b_mod, norm1_g,
                   w_in, w_alpha, b_alpha, gla_norm_g, w_gla_out, conv_w, conv_b, w_conv_out, w_o,
                   norm2_g, w_ffn_in, w_ffn_out, norm_f_g):
    f = np.float32
    TPB = cfg.tpb
    TP = TPB * 128
    T = TP + 66
    half_tok = cfg.ntm * 128
    vecs = np.zeros((128, NV), f)
    vecs[:, V_N1:V_N1 + 8] = _pack_vec(np.asarray(norm1_g[0], f), 8)
    vecs[:, V_N2:V_N2 + 8] = _pack_vec(np.asarray(norm2_g[0], f), 8)
    vecs[:, V_BMOD:V_BMOD + 48] = _pack_vec(np.asarray(b_mod[0], f), 48)
    vecs[:, V_BAL:V_BAL + 4] = _pack_vec(np.asarray(b_alpha[0], f), 4)
    vecs[:, V_CW:V_CW + 24] = np.concatenate([_pack_vec(np.asarray(conv_w[0, r], f), 8) for r in range(3)], axis=1)
    vecs[:, V_CBIAS:V_CBIAS + 8] = _pack_vec(np.asarray(conv_b[0], f), 8)
    bcp = np.zeros((128, 4 * D), f)
    bcp[:, 0:D] = np.broadcast_to(np.tile(np.asarray(gla_norm_g[0], f), 4), (128, D))
    bcp[:, D:2 * D] = np.broadcast_to(np.asarray(norm_f_g, f), (128, D))
    bcp[:, 2 * D:3 * D] = np.broadcast_to(np.asarray(b_mod[0, 2 * D:3 * D], f), (128, D))
    bcp[:, 3 * D:4 * D] = np.broadcast_to(np.asarray(b_mod[0, 5 * D:6 * D], f), (128, D))
    ident = np.eye(128, dtype=f)
    maskT = np.triu(np.ones((128, 128), f))
    rmask = np.ones((128, T), f)
    for c in list(range(0, TP, 128)) + [TP, TP + 32, TP + 64]:
        rmask[:, c] = 0.0
    shared = dict(w_mod=np.asarray(w_mod[0], f), w_in=np.asarray(w_in[0], f), w_alpha=np.asarray(w_alpha[0], f),
                  w_gla_out=np.asarray(w_gla_out[0], f), w_conv_out=np.asarray(w_conv_out[0], f), w_o=np.asarray(w_o[0], f),
                  w_ffn_in=np.asarray(w_ffn_in[0], f), w_ffn_out=np.asarray(w_ffn_out[0], f),
                  vecs=vecs, bcp=bcp, ident=ident, maskT=maskT, rmask=rmask)
    in_maps = []
    for c in range(ncores):
        sq, hf = c // 2, c % 2
        xseq = np.asarray(x_prompt[sq], f)
        m = dict(shared)
        m["xm"] = np.ascontiguousarray(xseq[hf * half_tok:(hf + 1) * half_tok])
        m["xp"] = np.ascontiguousarray(xseq[0:cfg.ntp * 128])
        m["xs"] = np.ascontiguousarray(np.asarray(x_sample[2 * c:2 * c + 2], f).reshape(64, D))
        m["xh"] = np.ascontiguousarray(xseq[half_tok - 2:half_tok]) if hf == 1 else np.ascontiguousarray(xseq[0:2])
        cs = np.stack([np.asarray(c_prompt[sq], f), np.asarray(c_sample[2 * c], f), np.asarray(c_sample[2 * c + 1], f)], axis=0)
        m["cT"] = np.ascontiguousarray(cs.reshape(3, KT, 128).transpose(2, 1, 0).reshape(128, KT * 3))
        m["flag"] = np.full((128, 1), float(hf), f)
        m["s0"] = np.ascontiguousarray(np.asarray(state_gla[0, 2 * c:2 * c + 2], f))
        m["cc0"] = np.ascontiguousarray(np.asarray(cache_conv[0, 2 * c:2 * c + 2], f).reshape(4, D))
        in_maps.append(m)
    return in_maps


def assemble(cfg, ncores, res):
    f = np.float32
    nseq = ncores // 2
    half_tok = cfg.ntm * 128
    y_prompt = np.zeros((nseq, 2 * half_tok, D), f)
    y_sample = np.zeros((2 * ncores, 32, D), f)
    sgp = np.zeros((1, nseq, NH, 128, 256), f)
    ccp = np.zeros((1, nseq, 2, D), f)
    sgs = np.zeros((1, 2 * ncores, NH, 128, 256), f)
    ccs = np.zeros((1, 2 * ncores, 2, D), f)
    for c in range(ncores):
        r = res[c]
        sq, hf = c // 2, c % 2
        y_prompt[sq, hf * half_tok:(hf + 1) * half_tok] = r["ym"]
        y_sample[2 * c:2 * c + 2] = np.asarray(r["ys"]).reshape(2, 32, D)
        if hf == 1:
            sgp[0, sq] = r["sp"]
            ccp[0, sq] = r["cp"]
        sgs[0, 2 * c:2 * c + 2] = r["ss"]
        ccs[0, 2 * c:2 * c + 2] = np.asarray(r["cs"]).reshape(2, 2, D)
    return (y_prompt, y_sample, sgp, ccp, sgs, ccs)


_NC_CACHE = {}


def kernel(**inputs):
    cfg = Cfg(16, 16, 4)
    ncores = 8
    in_maps = prepare_inputs(cfg, ncores, **inputs)
    if "nc" not in _NC_CACHE:
        _NC_CACHE["nc"] = build(cfg)
    nc = _NC_CACHE["nc"]
    res = run_bass_kernel_spmd(nc, in_maps, core_ids=list(range(ncores)))
    return assemble(cfg, ncores, res.results)
```
